# Optimizing a Trainium2 kernel written in Bass

```python
import jax, jax.numpy as jnp
from jax import lax
import numpy as np

D_MODEL = 4096
BATCH = 2
SEQ = 8192
DEPTH = 1
DEC_BATCH = 16
DEC_SEQ = 64
PAST_LEN = 4096

CHUNK = 64
D_MIX = D_MODEL
N_HEADS = 16
D_NOPE = 128
D_ROPE = 64
D_QK = D_NOPE + D_ROPE
D_V = 128
D_ATT = N_HEADS * D_V
Q_LORA = 1024
KV_LORA = 512
D_POOL = D_MIX - D_ATT
POOL_WINDOWS = (2, 4, 8, 16)
N_POOL_GROUPS = len(POOL_WINDOWS)
D_POOL_GROUP = D_POOL // N_POOL_GROUPS
POOL_BUF = max(POOL_WINDOWS) - 1
ROPE_THETA = 10000.0
EPS = 1e-6
Q_BLOCK = 128
SPLITS = (Q_LORA, KV_LORA, D_ROPE, D_ATT, D_POOL, D_POOL)
D_IN = sum(SPLITS)

kernel_name = 'hymba_mla_multiscale_pool_stream_step'


def rmsnorm(x, g):
    x32 = x.astype(jnp.float32)
    ms = jnp.mean(x32 * x32, axis=-1, keepdims=True)
    return (x32 * lax.rsqrt(ms + EPS) * g.astype(jnp.float32)).astype(x.dtype)


def rope(x, pos):
    half = D_ROPE // 2
    freqs = ROPE_THETA ** (-jnp.arange(half, dtype=jnp.float32) / half)
    ang = pos.astype(jnp.float32)[:, None] * freqs[None, :]
    ang = ang.reshape((ang.shape[0],) + (1,) * (x.ndim - 3) + (half,))
    cos = jnp.cos(ang).astype(x.dtype)
    sin = jnp.sin(ang).astype(x.dtype)
    x1, x2 = x[..., :half], x[..., half:]
    return jnp.concatenate([x1 * cos - x2 * sin, x2 * cos + x1 * sin], axis=-1)


def chunk_causal_attention(q_nope, q_rope, k_nope, k_rope, v, q_pos, k_pos):
    scale = D_QK ** -0.5
    k_chunk = k_pos // CHUNK

    def block(args):
        qn, qr, qp = args
        s = (jnp.einsum('bqhd,bkhd->bhqk', qn, k_nope, preferred_element_type=jnp.float32)
             + jnp.einsum('bqhd,bkd->bhqk', qr, k_rope, preferred_element_type=jnp.float32))
        mask = k_chunk[None, :] <= (qp // CHUNK)[:, None]
        s = jnp.where(mask[None, None], s * scale, -jnp.inf)
        p = jax.nn.softmax(s, axis=-1).astype(v.dtype)
        return jnp.einsum('bhqk,bkhd->bqhd', p, v)

    B, T = q_nope.shape[0], q_nope.shape[1]
    if T > Q_BLOCK and T % Q_BLOCK == 0:
        n = T // Q_BLOCK

        def split(a):
            return jnp.moveaxis(a.reshape((B, n, Q_BLOCK) + a.shape[2:]), 1, 0)

        out = lax.map(block, (split(q_nope), split(q_rope), q_pos.reshape(n, Q_BLOCK)))
        return jnp.moveaxis(out, 0, 1).reshape((B, T) + out.shape[3:])
    return block((q_nope, q_rope, q_pos))


def multiscale_pool(u, u_past, pos, w_pool, pool_scale):
    B, T = u.shape[0], u.shape[1]
    u_all = jnp.concatenate([u_past, u], axis=1)
    cs = jnp.cumsum(u_all.astype(jnp.float32), axis=1)
    cs = jnp.concatenate([jnp.zeros((B, 1, D_POOL), jnp.float32), cs], axis=1)
    end = cs[:, POOL_BUF + 1:]
    u32 = u.astype(jnp.float32)
    outs = []
    for g, w in enumerate(POOL_WINDOWS):
        sl = slice(g * D_POOL_GROUP, (g + 1) * D_POOL_GROUP)
        start = cs[:, POOL_BUF + 1 - w: POOL_BUF + 1 - w + T, sl]
        cnt = jnp.minimum(pos + 1, w).astype(jnp.float32)[None, :, None]
        outs.append((end[..., sl] - start) / cnt - u32[..., sl])
    p = jnp.stack(outs, axis=2).astype(u.dtype)
    h = jnp.einsum('btgc,gce->btge', p, w_pool).reshape(B, T, D_POOL)
    return h * pool_scale, u_all[:, -POOL_BUF:]


def mixer_layer(x, pos, ckv_past, krope_past, pool_past,
                g_norm, w_in, g_q_lat, w_uq, g_qn, g_qr, g_kv_lat, g_kr, w_ukv, g_kn,
                w_pool, pool_scale, w_out):
    B, T = x.shape[0], x.shape[1]
    h = rmsnorm(x, g_norm)
    proj = jnp.einsum('btd,de->bte', h, w_in)
    offs = np.cumsum((0,) + SPLITS)
    c_q, c_kv, k_r, z_att, u, z_pool = [proj[..., int(offs[i]):int(offs[i + 1])] for i in range(len(SPLITS))]

    q = jnp.einsum('btr,rhe->bthe', rmsnorm(c_q, g_q_lat), w_uq)
    q_nope = rmsnorm(q[..., :D_NOPE], g_qn)
    q_rope = rope(rmsnorm(q[..., D_NOPE:], g_qr), pos)
    c_kv = rmsnorm(c_kv, g_kv_lat)
    k_r = rope(rmsnorm(k_r, g_kr), pos)
    ckv_all = jnp.concatenate([ckv_past, c_kv], axis=1)
    kr_all = jnp.concatenate([krope_past, k_r], axis=1)
    kv = jnp.einsum('bsr,rhe->bshe', ckv_all, w_ukv)
    k_nope = rmsnorm(kv[..., :D_NOPE], g_kn)
    v = kv[..., D_NOPE:]
    S = ckv_all.shape[1]
    k_pos = jnp.arange(S, dtype=jnp.int32)
    att = chunk_causal_attention(q_nope, q_rope, k_nope, kr_all, v, pos, k_pos).reshape(B, T, D_ATT)
    att = att * jax.nn.silu(z_att)

    pool, pool_state = multiscale_pool(u, pool_past, pos, w_pool, pool_scale)
    pool = pool * jax.nn.silu(z_pool)

    mix = jnp.concatenate([att, pool], axis=-1)
    y = x + jnp.einsum('btm,md->btd', mix, w_out)
    return y, c_kv, k_r, pool_state


def setup_inputs(seed: int = 0) -> dict:
    key = jax.random.key(seed)
    ks = jax.random.split(key, 24)
    f32 = jnp.float32

    def nrm(k, shape, scale):
        return jax.random.normal(k, shape, f32) * scale

    def gain(k, shape):
        return 1.0 + 0.05 * jax.random.normal(k, shape, f32)

    return {
        'x_prompt': nrm(ks[0], (BATCH, SEQ, D_MODEL), 1.0),
        'x_sample': nrm(ks[1], (DEC_BATCH, DEC_SEQ, D_MODEL), 1.0),
        'cache_ckv': nrm(ks[2], (DEPTH, DEC_BATCH, PAST_LEN, KV_LORA), 1.0),
        'cache_krope': nrm(ks[3], (DEPTH, DEC_BATCH, PAST_LEN, D_ROPE), 1.0),
        'state_pool': nrm(ks[4], (DEPTH, DEC_BATCH, POOL_BUF, D_POOL), 1.0),
        'g_norm': gain(ks[5], (DEPTH, D_MODEL)),
        'w_in': nrm(ks[6], (DEPTH, D_MODEL, D_IN), D_MODEL ** -0.5),
        'g_q_lat': gain(ks[7], (DEPTH, Q_LORA)),
        'w_uq': nrm(ks[8], (DEPTH, Q_LORA, N_HEADS, D_QK), Q_LORA ** -0.5),
        'g_qn': gain(ks[9], (DEPTH, D_NOPE)),
        'g_qr': gain(ks[10], (DEPTH, D_ROPE)),
        'g_kv_lat': gain(ks[11], (DEPTH, KV_LORA)),
        'g_kr': gain(ks[12], (DEPTH, D_ROPE)),
        'w_ukv': nrm(ks[13], (DEPTH, KV_LORA, N_HEADS, D_NOPE + D_V), KV_LORA ** -0.5),
        'g_kn': gain(ks[14], (DEPTH, D_NOPE)),
        'w_pool': nrm(ks[15], (DEPTH, N_POOL_GROUPS, D_POOL_GROUP, D_POOL_GROUP), D_POOL_GROUP ** -0.5),
        'pool_scale': gain(ks[16], (DEPTH, D_POOL)),
        'w_out': nrm(ks[17], (DEPTH, D_MIX, D_MODEL), D_MIX ** -0.5),
    }


def reference(x_prompt, x_sample, cache_ckv, cache_krope, state_pool,
              g_norm, w_in, g_q_lat, w_uq, g_qn, g_qr, g_kv_lat, g_kr, w_ukv, g_kn,
              w_pool, pool_scale, w_out):
    B, T = x_prompt.shape[0], x_prompt.shape[1]
    Bd, Td = x_sample.shape[0], x_sample.shape[1]
    P = cache_ckv.shape[2]
    pos_p = jnp.arange(T, dtype=jnp.int32)
    pos_s = P + jnp.arange(Td, dtype=jnp.int32)
    dt = x_prompt.dtype
    yp, ys = x_prompt, x_sample
    ckv_p, kr_p, pool_p, ckv_s, kr_s, pool_s = [], [], [], [], [], []
    for l in range(DEPTH):
        params = (g_norm[l], w_in[l], g_q_lat[l], w_uq[l], g_qn[l], g_qr[l], g_kv_lat[l], g_kr[l],
                  w_ukv[l], g_kn[l], w_pool[l], pool_scale[l], w_out[l])
        yp, a, b, c = mixer_layer(yp, pos_p,
                                  jnp.zeros((B, 0, KV_LORA), dt), jnp.zeros((B, 0, D_ROPE), dt),
                                  jnp.zeros((B, POOL_BUF, D_POOL), dt), *params)
        ckv_p.append(a); kr_p.append(b); pool_p.append(c)
        ys, a, b, c = mixer_layer(ys, pos_s, cache_ckv[l], cache_krope[l], state_pool[l], *params)
        ckv_s.append(a); kr_s.append(b); pool_s.append(c)
    return (yp, ys, jnp.stack(ckv_p), jnp.stack(kr_p), jnp.stack(pool_p),
            jnp.stack(ckv_s), jnp.stack(kr_s), jnp.stack(pool_s))
```

```python
import contextlib
import numpy as np
import concourse.bass as bass
import concourse.mybir as mybir
from concourse.bass_utils import run_bass_kernel_spmd

F32 = mybir.dt.float32
BF16 = mybir.dt.bfloat16
AF = mybir.ActivationFunctionType
ALU = mybir.AluOpType

D = 4096
SEQ = 8192
NH = 16
QL = 1024
KVL = 512
DR = 64
DIN = 7744
OFF_Q, OFF_KV, OFF_KR, OFF_ZA, OFF_U, OFF_ZP = 0, 1024, 1536, 1600, 3648, 5696
PAST = 4096
EPS = 1e-6
SCALE = 192.0 ** -0.5
NEG = -30000.0
NDSEM = 20
LIMIT = None
MARKS = {}
PE_AT = {}


class Buf:
    __slots__ = ("w", "r", "excl")

    def __init__(self, excl=False):
        self.w = {}
        self.r = {}
        self.excl = excl


class Tracker:
    def __init__(self, nc, es):
        self.nc = nc
        self.streams = {k: [] for k in ("pe", "act", "dve", "pool", "sp")}
        self.sem = {}
        self.cnt = {}
        self.waited = {k: {} for k in self.streams}
        for k in ("pe", "act", "dve"):
            self.sem[k] = es.enter_context(nc.semaphore("s_" + k))
            self.cnt[k] = 0
        self.n_emit = 0
        self.n_pe = 0
        self.limit = LIMIT
        self.dma_i = {"sp": 0, "pool": 0}
        self.dma_last = {"sp": [None] * NDSEM, "pool": [None] * NDSEM}
        for q in ("sp", "pool"):
            for k in range(NDSEM):
                nm = "%sd%d" % (q, k)
                self.sem[nm] = es.enter_context(nc.semaphore("s_" + nm))
                self.cnt[nm] = 0

    def _wait(self, eng, tok):
        if tok is None:
            return
        s, v = tok
        if s == eng and eng == "pe":
            return
        if self.waited[eng].get(s, 0) >= v:
            return
        self.waited[eng][s] = v
        self.streams[eng].append(("wait", s, v))

    def _deps(self, eng, reads, writes):
        for b in reads:
            for s, v in list(b.w.items()):
                self._wait(eng, (s, v))
            if b.excl:
                for s, v in list(b.r.items()):
                    if s != eng:
                        self._wait(eng, (s, v))
        for b in writes:
            for s, v in list(b.w.items()):
                if s.startswith(eng + "d") or s == eng:
                    continue
                self._wait(eng, (s, v))
            for s, v in list(b.r.items()):
                if s == eng:
                    continue
                self._wait(eng, (s, v))

    def _mark(self, tok, reads, writes):
        s, v = tok
        for b in reads:
            if b.r.get(s, 0) < v:
                b.r[s] = v
        for b in writes:
            if s[-1].isdigit() and not b.r and b.w and all(k[-1].isdigit() for k in b.w):
                b.w[s] = v
            else:
                b.w = {s: v}
            b.r = {}

    def op(self, eng, fn, reads=(), writes=(), signal=True):
        self.n_emit += 1
        if self.limit is not None and self.n_emit > self.limit:
            return None
        self._deps(eng, reads, writes)
        if eng == "pe":
            self.n_pe += 1
            PE_AT[self.n_emit] = self.n_pe
        if signal:
            self.cnt[eng] += 1
            tok = (eng, self.cnt[eng])
        else:
            tok = (eng, self.cnt[eng] + 1)
        self.streams[eng].append(("op", fn, signal))
        self._mark(tok, reads, writes)
        return tok

    def dma(self, q, out, in_, reads=(), writes=()):
        self.n_emit += 1
        if self.limit is not None and self.n_emit > self.limit:
            return None
        self._deps(q, reads, writes)
        k = self.dma_i[q] % NDSEM
        self.dma_i[q] += 1
        self._wait(q, self.dma_last[q][k])
        nm = "%sd%d" % (q, k)
        self.cnt[nm] += 16
        tok = (nm, self.cnt[nm])
        self.dma_last[q][k] = tok
        self.streams[q].append(("dma", out, in_, nm))
        self._mark(tok, reads, writes)
        return tok

    def finish(self):
        for q in ("sp", "pool"):
            for k in range(NDSEM):
                self._wait(q, self.dma_last[q][k])

    def replay(self, eng, e):
        for it in self.streams[eng]:
            if it[0] == "wait":
                e.wait_ge(self.sem[it[1]], it[2])
            elif it[0] == "op":
                ins = it[1](e)
                if it[2]:
                    ins.then_inc(self.sem[eng], 1)
            else:
                e.dma_start(out=it[1], in_=it[2]).then_inc(self.sem[it[3]], 16)


def build_program():
    nc = bass.Bass("TRN2", target_bir_lowering=False)

    def din(name, shape, dt=F32):
        return nc.dram_tensor(name, list(shape), dt, kind="ExternalInput").ap()

    def dout(name, shape):
        return nc.dram_tensor(name, list(shape), F32, kind="ExternalOutput").ap()

    def dscr(name, shape):
        return nc.dram_tensor(name, list(shape), BF16, kind="Internal").ap()

    xseq = din("xseq", [SEQ, D])
    xown = din("xown", [4, 528, D])
    xsam = din("xsam", [128, D])
    cckv = din("cckv", [2, PAST, KVL])
    ckr = din("ckr", [2, PAST, DR])
    spool = din("spool", [2, 15, 2048])
    w_in = din("w_in", [D, DIN])
    w_uq = din("w_uq", [QL, NH, 192])
    w_ukv = din("w_ukv", [KVL, NH, 256])
    w_pool = din("w_pool", [4, 512, 512])
    w_out = din("w_out", [D, D])
    ident = din("ident", [128, 128])
    gcols_d = din("gcols", [128, 72])
    ropeA = din("ropeA", [2, 64, SEQ])
    ropeO = din("ropeO", [4, 2, 64, 512])
    ropeS = din("ropeS", [2, 64, 128])
    EK_d = din("EK", [32, 3072])
    LQ_d = din("LQ", [32, 512])
    invc_d = din("invc", [4, 128, 4, 512])
    invs_d = din("invs", [128, 4, 128])

    yown = dout("yown", [4, 512, D])
    ysam = dout("ysam", [128, D])
    ockv = dout("ockv", [4, 512, KVL])
    okr = dout("okr", [4, 512, DR])
    opool = dout("opool", [4, 15, 2048])
    sckv = dout("sckv", [128, KVL])
    skr = dout("skr", [128, DR])
    spo = dout("spo", [2, 15, 2048])

    KTp = dscr("KTp", [NH, 128, SEQ])
    Vp = dscr("Vp", [NH, 16, 128, 4, 128])
    krp = dscr("krp", [64, SEQ])
    KTs = dscr("KTs", [2, NH, 128, 4608])
    Vs = dscr("Vs", [2, NH, 9, 128, 4, 128])
    krs = dscr("krs", [2, 64, 4608])
    WIN = dscr("WIN", [61, 128, 32, 128])
    WOUT = dscr("WOUT", [32, 128, 32, 128])
    WQ = dscr("WQ", [16, 128, 8, 256])
    WP = dscr("WP", [4, 128, 4, 512])

    es = contextlib.ExitStack()
    with es:
        def sb(name, shape, dt=F32):
            return es.enter_context(nc.sbuf_tensor("sb_" + name, list(shape), dt))

        def ps(name, shape, dt=F32):
            return es.enter_context(nc.psum_tensor("ps_" + name, list(shape), dt))

        T = Tracker(nc, es)

        ident_f = sb("ident_f", [128, 128]); b_ident = Buf()
        ones_b = sb("ones_b", [128, 128], BF16); b_ones = Buf()
        eps_t = sb("eps_t", [128, 1]); b_eps = Buf()
        gcols = sb("gcols", [128, 72]); b_g = Buf()
        EKb = sb("EKb", [32, 3072], BF16); b_ek = Buf()
        xs = [sb("xs%d" % k, [128, D]) for k in range(2)]; b_xs = [Buf(), Buf()]
        ssx = [sb("ssx%d" % k, [128, 8]) for k in range(2)]; b_ssx = [Buf(), Buf()]
        hT = sb("hT", [128, 32, 512], BF16); b_hT = Buf()
        hTh = sb("hTh", [128, 32, 16], BF16); b_hTh = Buf()
        RM = sb("RM", [128, 16384], BF16); b_RM = Buf()
        cqnT = sb("cqnT", [128, 8, 512], BF16); b_cq = Buf()
        RW = sb("RW", [128, 16384], BF16)
        b_slot = [Buf() for _ in range(4)]
        KVR = sb("KVR", [128, 6144], BF16)
        KTc = [KVR[:, k * 1024:(k + 1) * 1024] for k in range(2)]
        Vc = [KVR[:, 2048 + k * 1024:2048 + (k + 1) * 1024].rearrange("p (b t d) -> p b t d", t=4, d=128) for k in range(2)]
        krc = [KVR[:, 4096 + k * 1024:4096 + (k + 1) * 1024] for k in range(2)]
        b_kvc = [Buf(), Buf()]
        b_vc = [Buf(), Buf()]
        b_scr = Buf(); b_conv = Buf()
        junk = KVR[:, 0:2048]
        NW = 8
        W = [sb("W%d" % k, [128, 512]) for k in range(NW)]; b_W = [Buf() for _ in range(NW)]
        NB = 6
        Bt = [sb("B%d" % k, [128, 512], BF16) for k in range(NB)]; b_B = [Buf() for _ in range(NB)]
        U = sb("U", [128, 4, 528]); b_U = Buf()
        PT = [sb("PT%d" % k, [128, 528]) for k in range(2)]; b_PT = [Buf(), Buf()]
        UH = sb("UH", [128, 16, 2, 16]); b_UH = Buf()
        ckvT0 = sb("ckvT0", [128, 4, 512], BF16); ckvT = [ckvT0, ckvT0]; b_ck0 = Buf(); b_ckvT = [b_ck0, b_ck0]
        krT = sb("krT", [64, 512], BF16); b_krT = Buf()
        Vts = [(KVR[:, 0:2048], [b_kvc[0], b_kvc[1]]), (KVR[:, 2048:4096], [b_vc[0], b_vc[1]])]
        KTt = [sb("KTt%d" % k, [128, 512], BF16) for k in range(2)]; b_KTt = [Buf(), Buf()]
        wpool_sb = sb("wpool_sb", [128, 4, 512], BF16); b_wpool = Buf()
        QnT = sb("QnT", [128, 512], BF16); b_Qn = Buf()
        QrT = sb("QrT", [128, 512], BF16); b_Qr = Buf()
        QnT2 = sb("QnT2", [128, 512], BF16); b_Qn2 = Buf()
        QrT2 = sb("QrT2", [128, 512], BF16); b_Qr2 = Buf()

        pA = ps("pA", [128, 512]); pB = ps("pB", [128, 512])
        pS = [ps("pS0", [128, 512]), ps("pS1", [128, 512])]
        pO = ps("pO", [128, 512]); pR = ps("pR", [128, 512])
        pT = [ps("pT0", [128, 512]), ps("pT1", [128, 512])]
        b_pA = Buf(True); b_pB = Buf(True); b_pS = [Buf(True), Buf(True)]; b_pO = Buf(True); b_pR = Buf(True); b_pT = [Buf(True), Buf(True)]
        acc = [(pA, b_pA), (pB, b_pB)]
        st = {"acc": 0, "pT": 0, "xs": 0, "slot": 0, "ev": 0, "kvc": 0, "pS": 0, "B": 0, "ktt": 0, "ckv": 0}

        wkv_v = RM[:, 0:32 * 512].rearrange("p (c e) -> p c e", e=512)
        wkr_v = cqnT[:, :, :].rearrange("p a b -> p (a b)").rearrange("p (c e) -> p c e", e=128)
        mixT = RM[:, 0:32 * 512].rearrange("p (c e) -> p c e", e=512)
        wk_v = RW[:, 0:8192].rearrange("p (c e) -> p c e", e=2048)
        wv_v = RW[:, 8192:16384].rearrange("p (c e) -> p c e", e=2048)
        slots = [RW[:, k * 4096:(k + 1) * 4096] for k in range(4)]

        def mm(out, lhsT, rhs, start, stop, reads, writes, signal=None):
            if signal is None:
                signal = stop
            T.op("pe", lambda e: e.matmul(out, lhsT=lhsT, rhs=rhs, start=start, stop=stop), reads, writes, signal)

        def tr(out, in_, idn, reads, writes, signal):
            T.op("pe", lambda e: e.transpose(out=out, in_=in_, identity=idn), reads, writes, signal)

        def act(out, in_, func, reads, writes, scale=1.0, bias=None, accum=None):
            kw = {}
            if bias is not None:
                kw["bias"] = bias
            if accum is not None:
                kw["accum_out"] = accum
            T.op("act", lambda e: e.activation(out=out, in_=in_, func=func, scale=scale, **kw), reads, writes)

        def stt(out, in0, scalar, in1, op0, op1, reads, writes):
            T.op("dve", lambda e: e.scalar_tensor_tensor(out=out, in0=in0, scalar=scalar, in1=in1, op0=op0, op1=op1), reads, writes)

        def tt(out, in0, in1, op, reads, writes):
            T.op("dve", lambda e: e.tensor_tensor(out=out, in0=in0, in1=in1, op=op), reads, writes)

        def ts(out, in0, s1, s2, op0, op1, reads, writes):
            T.op("dve", lambda e: e.tensor_scalar(out=out, in0=in0, scalar1=s1, scalar2=s2, op0=op0, op1=op1), reads, writes)

        def cp(out, in_, reads, writes, eng=None):
            if eng is None:
                st["ev"] ^= 1
                eng = "act" if st["ev"] else "dve"
            if eng == "act":
                act(out, in_, AF.Copy, reads, writes)
            else:
                T.op("dve", lambda e: e.tensor_copy(out=out, in_=in_), reads, writes)

        def memset(ap, val, writes):
            T.op("dve", lambda e: e.memset(ap, val), (), writes)

        def next_acc():
            st["acc"] ^= 1
            return acc[st["acc"]]

        def next_pT():
            st["pT"] ^= 1
            return pT[st["pT"]], b_pT[st["pT"]]

        def rstd_from(psum_ap, b_ps, n_feat, wl, b_wl, wr, b_wr, npart):
            act(wl, psum_ap, AF.Ln, [b_ps, b_eps], [b_wl], scale=1.0 / n_feat, bias=eps_t[0:npart, 0:1])
            act(wr, wl, AF.Exp, [b_wl], [b_wr], scale=-0.5)

        T.dma("sp", ident_f[:, :], ident[:, :], (), [b_ident])
        T.dma("sp", gcols[:, :], gcols_d[:, :], (), [b_g])
        T.dma("pool", EKb[:, :], EK_d[:, :], (), [b_ek])
        memset(QrT[:, :], 0.0, [b_Qr])
        T.dma("pool", QrT[64:96, :], LQ_d[:, :], (), [b_Qr])
        memset(QrT2[:, :], 0.0, [b_Qr2])
        T.dma("pool", QrT2[64:96, :], LQ_d[:, :], (), [b_Qr2])
        memset(KVR[64:128, 4096:6144], 0.0, [b_kvc[0], b_kvc[1]])
        memset(ones_b[:, :], 1.0, [b_ones])
        memset(eps_t[:, :], EPS, [b_eps])
        MARKS["const"] = T.n_emit

        def x_front(x_rows, ntok):
            k = st["xs"] = st["xs"] ^ 1
            X = xs[k]; bX = b_xs[k]; S = ssx[k]; bS = b_ssx[k]
            T.dma("sp", X[0:ntok, :], x_rows, (), [bX])
            memset(S[:, :], 0.0, [bS])
            for hh in range(2):
                act(junk[0:ntok, :], X[0:ntok, hh * 2048:(hh + 1) * 2048], AF.Square, [bX], [b_kvc[0], b_kvc[1], bS],
                    accum=S[0:ntok, hh:hh + 1])
            act(S[0:ntok, 2:3], S[0:ntok, 0:1], AF.Identity, [bS], [bS], bias=S[0:ntok, 1:2])
            act(S[0:ntok, 3:4], S[0:ntok, 2:3], AF.Ln, [bS, b_eps], [bS], scale=1.0 / D, bias=eps_t[0:ntok, 0:1])
            act(S[0:ntok, 4:5], S[0:ntok, 3:4], AF.Exp, [bS], [bS], scale=-0.5)
            for hh in range(2):
                act(X[0:ntok, hh * 2048:(hh + 1) * 2048], X[0:ntok, hh * 2048:(hh + 1) * 2048], AF.Identity, [bS], [bX],
                    scale=S[0:ntok, 4:5])
            return k

        def x_back(k, ntok, dst, b_dst):
            X = xs[k]; bX = b_xs[k]
            for c4 in range(8):
                P, bP = next_pT()
                for q in range(4):
                    c = c4 * 4 + q
                    tr(P[:, q * 128:q * 128 + ntok], X[0:ntok, c * 128:(c + 1) * 128], ident_f[0:ntok, 0:ntok],
                       [bX, b_ident], [bP], q == 3)
                for q in range(4):
                    c = c4 * 4 + q
                    ts(dst[:, c, :], P[:, q * 128:q * 128 + ntok], gcols[:, c:c + 1], 1.0, ALU.mult, ALU.mult,
                       [bP, b_g], [b_dst])

        def x_to_hT(x_rows, ntok, dst, b_dst):
            x_back(x_front(x_rows, ntok), ntok, dst, b_dst)

        def stream_chunk(src_tile, kc, width):
            k = st["slot"] = (st["slot"] + 1) % 4
            v = slots[k][:, 0:kc * width].rearrange("p (c e) -> p c e", e=width)
            T.dma("sp", v, src_tile, [b_conv], [b_slot[k]])
            return v, b_slot[k]

        def proj(wv, bw, col0, M, rhs_fn, kc, n, out_ps, b_out, rhs_bufs):
            for k in range(kc):
                mm(out_ps[0:M, 0:n], wv[:, k, col0:col0 + M], rhs_fn(k), k == 0, k == kc - 1, [bw] + rhs_bufs, [b_out])

        def out_tokmajor(srcs, bsrcs, npart, n, dst_rows_fn):
            for t0 in range(0, n, 128):
                ntok = min(128, n - t0)
                P, bP = next_pT()
                for q, s in enumerate(srcs):
                    tr(P[0:ntok, q * npart:(q + 1) * npart], s[0:npart, t0:t0 + ntok], ident_f[0:npart, 0:npart],
                       bsrcs + [b_ident], [bP], q == len(srcs) - 1)
                wi = st["B"] = (st["B"] + 1) % 2
                Wt, bWt = (W[7], b_W[7]) if wi == 0 else (PT[0], b_PT[0])
                cp(Wt[0:ntok, 0:len(srcs) * npart], P[0:ntok, 0:len(srcs) * npart], [bP], [bWt])
                T.dma("pool", dst_rows_fn(t0, ntok), Wt[0:ntok, 0:len(srcs) * npart], [bWt], ())

        def latents(get_w, rhs_fn, rhs_bufs, n, ropeC_ap, ropeS_ap, ck, b_ck, out_ckv_fn=None, out_kr_fn=None):
            RC, bRC = W[5], b_W[5]
            RS, bRS = W[6], b_W[6]
            T.dma("sp", RC[0:64, 0:n], ropeC_ap, (), [bRC])
            T.dma("sp", RS[0:64, 0:n], ropeS_ap, (), [bRS])
            for c4 in range(4):
                wv, bw, col0 = get_w(c4)
                P, bP = next_acc()
                proj(wv, bw, col0, 128, rhs_fn, 32, n, P, bP, rhs_bufs)
                cp(W[c4][:, 0:n], P[:, 0:n], [bP], [b_W[c4]], eng="dve")
                act(Bt[4 + c4 % 2][:, 0:n], P[:, 0:n], AF.Square, [bP], [b_B[4 + c4 % 2]])
                if c4 > 0:
                    mm(pR[:, 0:n], ones_b[:, :], Bt[4 + (c4 - 1) % 2][:, 0:n], c4 == 1, False, [b_ones, b_B[4 + (c4 - 1) % 2]], [b_pR], False)
            mm(pR[:, 0:n], ones_b[:, :], Bt[5][:, 0:n], False, True, [b_ones, b_B[5]], [b_pR], True)
            rstd_from(pR[:, 0:n], b_pR, KVL, W[4][:, 0:n], b_W[4], W[4][:, 0:n], b_W[4], 128)
            for c4 in range(4):
                stt(ck[:, c4, 0:n], W[c4][:, 0:n], gcols[:, 40 + c4:41 + c4], W[4][:, 0:n], ALU.mult, ALU.mult,
                    [b_W[c4], b_W[4], b_g], [b_ck])
                if out_ckv_fn is not None:
                    stt(W[c4][:, 0:n], W[c4][:, 0:n], gcols[:, 40 + c4:41 + c4], W[4][:, 0:n], ALU.mult, ALU.mult,
                        [b_W[4], b_g], [b_W[c4]])
            if out_ckv_fn is not None:
                out_tokmajor([W[c][:, :] for c in range(4)], [b_W[c] for c in range(4)], 128, n, out_ckv_fn)
            wv, bw, col0 = get_w(4)
            proj(wv, bw, col0, 64, rhs_fn, 32, n, pA, b_pA, rhs_bufs)
            wv, bw, col0 = get_w(5)
            proj(wv, bw, col0, 64, rhs_fn, 32, n, pB, b_pB, rhs_bufs)
            act(Bt[4][0:64, 0:n], pA[0:64, 0:n], AF.Square, [b_pA], [b_B[4]])
            mm(pR[0:64, 0:n], ones_b[0:64, 0:64], Bt[4][0:64, 0:n], True, True, [b_ones, b_B[4]], [b_pR])
            rstd_from(pR[0:64, 0:n], b_pR, DR, W[4][0:64, 0:n], b_W[4], W[4][0:64, 0:n], b_W[4], 64)
            stt(W[0][0:64, 0:n], pA[0:64, 0:n], gcols[0:64, 48:49], RC[0:64, 0:n], ALU.mult, ALU.mult, [b_pA, bRC, b_g], [b_W[0]])
            stt(W[1][0:64, 0:n], pB[0:64, 0:n], gcols[0:64, 49:50], RS[0:64, 0:n], ALU.mult, ALU.mult, [b_pB, bRS, b_g], [b_W[1]])
            tt(W[0][0:64, 0:n], W[0][0:64, 0:n], W[1][0:64, 0:n], ALU.add, [b_W[1]], [b_W[0]])
            tt(krT[0:64, 0:n], W[0][0:64, 0:n], W[4][0:64, 0:n], ALU.mult, [b_W[0], b_W[4]], [b_krT])
            if out_kr_fn is not None:
                tt(W[1][0:64, 0:n], W[0][0:64, 0:n], W[4][0:64, 0:n], ALU.mult, [b_W[0], b_W[4]], [b_W[1]])
                out_tokmajor([W[1][:, :]], [b_W[1]], 64, n, out_kr_fn)

        def kv_block(ck, b_ck, n, kt_dst_fn, v_dst_fn):
            accs3 = [(pA, b_pA), (pB, b_pB), (pO, b_pO)]
            wrs = [(W[2], b_W[2]), (W[3], b_W[3]), (W[7], b_W[7])]
            hold = {}

            def k_stage1(h):
                P, bP = accs3[h % 3]
                for c in range(4):
                    mm(P[:, 0:n], wk_v[:, c, h * 128:(h + 1) * 128], ck[:, c, 0:n], c == 0, c == 3, [b_slot[0], b_slot[1], b_ck], [bP])
                act(Bt[h % 3][:, 0:n], P[:, 0:n], AF.Square, [bP], [b_B[h % 3]])

            def k_stage2(h):
                P, bP = accs3[h % 3]
                X, bX = next_pT()
                mm(X[:, 0:n], ones_b[:, :], Bt[h % 3][:, 0:n], True, True, [b_ones, b_B[h % 3]], [bX])
                Wr, bWr = wrs[h % 3]
                rstd_from(X[:, 0:n], bX, 128, Wr[:, 0:n], bWr, Wr[:, 0:n], bWr, 128)
                ki = h % 2
                stt(KTt[ki][:, 0:n], P[:, 0:n], gcols[:, 45:46], Wr[:, 0:n], ALU.mult, ALU.mult, [bP, bWr, b_g], [b_KTt[ki]])
                T.dma("pool", kt_dst_fn(h), KTt[ki][:, 0:n], [b_KTt[ki]], [b_scr])

            k_stage1(0)
            for h in range(NH):
                if h + 1 < NH:
                    k_stage1(h + 1)
                k_stage2(h)
            for t0 in range(0, n, 128):
                ntok = min(128, n - t0)
                vi = st["vt"] = st.get("vt", 0) ^ 1
                Vt, bVt = Vts[vi]
                for hg in range(4):
                    P, bP = next_acc()
                    for c in range(4):
                        mm(P[0:ntok, :], ck[:, c, t0:t0 + ntok], wv_v[:, c, hg * 512:(hg + 1) * 512], c == 0, c == 3,
                           [b_slot[2], b_slot[3], b_ck], [bP])
                    cp(Vt[0:ntok, hg * 512:(hg + 1) * 512], P[0:ntok, :], [bP], bVt)
                T.dma("pool", v_dst_fn(t0 // 128, ntok), Vt[0:ntok, :].rearrange("p (h d) -> p h d", d=128), bVt, [b_scr])

        T.dma("pool", wkv_v[:, :, 0:512], w_in[:, OFF_KV:OFF_KV + 512].rearrange("(c p) e -> p c e", p=128), (), [b_RM])
        T.dma("pool", wkr_v[:, :, 0:64], w_in[:, OFF_KR:OFF_KR + 64].rearrange("(c p) e -> p c e", p=128), (), [b_cq])
        T.dma("pool", wkr_v[:, :, 64:96], w_in[:, OFF_KR + 32:OFF_KR + 64].rearrange("(c p) e -> p c e", p=128), (), [b_cq])
        T.dma("pool", wkr_v[:, :, 96:128], w_in[:, OFF_KR:OFF_KR + 32].rearrange("(c p) e -> p c e", p=128), (), [b_cq])
        for c in range(4):
            T.dma("pool", wk_v[:, c, :].rearrange("p (h d) -> p h d", d=128), w_ukv[c * 128:(c + 1) * 128, :, 0:128], (), [b_slot[0], b_slot[1]])
            T.dma("pool", wv_v[:, c, :].rearrange("p (h d) -> p h d", d=128), w_ukv[c * 128:(c + 1) * 128, :, 128:256], (), [b_slot[2], b_slot[3]])

        def win_idx(col0):
            return col0 // 128 if col0 < OFF_KR else 13 + (col0 - OFF_ZA) // 128
        rr = lambda ap: ap.rearrange("(c p) e -> p c e", p=128)
        conv = []
        for col0 in list(range(0, OFF_KR, 128)) + list(range(OFF_ZA, DIN, 128)):
            conv.append((WIN[win_idx(col0)], rr(w_in[:, col0:col0 + 128])))
        conv.append((WIN[12, :, :, 0:64], rr(w_in[:, OFF_KR:OFF_KR + 64])))
        conv.append((WIN[12, :, :, 64:96], rr(w_in[:, OFF_KR + 32:OFF_KR + 64])))
        conv.append((WIN[12, :, :, 96:128], rr(w_in[:, OFF_KR:OFF_KR + 32])))
        for h in range(NH):
            conv.append((WQ[h, :, :, 0:192], rr(w_uq[:, h, :])))
            conv.append((WQ[h, :, :, 192:224], rr(w_uq[:, h, 160:192])))
            conv.append((WQ[h, :, :, 224:256], rr(w_uq[:, h, 128:160])))
        for g in range(4):
            conv.append((WP[g], rr(w_pool[g, :, :])))
        for dc in range(32):
            conv.append((WOUT[dc], rr(w_out[:, dc * 128:(dc + 1) * 128])))
        conv_pos = [0]

        def conv_some(k):
            while k > 0 and conv_pos[0] < len(conv):
                d_, s_ = conv[conv_pos[0]]
                T.dma("pool", d_, s_, (), [b_conv])
                conv_pos[0] += 1
                k -= 1

        def get_w_res(j):
            if j < 4:
                return wkv_v, b_RM, j * 128
            return wkr_v, b_cq, (j - 4) * 64

        pend = x_front(xseq[0:128, :], 128)
        for blk in range(16):
            conv_some(6)
            for t4 in range(4):
                r0 = blk * 512 + t4 * 128
                cur = pend
                if r0 + 128 < SEQ:
                    pend = x_front(xseq[r0 + 128:r0 + 256, :], 128)
                x_back(cur, 128, hT[:, :, t4 * 128:(t4 + 1) * 128], b_hT)
            ci = st["ckv"] = st["ckv"] ^ 1
            latents(get_w_res, lambda k: hT[:, k, 0:512], [b_hT], 512,
                    ropeA[0, :, blk * 512:(blk + 1) * 512], ropeA[1, :, blk * 512:(blk + 1) * 512], ckvT[ci], b_ckvT[ci])
            T.dma("pool", krp[:, blk * 512:(blk + 1) * 512], krT[:, :], [b_krT], [b_scr])
            kv_block(ckvT[ci], b_ckvT[ci], 512,
                     lambda h, blk=blk: KTp[h, :, blk * 512:(blk + 1) * 512],
                     lambda t4, ntok, blk=blk: Vp[:, blk, 0:ntok, t4, :].rearrange("h p d -> p h d"))
            MARKS["A%d" % blk] = T.n_emit

        for s in range(2):
            for blk in range(8):
                conv_some(6)
                k = st["xs"] = st["xs"] ^ 1
                X = xs[k]; bX = b_xs[k]
                Xv = X[:, 0:2048].rearrange("p (t f) -> p t f", f=512)
                T.dma("sp", Xv, cckv[s, blk * 512:(blk + 1) * 512, :].rearrange("(t p) f -> p t f", p=128), (), [bX])
                Kv = X[:, 2048:2304].rearrange("p (t f) -> p t f", f=64)
                T.dma("sp", Kv, ckr[s, blk * 512:(blk + 1) * 512, :].rearrange("(t p) f -> p t f", p=128), (), [bX])
                ci = st["ckv"] = st["ckv"] ^ 1
                for t4 in range(4):
                    P, bP = next_pT()
                    for c in range(4):
                        tr(P[:, c * 128:(c + 1) * 128], Xv[:, t4, c * 128:(c + 1) * 128], ident_f[:, :], [bX, b_ident], [bP], c == 3)
                    cp(ckvT[ci][:, :, t4 * 128:(t4 + 1) * 128], P[:, :].rearrange("p (c t) -> p c t", t=128), [bP], [b_ckvT[ci]])
                P, bP = next_pT()
                for t4 in range(4):
                    tr(P[0:64, t4 * 128:(t4 + 1) * 128], Kv[:, t4, :], ident_f[:, :], [bX, b_ident], [bP], t4 == 3)
                cp(krT[:, :], P[0:64, :], [bP], [b_krT])
                T.dma("pool", krs[s, :, blk * 512:(blk + 1) * 512], krT[:, :], [b_krT], [b_scr])
                kv_block(ckvT[ci], b_ckvT[ci], 512,
                         lambda h, blk=blk, s=s: KTs[s, h, :, blk * 512:(blk + 1) * 512],
                         lambda t4, ntok, blk=blk, s=s: Vs[s, :, blk, 0:ntok, t4, :].rearrange("h p d -> p h d"))
                MARKS["SC%d_%d" % (s, blk)] = T.n_emit

        conv_some(10 ** 6)
        x_to_hT(xsam[:, :], 128, hT[:, :, 0:128], b_hT)
        ci = st["ckv"] = st["ckv"] ^ 1
        latents(get_w_res, lambda k: hT[:, k, 0:128], [b_hT], 128, ropeS[0, :, :], ropeS[1, :, :], ckvT[ci], b_ckvT[ci],
                out_ckv_fn=lambda t0, ntok: sckv[t0:t0 + ntok, :], out_kr_fn=lambda t0, ntok: skr[t0:t0 + ntok, :])
        for s in range(2):
            T.dma("pool", krs[s, :, 4096:4160], krT[:, s * 64:(s + 1) * 64], [b_krT], [b_scr])
        for h in range(NH):
            P, bP = next_acc()
            for c in range(4):
                mm(P[:, 0:128], wk_v[:, c, h * 128:(h + 1) * 128], ckvT[ci][:, c, 0:128], c == 0, c == 3, [b_slot[0], b_slot[1], b_ckvT[ci]], [bP])
            act(Bt[0][:, 0:128], P[:, 0:128], AF.Square, [bP], [b_B[0]])
            X, bX = next_pT()
            mm(X[:, 0:128], ones_b[:, :], Bt[0][:, 0:128], True, True, [b_ones, b_B[0]], [bX])
            rstd_from(X[:, 0:128], bX, 128, W[2][:, 0:128], b_W[2], W[2][:, 0:128], b_W[2], 128)
            ki = st["ktt"] = st["ktt"] ^ 1
            stt(KTt[ki][:, 0:128], P[:, 0:128], gcols[:, 45:46], W[2][:, 0:128], ALU.mult, ALU.mult, [bP, b_W[2], b_g], [b_KTt[ki]])
            for s in range(2):
                T.dma("pool", KTs[s, h, :, 4096:4160], KTt[ki][:, s * 64:(s + 1) * 64], [b_KTt[ki]], [b_scr])
        for s in range(2):
            for hg in range(4):
                P, bP = next_acc()
                for c in range(4):
                    mm(P[0:64, :], ckvT[ci][:, c, s * 64:(s + 1) * 64], wv_v[:, c, hg * 512:(hg + 1) * 512], c == 0, c == 3,
                       [b_slot[2], b_slot[3], b_ckvT[ci]], [bP])
                cp(Vts[s][0][0:64, hg * 512:(hg + 1) * 512], P[0:64, :], [bP], Vts[s][1])
            T.dma("pool", Vs[s, :, 8, 0:64, 0, :].rearrange("h p d -> p h d"), Vts[s][0][0:64, :].rearrange("p (h d) -> p h d", d=128), Vts[s][1], [b_scr])

        MARKS["SN"] = T.n_emit
        memset(UH[:, :, :, :], 0.0, [b_UH])
        for s in range(2):
            k = st["xs"] = st["xs"] ^ 1
            X = xs[k]; bX = b_xs[k]
            T.dma("sp", X[0:15, 0:2048], spool[s, :, :], (), [bX])
            for c4 in range(4):
                P, bP = next_pT()
                for q in range(4):
                    c = c4 * 4 + q
                    tr(P[:, q * 16:q * 16 + 15], X[0:15, c * 128:(c + 1) * 128], ident_f[0:15, 0:15], [bX, b_ident], [bP], q == 3)
                cp(UH[:, c4 * 4:(c4 + 1) * 4, s, 1:16], P[:, 0:64].rearrange("p (c t) -> p c t", t=16)[:, :, 0:15], [bP], [b_UH])

        MARKS["UH"] = T.n_emit
        def wslice(col0, width):
            assert width == 128
            return stream_chunk(WIN[win_idx(col0)], 32, 128)

        def own_block(kind, i):
            if kind == "p":
                n, nseq, nt = 512, 1, 512
            else:
                n, nseq, nt = 128, 2, 64
            L = 16 + nt
            rhs = lambda k: hT[:, k, 0:n]
            if kind == "p":
                kh = x_front(xown[i, 0:16, :], 16)
                pend_o = x_front(xown[i, 16:144, :], 128)
                x_back(kh, 16, hTh[:, :, 0:16], b_hTh)
                for t4 in range(4):
                    cur = pend_o
                    if t4 + 1 < 4:
                        pend_o = x_front(xown[i, 16 + (t4 + 1) * 128:16 + (t4 + 2) * 128, :], 128)
                    x_back(cur, 128, hT[:, :, t4 * 128:(t4 + 1) * 128], b_hT)
                kr_w = [None]

                def get_w_str(j):
                    if j < 4:
                        v, b = wslice(OFF_KV + j * 128, 128)
                        return v, b, 0
                    if j == 4:
                        v, b = stream_chunk(WIN[12], 32, 128)
                        kr_w[0] = (v, b)
                        return v, b, 0
                    v, b = kr_w[0]
                    return v, b, 64
                ci = st["ckv"] = st["ckv"] ^ 1
                latents(get_w_str, rhs, [b_hT], n, ropeO[i, 0, :, :], ropeO[i, 1, :, :], ckvT[ci], b_ckvT[ci],
                        out_ckv_fn=lambda t0, ntok: ockv[i, t0:t0 + ntok, :], out_kr_fn=lambda t0, ntok: okr[i, t0:t0 + ntok, :])
            MARKS["B1_%s%d" % (kind, i)] = T.n_emit
            rawt = [W[0], W[1], W[2], W[3], PT[0], PT[1], W[5], W[6]]
            rawb = [b_W[0], b_W[1], b_W[2], b_W[3], b_PT[0], b_PT[1], b_W[5], b_W[6]]
            for c8 in range(8):
                v, b = wslice(OFF_Q + c8 * 128, 128)
                P, bP = next_acc()
                proj(v, b, 0, 128, rhs, 32, n, P, bP, [b_hT])
                cp(rawt[c8][:, 0:n], P[:, 0:n], [bP], [rawb[c8]], eng="dve")
                act(Bt[4 + c8 % 2][:, 0:n], P[:, 0:n], AF.Square, [bP], [b_B[4 + c8 % 2]])
                if c8 > 0:
                    mm(pR[:, 0:n], ones_b[:, :], Bt[4 + (c8 - 1) % 2][:, 0:n], c8 == 1, False, [b_ones, b_B[4 + (c8 - 1) % 2]], [b_pR], False)
            mm(pR[:, 0:n], ones_b[:, :], Bt[5][:, 0:n], False, True, [b_ones, b_B[5]], [b_pR], True)
            rstd_from(pR[:, 0:n], b_pR, QL, W[4][:, 0:n], b_W[4], W[4][:, 0:n], b_W[4], 128)
            for c8 in range(8):
                stt(cqnT[:, c8, 0:n], rawt[c8][:, 0:n], gcols[:, 32 + c8:33 + c8], W[4][:, 0:n], ALU.mult, ALU.mult,
                    [rawb[c8], b_W[4], b_g], [b_cq])

            def silu_from(P, bP, Wg, bWg):
                act(PT[1][:, 0:n], P[:, 0:n], AF.Exp, [bP], [b_PT[1]], scale=-1.0)
                ts(PT[1][:, 0:n], PT[1][:, 0:n], 1.0, 1.0, ALU.add, ALU.mult, [], [b_PT[1]])
                T.op("dve", lambda e: e.reciprocal(out=PT[1][:, 0:n], in_=PT[1][:, 0:n]), [], [b_PT[1]])
                stt(Wg[:, 0:n], PT[1][:, 0:n], 1.0, P[:, 0:n], ALU.mult, ALU.mult, [b_PT[1], bP], [bWg])

            def gate_from(colbase, Wg, bWg):
                v, b = wslice(colbase, 128)
                P, bP = next_acc()
                proj(v, b, 0, 128, rhs, 32, n, P, bP, [b_hT])
                silu_from(P, bP, Wg, bWg)

            MARKS["B2_%s%d" % (kind, i)] = T.n_emit
            IC, bIC = W[7], b_W[7]
            Uv = U[:, :, 0:nseq * L].rearrange("p c (s l) -> p c s l", l=L)
            pout = xs[0]; bpout = b_xs[0]
            for g in range(4):
                wwin = 2 << g
                if kind == "p":
                    T.dma("sp", IC[:, 0:n], invc_d[i, :, g, :], (), [bIC])
                else:
                    T.dma("sp", IC[:, 0:n], invs_d[:, g, :], (), [bIC])
                for c in range(4):
                    v, b = wslice(OFF_U + (g * 4 + c) * 128, 128)
                    P, bP = next_acc()
                    proj(v, b, 0, 128, rhs, 32, n, P, bP, [b_hT])
                    cp(Uv[:, c, :, 16:L], P[:, 0:n].rearrange("p (s t) -> p s t", t=nt), [bP], [b_U], eng="dve")
                    if kind == "p":
                        P2, bP2 = next_acc()
                        proj(v, b, 0, 128, lambda k: hTh[:, k, 0:16], 32, 16, P2, bP2, [b_hTh])
                        cp(Uv[:, c, 0, 0:16], P2[:, 0:16], [bP2], [b_U], eng="dve")
                    else:
                        cp(Uv[:, c, :, 0:16], UH[:, g * 4 + c, :, :], [b_UH], [b_U], eng="dve")
                for s in range(nseq):
                    P, bP = next_pT()
                    for c in range(4):
                        tr(P[0:15, c * 128:(c + 1) * 128], Uv[:, c, s, L - 15:L], ident_f[:, :], [b_U, b_ident], [bP], c == 3)
                    cp(pout[32 * s:32 * s + 15, g * 512:(g + 1) * 512], P[0:15, :], [bP], [bpout], eng="dve")
                for c in range(4):
                    src = Uv[:, c, :, :]
                    sh = 1
                    cur = src; bcur = b_U
                    pi = 0
                    lo = 0
                    while sh < wwin:
                        lo2 = lo + sh
                        dstt = PT[pi][:, 0:nseq * L].rearrange("p (s l) -> p s l", l=L)
                        tt(dstt[:, :, lo2:L], cur[:, :, lo2:L], cur[:, :, lo:L - sh], ALU.add, [bcur], [b_PT[pi]])
                        cur = dstt; bcur = b_PT[pi]
                        pi ^= 1
                        lo = lo2
                        sh *= 2
                    tmp = PT[pi][:, 0:n].rearrange("p (s t) -> p s t", t=nt)
                    tt(tmp, cur[:, :, 16:L], IC[:, 0:n].rearrange("p (s t) -> p s t", t=nt), ALU.mult, [bcur, bIC], [b_PT[pi]])
                    tt(Bt[c][:, 0:n].rearrange("p (s t) -> p s t", t=nt), tmp, src[:, :, 16:L], ALU.subtract, [b_PT[pi], b_U], [b_B[c]])
                wp = wpool_sb
                T.dma("sp", wp[:, :, :], WP[g], [b_conv], [b_wpool])
                for ec in range(4):
                    gate_from(OFF_ZP + (g * 4 + ec) * 128, W[4], b_W[4])
                    for c in range(4):
                        mm(pO[:, 0:n], wp[:, c, ec * 128:(ec + 1) * 128], Bt[c][:, 0:n], c == 0, c == 3, [b_wpool, b_B[c]], [b_pO])
                    stt(W[4][:, 0:n], W[4][:, 0:n], 1.0, pO[:, 0:n], ALU.mult, ALU.mult, [b_pO], [b_W[4]])
                    ts(mixT[:, 16 + g * 4 + ec, 0:n], W[4][:, 0:n], gcols[:, 50 + g * 4 + ec:51 + g * 4 + ec], 1.0, ALU.mult, ALU.mult,
                       [b_W[4], b_g], [b_RM])
            for s in range(nseq):
                dst = opool[i, :, :] if kind == "p" else spo[s, :, :]
                T.dma("pool", dst, pout[32 * s:32 * s + 15, 0:2048], [bpout], ())

            MARKS["B3_%s%d" % (kind, i)] = T.n_emit
            if kind == "p":
                E = 2048 * (i + 1)
                seqs = [(0, 512, E, None)]
            else:
                seqs = [(0, 64, 4160, 0), (64, 64, 4160, 1)]
            RC, bRC = W[5], b_W[5]
            RS, bRS = W[6], b_W[6]
            if kind == "p":
                T.dma("sp", RC[0:64, 0:n], ropeO[i, 0, :, :], (), [bRC])
                T.dma("sp", RS[0:64, 0:n], ropeO[i, 1, :, :], (), [bRS])
            else:
                T.dma("sp", RC[0:64, 0:n], ropeS[0, :, :], (), [bRC])
                T.dma("sp", RS[0:64, 0:n], ropeS[1, :, :], (), [bRS])
            qbuf = [(QnT, b_Qn, QrT, b_Qr, W[7], b_W[7]), (QnT2, b_Qn2, QrT2, b_Qr2, W[4], b_W[4])]

            def prep_stages(h):
                Qn_, bQn_, Qr_, bQr_, Wg, bWg = qbuf[h % 2]
                stt_ = {}
                qr = lambda kk: cqnT[:, kk, 0:n]

                def s1():
                    v, b = wslice(OFF_ZA + h * 128, 128)
                    P, bP = next_acc()
                    proj(v, b, 0, 128, rhs, 32, n, P, bP, [b_hT])
                    stt_["g"] = (P, bP)

                def s2():
                    P, bP = stt_["g"]
                    silu_from(P, bP, Wg, bWg)

                def s3():
                    k = st["slot"] = (st["slot"] + 1) % 4
                    wq = slots[k][:, 0:8 * 256].rearrange("p (c e) -> p c e", e=256)
                    T.dma("sp", wq, WQ[h], [b_conv], [b_slot[k]])
                    P, bP = next_acc()
                    proj(wq, b_slot[k], 0, 128, qr, 8, n, P, bP, [b_cq])
                    P1, bP1 = next_acc()
                    proj(wq, b_slot[k], 128, 64, qr, 8, n, P1, bP1, [b_cq])
                    stt_["q"] = (wq, b_slot[k], P, bP, P1, bP1)

                def s4():
                    wq, bwq, P, bP, P1, bP1 = stt_["q"]
                    act(Bt[2][:, 0:n], P[:, 0:n], AF.Square, [bP], [b_B[2]])
                    cp(W[1][0:64, 0:n], P1[0:64, 0:n], [bP1], [b_W[1]], eng="dve")
                    act(Bt[3][0:64, 0:n], P1[0:64, 0:n], AF.Square, [bP1], [b_B[3]])
                    P2, bP2 = next_pT()
                    proj(wq, bwq, 192, 64, qr, 8, n, P2, bP2, [b_cq])
                    stt_["p2"] = (P2, bP2)

                def s5():
                    wq, bwq, P, bP, P1, bP1 = stt_["q"]
                    P2, bP2 = stt_["p2"]
                    X, bX = next_pT()
                    mm(X[:, 0:n], ones_b[:, :], Bt[2][:, 0:n], True, True, [b_ones, b_B[2]], [bX])
                    rstd_from(X[:, 0:n], bX, 128, W[0][:, 0:n], b_W[0], W[0][:, 0:n], b_W[0], 128)
                    stt(Qn_[:, 0:n], P[:, 0:n], gcols[:, 44:45], W[0][:, 0:n], ALU.mult, ALU.mult, [bP, b_W[0], b_g], [bQn_])
                    stt(W[2][0:64, 0:n], P2[0:64, 0:n], gcols[0:64, 47:48], RS[0:64, 0:n], ALU.mult, ALU.mult, [bP2, bRS, b_g], [b_W[2]])

                def s6():
                    X, bX = next_pT()
                    mm(X[0:64, 0:n], ones_b[0:64, 0:64], Bt[3][0:64, 0:n], True, True, [b_ones, b_B[3]], [bX])
                    rstd_from(X[0:64, 0:n], bX, DR, W[0][0:64, 0:n], b_W[0], W[0][0:64, 0:n], b_W[0], 64)
                    stt(W[1][0:64, 0:n], W[1][0:64, 0:n], gcols[0:64, 46:47], RC[0:64, 0:n], ALU.mult, ALU.mult, [bRC, b_g], [b_W[1]])
                    tt(W[1][0:64, 0:n], W[1][0:64, 0:n], W[2][0:64, 0:n], ALU.add, [b_W[2]], [b_W[1]])
                    tt(Qr_[0:64, 0:n], W[1][0:64, 0:n], W[0][0:64, 0:n], ALU.mult, [b_W[1], b_W[0]], [bQr_])

                return [s1, s2, s3, s4, s5, s6]

            def prep(h):
                for f_ in prep_stages(h):
                    f_()

            def load_chunk(h, sidx, S, ci, ntile):
                kc = st["kvc"] = st["kvc"] ^ 1
                bk = b_kvc[kc]
                k0 = ci * 1024
                nkc = min(1024, S - k0)
                if sidx is None:
                    nch = ntile // 8
                    eoff = 1024 if ci == nch - 2 else (2048 if ci == nch - 1 else 0)
                    T.dma("sp", KTc[kc][:, 0:nkc], KTp[h, :, k0:k0 + nkc], [b_scr], [bk])
                    T.dma("sp", krc[kc][0:64, 0:nkc], krp[:, k0:k0 + nkc], [b_scr], [bk])
                    T.dma("sp", krc[kc][64:96, 0:nkc], EKb[:, eoff:eoff + nkc], [b_ek], [bk])
                    T.dma("sp", Vc[kc][:, :, :, :], Vp[h, k0 // 512:k0 // 512 + 2, :, :, :].rearrange("b p t d -> p b t d"), [b_scr], [b_vc[kc]])
                else:
                    T.dma("sp", KTc[kc][:, 0:nkc], KTs[sidx, h, :, k0:k0 + nkc], [b_scr], [bk])
                    T.dma("sp", krc[kc][0:64, 0:nkc], krs[sidx, :, k0:k0 + nkc], [b_scr], [bk])
                    T.dma("sp", krc[kc][64:96, 0:nkc], EKb[:, 0:nkc], [b_ek], [bk])
                    if nkc == 1024:
                        T.dma("sp", Vc[kc][:, :, :, :], Vs[sidx, h, k0 // 512:k0 // 512 + 2, :, :, :].rearrange("b p t d -> p b t d"), [b_scr], [b_vc[kc]])
                    else:
                        T.dma("sp", Vc[kc][0:nkc, 0, 0, :], Vs[sidx, h, k0 // 512, 0:nkc, 0, :], [b_scr], [b_vc[kc]])
                return kc

            chunk_slot = {}

            def attention(h, extra):
                Qn_, bQn_, Qr_, bQr_, Wg, bWg = qbuf[h % 2]
                tiles = []
                for (q0, nq, S, sidx) in seqs:
                    ntile = (S + 127) // 128
                    for t in range(ntile):
                        tiles.append((q0, nq, S, sidx, ntile, t))
                def ensure_chunk(q0, nq, S, sidx, ntile, ci, hh=None):
                    hh = h if hh is None else hh
                    key = (hh, sidx, ci)
                    if key not in chunk_slot and ci * 1024 < S and hh < NH:
                        chunk_slot[key] = load_chunk(hh, sidx, S, ci, ntile)

                groups = []
                for tl_ in tiles:
                    q0, nq, S, sidx, ntile, t = tl_
                    gsz = 512 // nq
                    full = min(128, S - t * 128) == 128
                    if groups and full and len(groups[-1]) < gsz and groups[-1][0][3] == sidx and groups[-1][0][5] // 8 == t // 8 \
                            and min(128, groups[-1][0][2] - groups[-1][0][5] * 128) == 128:
                        groups[-1].append(tl_)
                    else:
                        groups.append([tl_])
                info = {}

                def S_stage(j):
                    si = st["pS"] = st["pS"] ^ 1
                    lst = []
                    for gi, (q0, nq, S, sidx, ntile, t) in enumerate(groups[j]):
                        ci = t // 8
                        ensure_chunk(q0, nq, S, sidx, ntile, ci)
                        if len(groups[j]) == 1 and t % 8 == 1:
                            if (ci + 1) * 1024 < S:
                                ensure_chunk(q0, nq, S, sidx, ntile, ci + 1)
                            elif len(seqs) == 1:
                                ensure_chunk(q0, nq, S, sidx, ntile, 0, hh=h + 1)
                        kc = chunk_slot[(h, sidx, ci)]
                        bk = b_kvc[kc]
                        nk = min(128, S - t * 128)
                        tl = t % 8
                        last = gi == len(groups[j]) - 1
                        mm(pS[si][0:nk, gi * nq:(gi + 1) * nq], KTc[kc][:, tl * 128:tl * 128 + nk], Qn_[:, q0:q0 + nq], True, False, [bk, bQn_], [b_pS[si]], False)
                        mm(pS[si][0:nk, gi * nq:(gi + 1) * nq], krc[kc][:, tl * 128:tl * 128 + nk], Qr_[:, q0:q0 + nq], False, True, [bk, bQr_], [b_pS[si]], last)
                        lst.append((kc, nk, tl))
                    info[j] = (si, lst)

                def E_stage(j):
                    si, lst = info[j]
                    nq = groups[j][0][1]
                    nk = lst[0][1]
                    w = len(lst) * nq
                    bi = (4, 5, 0, 1)[j % 4]
                    act(Bt[bi][0:nk, 0:w], pS[si][0:nk, 0:w], AF.Exp, [b_pS[si]], [b_B[bi]], scale=SCALE)

                def PV_stage(j):
                    si, lst = info[j]
                    bi = (4, 5, 0, 1)[j % 4]
                    for gi, (q0, nq, S, sidx, ntile, t) in enumerate(groups[j]):
                        kc, nk, tl = lst[gi]
                        last = gi == len(groups[j]) - 1
                        if dve_sum:
                            mm(pO[:, q0:q0 + nq], Vc[kc][0:nk, tl // 4, tl % 4, :], Bt[bi][0:nk, gi * nq:(gi + 1) * nq], t == 0, t == ntile - 1, [b_vc[kc], b_B[bi]], [b_pO], True)
                            if t == 0:
                                T.op("dve", lambda e, bi=bi: e.tensor_copy(out=acc, in_=Bt[bi][:, 0:512]), [b_B[bi]], [b_U])
                            else:
                                tt(acc, acc, Bt[bi][:, 0:512], ALU.add, [b_B[bi]], [b_U])
                        else:
                            mm(pO[:, q0:q0 + nq], Vc[kc][0:nk, tl // 4, tl % 4, :], Bt[bi][0:nk, gi * nq:(gi + 1) * nq], t == 0, t == ntile - 1, [b_vc[kc], b_B[bi]], [b_pO], False)
                            mm(pR[:, q0:q0 + nq], ones_b[0:nk, :], Bt[bi][0:nk, gi * nq:(gi + 1) * nq], t == 0, t == ntile - 1, [b_ones, b_B[bi]], [b_pR], last)

                ng = len(groups)
                if kind == "p":
                    for f_ in extra:
                        f_()
                    when = {}
                else:
                    when = {1 + k: f_ for k, f_ in enumerate(extra)} if ng > len(extra) else {}
                dve_sum = False
                acc = U[:, 0, 0:512]
                S_stage(0)
                for j in range(ng):
                    if j + 1 < ng:
                        S_stage(j + 1)
                    E_stage(j)
                    PV_stage(j)
                    if j in when:
                        when.pop(j)()
                    if len(seqs) > 1:
                        jj = j + 2
                        hh_ = h
                        if jj >= len(groups):
                            jj -= len(groups)
                            hh_ = h + 1
                        q0_, nq_, S_, sidx_, ntile_, t_ = groups[jj][0]
                        ensure_chunk(q0_, nq_, S_, sidx_, ntile_, t_ // 8, hh=hh_)
                if dve_sum:
                    T.op("dve", lambda e: e.tensor_copy(out=KTt[0][:, 0:512], in_=acc), [b_U], [b_KTt[0]])
                    tt(KTt[1][:, 0:512], acc, KTt[0][:, 0:512], ALU.subtract, [b_U, b_KTt[0]], [b_KTt[1]])
                    mm(pR[:, 0:512], ones_b[:, :], KTt[0][:, 0:512], True, False, [b_ones, b_KTt[0]], [b_pR], False)
                    mm(pR[:, 0:512], ones_b[:, :], KTt[1][:, 0:512], False, True, [b_ones, b_KTt[1]], [b_pR], True)
                cp(PT[0][:, 0:n], pO[:, 0:n], [b_pO], [b_PT[0]], eng="act")
                cp(W[3][:, 0:n], pR[:, 0:n], [b_pR], [b_W[3]], eng="dve")
                T.op("dve", lambda e: e.reciprocal(out=W[3][:, 0:n], in_=W[3][:, 0:n]), [], [b_W[3]])
                stt(W[3][:, 0:n], W[3][:, 0:n], 1.0, Wg[:, 0:n], ALU.mult, ALU.mult, [bWg], [b_W[3]])
                tt(mixT[:, h, 0:n], PT[0][:, 0:n], W[3][:, 0:n], ALU.mult, [b_PT[0], b_W[3]], [b_RM])
                for k_ in sorted(when):
                    when[k_]()

            prep(0)
            for h in range(NH):
                attention(h, prep_stages(h + 1) if h + 1 < NH else [])

            MARKS["B4_%s%d" % (kind, i)] = T.n_emit
            ydst = (lambda t0, ntok, c0: yown[i, t0:t0 + ntok, c0:c0 + 512]) if kind == "p" else (lambda t0, ntok, c0: ysam[t0:t0 + ntok, c0:c0 + 512])
            xsrc = (lambda t0, ntok, c0: xown[i, 16 + t0:16 + t0 + ntok, c0:c0 + 512]) if kind == "p" else (lambda t0, ntok, c0: xsam[t0:t0 + ntok, c0:c0 + 512])
            X2s = [(W[4], b_W[4]), (W[5], b_W[5]), (W[6], b_W[6]), (W[7], b_W[7])]
            for dg in range(8):
                for t0 in range(0, n, 128):
                    X2, bX2 = X2s[(t0 // 128) % 4]
                    T.dma("sp", X2[:, :], xsrc(t0, 128, dg * 512), (), [bX2])
                for q in range(4):
                    dc = dg * 4 + q
                    v, b = stream_chunk(WOUT[dc], 32, 128)
                    P, bP = next_acc()
                    proj(v, b, 0, 128, lambda kk: mixT[:, kk, 0:n], 32, n, P, bP, [b_RM])
                    cp(W[q][:, 0:n], P[:, 0:n], [bP], [b_W[q]])
                for t0 in range(0, n, 128):
                    X2, bX2 = X2s[(t0 // 128) % 4]
                    P, bP = next_pT()
                    for q in range(4):
                        tr(P[:, q * 128:(q + 1) * 128], W[q][:, t0:t0 + 128], ident_f[:, :], [b_W[q], b_ident], [bP], q == 3)
                    tt(X2[:, :], X2[:, :], P[:, :], ALU.add, [bP], [bX2])
                    T.dma("pool", ydst(t0, 128, dg * 512), X2[:, :], [bX2], ())

        own_block("s", 0)
        MARKS["B5_s0"] = T.n_emit
        for i in range(4):
            own_block("p", i)
            MARKS["B5_p%d" % i] = T.n_emit

        T.finish()

        with nc.Block() as block:
            @block.tensor
            def _(e):
                T.replay("pe", e)

            @block.scalar
            def _(e):
                T.replay("act", e)

            @block.vector
            def _(e):
                T.replay("dve", e)

            @block.gpsimd
            def _(e):
                T.replay("pool", e)

            @block.sync
            def _(e):
                T.replay("sp", e)
    return nc


_NC = None


def _rope_tables(pos):
    half = DR // 2
    freqs = (np.float32(10000.0) ** (-np.arange(half, dtype=np.float32) / np.float32(half))).astype(np.float32)
    ang = pos.astype(np.float32)[:, None] * freqs[None, :]
    cos = np.cos(ang).astype(np.float32).T
    sin = np.sin(ang).astype(np.float32).T
    C = np.concatenate([cos, cos], 0)
    S = np.concatenate([-sin, sin], 0)
    return np.ascontiguousarray(np.stack([C, S], 0))


def kernel(x_prompt, x_sample, cache_ckv, cache_krope, state_pool, g_norm, w_in, g_q_lat, w_uq, g_qn, g_qr,
           g_kv_lat, g_kr, w_ukv, g_kn, w_pool, pool_scale, w_out):
    global _NC
    if _NC is None:
        _NC = build_program()
    nc = _NC
    in_maps = prepare(x_prompt, x_sample, cache_ckv, cache_krope, state_pool, g_norm, w_in, g_q_lat, w_uq, g_qn, g_qr,
                      g_kv_lat, g_kr, w_ukv, g_kn, w_pool, pool_scale, w_out)
    res = run_bass_kernel_spmd(nc, in_maps, core_ids=list(range(8)))
    return assemble(res.results)


def prepare(x_prompt, x_sample, cache_ckv, cache_krope, state_pool, g_norm, w_in, g_q_lat, w_uq, g_qn, g_qr,
            g_kv_lat, g_kr, w_ukv, g_kn, w_pool, pool_scale, w_out, cores=range(8)):
    f = lambda a: np.ascontiguousarray(np.asarray(a, dtype=np.float32))
    x_prompt = f(x_prompt); x_sample = f(x_sample); cache_ckv = f(cache_ckv); cache_krope = f(cache_krope)
    state_pool = f(state_pool)
    w_in0 = f(w_in)[0]; w_uq0 = f(w_uq)[0]; w_ukv0 = f(w_ukv)[0]; w_pool0 = f(w_pool)[0]; w_out0 = f(w_out)[0]
    gcols = np.zeros((128, 72), np.float32)
    gcols[:, 0:32] = f(g_norm)[0].reshape(32, 128).T
    gcols[:, 32:40] = f(g_q_lat)[0].reshape(8, 128).T
    gcols[:, 40:44] = f(g_kv_lat)[0].reshape(4, 128).T
    gcols[:, 44] = f(g_qn)[0]
    gcols[:, 45] = f(g_kn)[0]
    gqr = f(g_qr)[0]; gkr = f(g_kr)[0]
    gcols[0:64, 46] = gqr
    gcols[0:64, 47] = np.concatenate([gqr[32:], gqr[:32]])
    gcols[0:64, 48] = gkr
    gcols[0:64, 49] = np.concatenate([gkr[32:], gkr[:32]])
    gcols[:, 50:66] = f(pool_scale)[0].reshape(16, 128).T

    ident = np.eye(128, dtype=np.float32)
    ropeA = _rope_tables(np.arange(SEQ))
    rs = _rope_tables(PAST + np.arange(64))
    ropeS = np.ascontiguousarray(np.concatenate([rs, rs], 2))
    EK = np.zeros((32, 3072), np.float32)
    EK[np.arange(2048) // 64, 1024 + np.arange(2048)] = 1.0
    windows = (2, 4, 8, 16)
    invs = np.zeros((128, 4, 128), np.float32)
    for g, w in enumerate(windows):
        invs[:, g, :] = 1.0 / w

    in_maps = []
    for c in cores:
        s, j = c // 4, c % 4
        xown = np.zeros((4, 528, D), np.float32)
        ropeO = np.zeros((4, 2, 64, 512), np.float32)
        invc = np.zeros((4, 128, 4, 512), np.float32)
        for i in range(4):
            b = 4 * i + j
            lo = b * 512 - 16
            if lo >= 0:
                xown[i] = x_prompt[s, lo:lo + 528]
            else:
                xown[i, 16:] = x_prompt[s, 0:512]
            ropeO[i] = ropeA[:, :, b * 512:(b + 1) * 512]
            pos = b * 512 + np.arange(512)
            for g, w in enumerate(windows):
                invc[i, :, g, :] = (1.0 / np.minimum(pos + 1, w).astype(np.float32))[None, :]
        LQ = np.zeros((32, 512), np.float32)
        qchunk = 8 * j + np.arange(512) // 64
        LQ[np.arange(32)[:, None] > qchunk[None, :]] = NEG
        in_maps.append({
            "xseq": x_prompt[s], "xown": xown, "xsam": np.ascontiguousarray(x_sample[2 * c:2 * c + 2].reshape(128, D)),
            "cckv": np.ascontiguousarray(cache_ckv[0, 2 * c:2 * c + 2]), "ckr": np.ascontiguousarray(cache_krope[0, 2 * c:2 * c + 2]),
            "spool": np.ascontiguousarray(state_pool[0, 2 * c:2 * c + 2]),
            "w_in": w_in0, "w_uq": w_uq0, "w_ukv": w_ukv0, "w_pool": w_pool0, "w_out": w_out0,
            "ident": ident, "gcols": gcols, "ropeA": ropeA, "ropeO": ropeO, "ropeS": ropeS, "EK": EK, "LQ": LQ,
            "invc": invc, "invs": invs,
        })
    return in_maps


def assemble(R):
    y_p = np.zeros((2, SEQ, D), np.float32)
    ckv_p = np.zeros((1, 2, SEQ, KVL), np.float32)
    kr_p = np.zeros((1, 2, SEQ, DR), np.float32)
    pool_p = np.zeros((1, 2, 15, 2048), np.float32)
    y_s = np.zeros((16, 64, D), np.float32)
    ckv_s = np.zeros((1, 16, 64, KVL), np.float32)
    kr_s = np.zeros((1, 16, 64, DR), np.float32)
    pool_s = np.zeros((1, 16, 15, 2048), np.float32)
    for c in range(8):
        s, j = c // 4, c % 4
        r = R[c]
        for i in range(4):
            b = 4 * i + j
            y_p[s, b * 512:(b + 1) * 512] = r["yown"][i]
            ckv_p[0, s, b * 512:(b + 1) * 512] = r["ockv"][i]
            kr_p[0, s, b * 512:(b + 1) * 512] = r["okr"][i]
            if b == 15:
                pool_p[0, s] = r["opool"][i]
        y_s[2 * c:2 * c + 2] = r["ysam"].reshape(2, 64, D)
        ckv_s[0, 2 * c:2 * c + 2] = r["sckv"].reshape(2, 64, KVL)
        kr_s[0, 2 * c:2 * c + 2] = r["skr"].reshape(2, 64, DR)
        pool_s[0, 2 * c:2 * c + 2] = r["spo"]
    return (y_p, y_s, ckv_p, kr_p, pool_p, ckv_s, kr_s, pool_s)
```

```python
import contextlib
import numpy as np
import concourse.bass as bass
import concourse.mybir as mybir
from concourse.bass_utils import run_bass_kernel_spmd

F32 = mybir.dt.float32
BF16 = mybir.dt.bfloat16
AF = mybir.ActivationFunctionType
ALU = mybir.AluOpType

D = 4096
SEQ = 8192
NH = 16
QL = 1024
KVL = 512
DR = 64
DIN = 7744
OFF_Q, OFF_KV, OFF_KR, OFF_ZA, OFF_U, OFF_ZP = 0, 1024, 1536, 1600, 3648, 5696
PAST = 4096
EPS = 1e-6
SCALE = 192.0 ** -0.5
NEG = -30000.0
NDSEM = 20
LIMIT = None
MARKS = {}
PE_AT = {}


class Buf:
    __slots__ = ("w", "r", "excl")

    def __init__(self, excl=False):
        self.w = {}
        self.r = {}
        self.excl = excl


class Tracker:
    def __init__(self, nc, es):
        self.nc = nc
        self.streams = {k: [] for k in ("pe", "act", "dve", "pool", "sp")}
        self.sem = {}
        self.cnt = {}
        self.waited = {k: {} for k in self.streams}
        for k in ("pe", "act", "dve"):
            self.sem[k] = es.enter_context(nc.semaphore("s_" + k))
            self.cnt[k] = 0
        self.n_emit = 0
        self.n_pe = 0
        self.limit = LIMIT
        self.dma_i = {"sp": 0, "pool": 0}
        self.dma_last = {"sp": [None] * NDSEM, "pool": [None] * NDSEM}
        for q in ("sp", "pool"):
            for k in range(NDSEM):
                nm = "%sd%d" % (q, k)
                self.sem[nm] = es.enter_context(nc.semaphore("s_" + nm))
                self.cnt[nm] = 0

    def _wait(self, eng, tok):
        if tok is None:
            return
        s, v = tok
        if s == eng and eng == "pe":
            return
        if self.waited[eng].get(s, 0) >= v:
            return
        self.waited[eng][s] = v
        self.streams[eng].append(("wait", s, v))

    def _deps(self, eng, reads, writes):
        for b in reads:
            for s, v in list(b.w.items()):
                self._wait(eng, (s, v))
            if b.excl:
                for s, v in list(b.r.items()):
                    if s != eng:
                        self._wait(eng, (s, v))
        for b in writes:
            for s, v in list(b.w.items()):
                if s.startswith(eng + "d") or s == eng:
                    continue
                self._wait(eng, (s, v))
            for s, v in list(b.r.items()):
                if s == eng:
                    continue
                self._wait(eng, (s, v))

    def _mark(self, tok, reads, writes):
        s, v = tok
        for b in reads:
            if b.r.get(s, 0) < v:
                b.r[s] = v
        for b in writes:
            if s[-1].isdigit() and not b.r and b.w and all(k[-1].isdigit() for k in b.w):
                b.w[s] = v
            else:
                b.w = {s: v}
            b.r = {}

    def op(self, eng, fn, reads=(), writes=(), signal=True):
        self.n_emit += 1
        if self.limit is not None and self.n_emit > self.limit:
            return None
        self._deps(eng, reads, writes)
        if eng == "pe":
            self.n_pe += 1
            PE_AT[self.n_emit] = self.n_pe
        if signal:
            self.cnt[eng] += 1
            tok = (eng, self.cnt[eng])
        else:
            tok = (eng, self.cnt[eng] + 1)
        self.streams[eng].append(("op", fn, signal))
        self._mark(tok, reads, writes)
        return tok

    def dma(self, q, out, in_, reads=(), writes=()):
        self.n_emit += 1
        if self.limit is not None and self.n_emit > self.limit:
            return None
        self._deps(q, reads, writes)
        k = self.dma_i[q] % NDSEM
        self.dma_i[q] += 1
        self._wait(q, self.dma_last[q][k])
        nm = "%sd%d" % (q, k)
        self.cnt[nm] += 16
        tok = (nm, self.cnt[nm])
        self.dma_last[q][k] = tok
        self.streams[q].append(("dma", out, in_, nm))
        self._mark(tok, reads, writes)
        return tok

    def finish(self):
        for q in ("sp", "pool"):
            for k in range(NDSEM):
                self._wait(q, self.dma_last[q][k])

    def replay(self, eng, e):
        for it in self.streams[eng]:
            if it[0] == "wait":
                e.wait_ge(self.sem[it[1]], it[2])
            elif it[0] == "op":
                ins = it[1](e)
                if it[2]:
                    ins.then_inc(self.sem[eng], 1)
            else:
                e.dma_start(out=it[1], in_=it[2]).then_inc(self.sem[it[3]], 16)


def build_program():
    nc = bass.Bass("TRN2", target_bir_lowering=False)

    def din(name, shape, dt=F32):
        return nc.dram_tensor(name, list(shape), dt, kind="ExternalInput").ap()

    def dout(name, shape):
        return nc.dram_tensor(name, list(shape), F32, kind="ExternalOutput").ap()

    def dscr(name, shape):
        return nc.dram_tensor(name, list(shape), BF16, kind="Internal").ap()

    xseq = din("xseq", [SEQ, D])
    xown = din("xown", [4, 528, D])
    xsam = din("xsam", [128, D])
    cckv = din("cckv", [2, PAST, KVL])
    ckr = din("ckr", [2, PAST, DR])
    spool = din("spool", [2, 15, 2048])
    w_in = din("w_in", [D, DIN])
    w_uq = din("w_uq", [QL, NH, 192])
    w_ukv = din("w_ukv", [KVL, NH, 256])
    w_pool = din("w_pool", [4, 512, 512])
    w_out = din("w_out", [D, D])
    ident = din("ident", [128, 128])
    gcols_d = din("gcols", [128, 72])
    ropeA = din("ropeA", [2, 64, SEQ])
    ropeO = din("ropeO", [4, 2, 64, 512])
    ropeS = din("ropeS", [2, 64, 128])
    EK_d = din("EK", [32, 3072])
    LQ_d = din("LQ", [32, 512])
    invc_d = din("invc", [4, 128, 4, 512])
    invs_d = din("invs", [128, 4, 128])

    yown = dout("yown", [4, 512, D])
    ysam = dout("ysam", [128, D])
    ockv = dout("ockv", [4, 512, KVL])
    okr = dout("okr", [4, 512, DR])
    opool = dout("opool", [4, 15, 2048])
    sckv = dout("sckv", [128, KVL])
    skr = dout("skr", [128, DR])
    spo = dout("spo", [2, 15, 2048])

    KTp = dscr("KTp", [NH, 128, SEQ])
    Vp = dscr("Vp", [NH, 16, 128, 4, 128])
    krp = dscr("krp", [64, SEQ])
    KTs = dscr("KTs", [2, NH, 128, 4608])
    Vs = dscr("Vs", [2, NH, 9, 128, 4, 128])
    krs = dscr("krs", [2, 64, 4608])
    WIN = dscr("WIN", [61, 128, 32, 128])
    WOUT = dscr("WOUT", [32, 128, 32, 128])
    WQ = dscr("WQ", [16, 128, 8, 256])
    WP = dscr("WP", [4, 128, 4, 512])

    es = contextlib.ExitStack()
    with es:
        def sb(name, shape, dt=F32):
            return es.enter_context(nc.sbuf_tensor("sb_" + name, list(shape), dt))

        def ps(name, shape, dt=F32):
            return es.enter_context(nc.psum_tensor("ps_" + name, list(shape), dt))

        T = Tracker(nc, es)

        ident_f = sb("ident_f", [128, 128]); b_ident = Buf()
        ones_b = sb("ones_b", [128, 128], BF16); b_ones = Buf()
        eps_t = sb("eps_t", [128, 1]); b_eps = Buf()
        gcols = sb("gcols", [128, 72]); b_g = Buf()
        EKb = sb("EKb", [32, 3072], BF16); b_ek = Buf()
        xs = [sb("xs%d" % k, [128, D]) for k in range(2)]; b_xs = [Buf(), Buf()]
        ssx = [sb("ssx%d" % k, [128, 8]) for k in range(2)]; b_ssx = [Buf(), Buf()]
        hT = sb("hT", [128, 32, 512], BF16); b_hT = Buf()
        hTh = sb("hTh", [128, 32, 16], BF16); b_hTh = Buf()
        RM = sb("RM", [128, 16384], BF16); b_RM = Buf()
        cqnT = sb("cqnT", [128, 8, 512], BF16); b_cq = Buf()
        RW = sb("RW", [128, 16384], BF16)
        b_slot = [Buf() for _ in range(4)]
        KVR = sb("KVR", [128, 6144], BF16)
        KTc = [KVR[:, k * 1024:(k + 1) * 1024] for k in range(2)]
        Vc = [KVR[:, 2048 + k * 1024:2048 + (k + 1) * 1024].rearrange("p (b t d) -> p b t d", t=4, d=128) for k in range(2)]
        krc = [KVR[:, 4096 + k * 1024:4096 + (k + 1) * 1024] for k in range(2)]
        b_kvc = [Buf(), Buf()]
        b_vc = [Buf(), Buf()]
        b_scr = Buf(); b_conv = Buf()
        junk = KVR[:, 0:2048]
        NW = 8
        W = [sb("W%d" % k, [128, 512]) for k in range(NW)]; b_W = [Buf() for _ in range(NW)]
        NB = 6
        Bt = [sb("B%d" % k, [128, 512], BF16) for k in range(NB)]; b_B = [Buf() for _ in range(NB)]
        U = sb("U", [128, 4, 528]); b_U = Buf()
        PT = [sb("PT%d" % k, [128, 528]) for k in range(2)]; b_PT = [Buf(), Buf()]
        UH = sb("UH", [128, 16, 2, 16]); b_UH = Buf()
        ckvT0 = sb("ckvT0", [128, 4, 512], BF16); ckvT = [ckvT0, ckvT0]; b_ck0 = Buf(); b_ckvT = [b_ck0, b_ck0]
        krT = sb("krT", [64, 512], BF16); b_krT = Buf()
        Vts = [(KVR[:, 0:2048], [b_kvc[0], b_kvc[1]]), (KVR[:, 2048:4096], [b_vc[0], b_vc[1]])]
        KTt = [sb("KTt%d" % k, [128, 512], BF16) for k in range(2)]; b_KTt = [Buf(), Buf()]
        wpool_sb = sb("wpool_sb", [128, 4, 512], BF16); b_wpool = Buf()
        QnT = sb("QnT", [128, 512], BF16); b_Qn = Buf()
        QrT = sb("QrT", [128, 512], BF16); b_Qr = Buf()
        QnT2 = sb("QnT2", [128, 512], BF16); b_Qn2 = Buf()
        QrT2 = sb("QrT2", [128, 512], BF16); b_Qr2 = Buf()

        pA = ps("pA", [128, 512]); pB = ps("pB", [128, 512])
        pS = [ps("pS0", [128, 512]), ps("pS1", [128, 512])]
        pO = ps("pO", [128, 512]); pR = ps("pR", [128, 512])
        pT = [ps("pT0", [128, 512]), ps("pT1", [128, 512])]
        b_pA = Buf(True); b_pB = Buf(True); b_pS = [Buf(True), Buf(True)]; b_pO = Buf(True); b_pR = Buf(True); b_pT = [Buf(True), Buf(True)]
        acc = [(pA, b_pA), (pB, b_pB)]
        st = {"acc": 0, "pT": 0, "xs": 0, "slot": 0, "ev": 0, "kvc": 0, "pS": 0, "B": 0, "ktt": 0, "ckv": 0}

        wkv_v = RM[:, 0:32 * 512].rearrange("p (c e) -> p c e", e=512)
        wkr_v = cqnT[:, :, :].rearrange("p a b -> p (a b)").rearrange("p (c e) -> p c e", e=128)
        mixT = RM[:, 0:32 * 512].rearrange("p (c e) -> p c e", e=512)
        wk_v = RW[:, 0:8192].rearrange("p (c e) -> p c e", e=2048)
        wv_v = RW[:, 8192:16384].rearrange("p (c e) -> p c e", e=2048)
        slots = [RW[:, k * 4096:(k + 1) * 4096] for k in range(4)]

        def mm(out, lhsT, rhs, start, stop, reads, writes, signal=None):
            if signal is None:
                signal = stop
            T.op("pe", lambda e: e.matmul(out, lhsT=lhsT, rhs=rhs, start=start, stop=stop), reads, writes, signal)

        def tr(out, in_, idn, reads, writes, signal):
            T.op("pe", lambda e: e.transpose(out=out, in_=in_, identity=idn), reads, writes, signal)

        def act(out, in_, func, reads, writes, scale=1.0, bias=None, accum=None):
            kw = {}
            if bias is not None:
                kw["bias"] = bias
            if accum is not None:
                kw["accum_out"] = accum
            T.op("act", lambda e: e.activation(out=out, in_=in_, func=func, scale=scale, **kw), reads, writes)

        def stt(out, in0, scalar, in1, op0, op1, reads, writes):
            T.op("dve", lambda e: e.scalar_tensor_tensor(out=out, in0=in0, scalar=scalar, in1=in1, op0=op0, op1=op1), reads, writes)

        def tt(out, in0, in1, op, reads, writes):
            T.op("dve", lambda e: e.tensor_tensor(out=out, in0=in0, in1=in1, op=op), reads, writes)

        def ts(out, in0, s1, s2, op0, op1, reads, writes):
            T.op("dve", lambda e: e.tensor_scalar(out=out, in0=in0, scalar1=s1, scalar2=s2, op0=op0, op1=op1), reads, writes)

        def cp(out, in_, reads, writes, eng=None):
            if eng is None:
                st["ev"] ^= 1
                eng = "act" if st["ev"] else "dve"
            if eng == "act":
                act(out, in_, AF.Copy, reads, writes)
            else:
                T.op("dve", lambda e: e.tensor_copy(out=out, in_=in_), reads, writes)

        def memset(ap, val, writes):
            T.op("dve", lambda e: e.memset(ap, val), (), writes)

        def next_acc():
            st["acc"] ^= 1
            return acc[st["acc"]]

        def next_pT():
            st["pT"] ^= 1
            return pT[st["pT"]], b_pT[st["pT"]]

        def rstd_from(psum_ap, b_ps, n_feat, wl, b_wl, wr, b_wr, npart):
            act(wl, psum_ap, AF.Ln, [b_ps, b_eps], [b_wl], scale=1.0 / n_feat, bias=eps_t[0:npart, 0:1])
            act(wr, wl, AF.Exp, [b_wl], [b_wr], scale=-0.5)

        T.dma("sp", ident_f[:, :], ident[:, :], (), [b_ident])
        T.dma("sp", gcols[:, :], gcols_d[:, :], (), [b_g])
        T.dma("pool", EKb[:, :], EK_d[:, :], (), [b_ek])
        memset(QrT[:, :], 0.0, [b_Qr])
        T.dma("pool", QrT[64:96, :], LQ_d[:, :], (), [b_Qr])
        memset(QrT2[:, :], 0.0, [b_Qr2])
        T.dma("pool", QrT2[64:96, :], LQ_d[:, :], (), [b_Qr2])
        memset(KVR[64:128, 4096:6144], 0.0, [b_kvc[0], b_kvc[1]])
        memset(ones_b[:, :], 1.0, [b_ones])
        memset(eps_t[:, :], EPS, [b_eps])
        MARKS["const"] = T.n_emit

        def x_front(x_rows, ntok):
            k = st["xs"] = st["xs"] ^ 1
            X = xs[k]; bX = b_xs[k]; S = ssx[k]; bS = b_ssx[k]
            T.dma("sp", X[0:ntok, :], x_rows, (), [bX])
            memset(S[:, :], 0.0, [bS])
            for hh in range(2):
                act(junk[0:ntok, :], X[0:ntok, hh * 2048:(hh + 1) * 2048], AF.Square, [bX], [b_kvc[0], b_kvc[1], bS],
                    accum=S[0:ntok, hh:hh + 1])
            act(S[0:ntok, 2:3], S[0:ntok, 0:1], AF.Identity, [bS], [bS], bias=S[0:ntok, 1:2])
            act(S[0:ntok, 3:4], S[0:ntok, 2:3], AF.Ln, [bS, b_eps], [bS], scale=1.0 / D, bias=eps_t[0:ntok, 0:1])
            act(S[0:ntok, 4:5], S[0:ntok, 3:4], AF.Exp, [bS], [bS], scale=-0.5)
            for hh in range(2):
                act(X[0:ntok, hh * 2048:(hh + 1) * 2048], X[0:ntok, hh * 2048:(hh + 1) * 2048], AF.Identity, [bS], [bX],
                    scale=S[0:ntok, 4:5])
            return k

        def x_back(k, ntok, dst, b_dst):
            X = xs[k]; bX = b_xs[k]
            for c4 in range(8):
                P, bP = next_pT()
                for q in range(4):
                    c = c4 * 4 + q
                    tr(P[:, q * 128:q * 128 + ntok], X[0:ntok, c * 128:(c + 1) * 128], ident_f[0:ntok, 0:ntok],
                       [bX, b_ident], [bP], q == 3)
                for q in range(4):
                    c = c4 * 4 + q
                    ts(dst[:, c, :], P[:, q * 128:q * 128 + ntok], gcols[:, c:c + 1], 1.0, ALU.mult, ALU.mult,
                       [bP, b_g], [b_dst])

        def x_to_hT(x_rows, ntok, dst, b_dst):
            x_back(x_front(x_rows, ntok), ntok, dst, b_dst)

        def stream_chunk(src_tile, kc, width):
            k = st["slot"] = (st["slot"] + 1) % 4
            v = slots[k][:, 0:kc * width].rearrange("p (c e) -> p c e", e=width)
            T.dma("sp", v, src_tile, [b_conv], [b_slot[k]])
            return v, b_slot[k]

        def proj(wv, bw, col0, M, rhs_fn, kc, n, out_ps, b_out, rhs_bufs):
            for k in range(kc):
                mm(out_ps[0:M, 0:n], wv[:, k, col0:col0 + M], rhs_fn(k), k == 0, k == kc - 1, [bw] + rhs_bufs, [b_out])

        def out_tokmajor(srcs, bsrcs, npart, n, dst_rows_fn):
            for t0 in range(0, n, 128):
                ntok = min(128, n - t0)
                P, bP = next_pT()
                for q, s in enumerate(srcs):
                    tr(P[0:ntok, q * npart:(q + 1) * npart], s[0:npart, t0:t0 + ntok], ident_f[0:npart, 0:npart],
                       bsrcs + [b_ident], [bP], q == len(srcs) - 1)
                wi = st["B"] = (st["B"] + 1) % 2
                Wt, bWt = (W[7], b_W[7]) if wi == 0 else (PT[0], b_PT[0])
                cp(Wt[0:ntok, 0:len(srcs) * npart], P[0:ntok, 0:len(srcs) * npart], [bP], [bWt])
                T.dma("pool", dst_rows_fn(t0, ntok), Wt[0:ntok, 0:len(srcs) * npart], [bWt], ())

        def latents(get_w, rhs_fn, rhs_bufs, n, ropeC_ap, ropeS_ap, ck, b_ck, out_ckv_fn=None, out_kr_fn=None):
            RC, bRC = W[5], b_W[5]
            RS, bRS = W[6], b_W[6]
            T.dma("sp", RC[0:64, 0:n], ropeC_ap, (), [bRC])
            T.dma("sp", RS[0:64, 0:n], ropeS_ap, (), [bRS])
            for c4 in range(4):
                wv, bw, col0 = get_w(c4)
                P, bP = next_acc()
                proj(wv, bw, col0, 128, rhs_fn, 32, n, P, bP, rhs_bufs)
                cp(W[c4][:, 0:n], P[:, 0:n], [bP], [b_W[c4]], eng="dve")
                act(Bt[4 + c4 % 2][:, 0:n], P[:, 0:n], AF.Square, [bP], [b_B[4 + c4 % 2]])
                if c4 > 0:
                    mm(pR[:, 0:n], ones_b[:, :], Bt[4 + (c4 - 1) % 2][:, 0:n], c4 == 1, False, [b_ones, b_B[4 + (c4 - 1) % 2]], [b_pR], False)
            mm(pR[:, 0:n], ones_b[:, :], Bt[5][:, 0:n], False, True, [b_ones, b_B[5]], [b_pR], True)
            rstd_from(pR[:, 0:n], b_pR, KVL, W[4][:, 0:n], b_W[4], W[4][:, 0:n], b_W[4], 128)
            for c4 in range(4):
                stt(ck[:, c4, 0:n], W[c4][:, 0:n], gcols[:, 40 + c4:41 + c4], W[4][:, 0:n], ALU.mult, ALU.mult,
                    [b_W[c4], b_W[4], b_g], [b_ck])
                if out_ckv_fn is not None:
                    stt(W[c4][:, 0:n], W[c4][:, 0:n], gcols[:, 40 + c4:41 + c4], W[4][:, 0:n], ALU.mult, ALU.mult,
                        [b_W[4], b_g], [b_W[c4]])
            if out_ckv_fn is not None:
                out_tokmajor([W[c][:, :] for c in range(4)], [b_W[c] for c in range(4)], 128, n, out_ckv_fn)
            wv, bw, col0 = get_w(4)
            proj(wv, bw, col0, 64, rhs_fn, 32, n, pA, b_pA, rhs_bufs)
            wv, bw, col0 = get_w(5)
            proj(wv, bw, col0, 64, rhs_fn, 32, n, pB, b_pB, rhs_bufs)
            act(Bt[4][0:64, 0:n], pA[0:64, 0:n], AF.Square, [b_pA], [b_B[4]])
            mm(pR[0:64, 0:n], ones_b[0:64, 0:64], Bt[4][0:64, 0:n], True, True, [b_ones, b_B[4]], [b_pR])
            rstd_from(pR[0:64, 0:n], b_pR, DR, W[4][0:64, 0:n], b_W[4], W[4][0:64, 0:n], b_W[4], 64)
            stt(W[0][0:64, 0:n], pA[0:64, 0:n], gcols[0:64, 48:49], RC[0:64, 0:n], ALU.mult, ALU.mult, [b_pA, bRC, b_g], [b_W[0]])
            stt(W[1][0:64, 0:n], pB[0:64, 0:n], gcols[0:64, 49:50], RS[0:64, 0:n], ALU.mult, ALU.mult, [b_pB, bRS, b_g], [b_W[1]])
            tt(W[0][0:64, 0:n], W[0][0:64, 0:n], W[1][0:64, 0:n], ALU.add, [b_W[1]], [b_W[0]])
            tt(krT[0:64, 0:n], W[0][0:64, 0:n], W[4][0:64, 0:n], ALU.mult, [b_W[0], b_W[4]], [b_krT])
            if out_kr_fn is not None:
                tt(W[1][0:64, 0:n], W[0][0:64, 0:n], W[4][0:64, 0:n], ALU.mult, [b_W[0], b_W[4]], [b_W[1]])
                out_tokmajor([W[1][:, :]], [b_W[1]], 64, n, out_kr_fn)

        def kv_block(ck, b_ck, n, kt_dst_fn, v_dst_fn):
            accs3 = [(pA, b_pA), (pB, b_pB), (pO, b_pO)]
            wrs = [(W[2], b_W[2]), (W[3], b_W[3]), (W[7], b_W[7])]
            hold = {}

            def k_stage1(h):
                P, bP = accs3[h % 3]
                for c in range(4):
                    mm(P[:, 0:n], wk_v[:, c, h * 128:(h + 1) * 128], ck[:, c, 0:n], c == 0, c == 3, [b_slot[0], b_slot[1], b_ck], [bP])
                act(Bt[h % 3][:, 0:n], P[:, 0:n], AF.Square, [bP], [b_B[h % 3]])

            def k_stage2(h):
                P, bP = accs3[h % 3]
                X, bX = next_pT()
                mm(X[:, 0:n], ones_b[:, :], Bt[h % 3][:, 0:n], True, True, [b_ones, b_B[h % 3]], [bX])
                Wr, bWr = wrs[h % 3]
                rstd_from(X[:, 0:n], bX, 128, Wr[:, 0:n], bWr, Wr[:, 0:n], bWr, 128)
                ki = h % 2
                stt(KTt[ki][:, 0:n], P[:, 0:n], gcols[:, 45:46], Wr[:, 0:n], ALU.mult, ALU.mult, [bP, bWr, b_g], [b_KTt[ki]])
                T.dma("pool", kt_dst_fn(h), KTt[ki][:, 0:n], [b_KTt[ki]], [b_scr])

            k_stage1(0)
            for h in range(NH):
                if h + 1 < NH:
                    k_stage1(h + 1)
                k_stage2(h)
            for t0 in range(0, n, 128):
                ntok = min(128, n - t0)
                vi = st["vt"] = st.get("vt", 0) ^ 1
                Vt, bVt = Vts[vi]
                for hg in range(4):
                    P, bP = next_acc()
                    for c in range(4):
                        mm(P[0:ntok, :], ck[:, c, t0:t0 + ntok], wv_v[:, c, hg * 512:(hg + 1) * 512], c == 0, c == 3,
                           [b_slot[2], b_slot[3], b_ck], [bP])
                    cp(Vt[0:ntok, hg * 512:(hg + 1) * 512], P[0:ntok, :], [bP], bVt)
                T.dma("pool", v_dst_fn(t0 // 128, ntok), Vt[0:ntok, :].rearrange("p (h d) -> p h d", d=128), bVt, [b_scr])

        T.dma("pool", wkv_v[:, :, 0:512], w_in[:, OFF_KV:OFF_KV + 512].rearrange("(c p) e -> p c e", p=128), (), [b_RM])
        T.dma("pool", wkr_v[:, :, 0:64], w_in[:, OFF_KR:OFF_KR + 64].rearrange("(c p) e -> p c e", p=128), (), [b_cq])
        T.dma("pool", wkr_v[:, :, 64:96], w_in[:, OFF_KR + 32:OFF_KR + 64].rearrange("(c p) e -> p c e", p=128), (), [b_cq])
        T.dma("pool", wkr_v[:, :, 96:128], w_in[:, OFF_KR:OFF_KR + 32].rearrange("(c p) e -> p c e", p=128), (), [b_cq])
        for c in range(4):
            T.dma("pool", wk_v[:, c, :].rearrange("p (h d) -> p h d", d=128), w_ukv[c * 128:(c + 1) * 128, :, 0:128], (), [b_slot[0], b_slot[1]])
            T.dma("pool", wv_v[:, c, :].rearrange("p (h d) -> p h d", d=128), w_ukv[c * 128:(c + 1) * 128, :, 128:256], (), [b_slot[2], b_slot[3]])

        def win_idx(col0):
            return col0 // 128 if col0 < OFF_KR else 13 + (col0 - OFF_ZA) // 128
        rr = lambda ap: ap.rearrange("(c p) e -> p c e", p=128)
        conv = []
        for col0 in list(range(0, OFF_KR, 128)) + list(range(OFF_ZA, DIN, 128)):
            conv.append((WIN[win_idx(col0)], rr(w_in[:, col0:col0 + 128])))
        conv.append((WIN[12, :, :, 0:64], rr(w_in[:, OFF_KR:OFF_KR + 64])))
        conv.append((WIN[12, :, :, 64:96], rr(w_in[:, OFF_KR + 32:OFF_KR + 64])))
        conv.append((WIN[12, :, :, 96:128], rr(w_in[:, OFF_KR:OFF_KR + 32])))
        for h in range(NH):
            conv.append((WQ[h, :, :, 0:192], rr(w_uq[:, h, :])))
            conv.append((WQ[h, :, :, 192:224], rr(w_uq[:, h, 160:192])))
            conv.append((WQ[h, :, :, 224:256], rr(w_uq[:, h, 128:160])))
        for g in range(4):
            conv.append((WP[g], rr(w_pool[g, :, :])))
        for dc in range(32):
            conv.append((WOUT[dc], rr(w_out[:, dc * 128:(dc + 1) * 128])))
        conv_pos = [0]

        def conv_some(k):
            while k > 0 and conv_pos[0] < len(conv):
                d_, s_ = conv[conv_pos[0]]
                T.dma("pool", d_, s_, (), [b_conv])
                conv_pos[0] += 1
                k -= 1

        def get_w_res(j):
            if j < 4:
                return wkv_v, b_RM, j * 128
            return wkr_v, b_cq, (j - 4) * 64

        pend = x_front(xseq[0:128, :], 128)
        for blk in range(16):
            conv_some(6)
            for t4 in range(4):
                r0 = blk * 512 + t4 * 128
                cur = pend
                if r0 + 128 < SEQ:
                    pend = x_front(xseq[r0 + 128:r0 + 256, :], 128)
                x_back(cur, 128, hT[:, :, t4 * 128:(t4 + 1) * 128], b_hT)
            ci = st["ckv"] = st["ckv"] ^ 1
            latents(get_w_res, lambda k: hT[:, k, 0:512], [b_hT], 512,
                    ropeA[0, :, blk * 512:(blk + 1) * 512], ropeA[1, :, blk * 512:(blk + 1) * 512], ckvT[ci], b_ckvT[ci])
            T.dma("pool", krp[:, blk * 512:(blk + 1) * 512], krT[:, :], [b_krT], [b_scr])
            kv_block(ckvT[ci], b_ckvT[ci], 512,
                     lambda h, blk=blk: KTp[h, :, blk * 512:(blk + 1) * 512],
                     lambda t4, ntok, blk=blk: Vp[:, blk, 0:ntok, t4, :].rearrange("h p d -> p h d"))
            MARKS["A%d" % blk] = T.n_emit

        for s in range(2):
            for blk in range(8):
                conv_some(6)
                k = st["xs"] = st["xs"] ^ 1
                X = xs[k]; bX = b_xs[k]
                Xv = X[:, 0:2048].rearrange("p (t f) -> p t f", f=512)
                T.dma("sp", Xv, cckv[s, blk * 512:(blk + 1) * 512, :].rearrange("(t p) f -> p t f", p=128), (), [bX])
                Kv = X[:, 2048:2304].rearrange("p (t f) -> p t f", f=64)
                T.dma("sp", Kv, ckr[s, blk * 512:(blk + 1) * 512, :].rearrange("(t p) f -> p t f", p=128), (), [bX])
                ci = st["ckv"] = st["ckv"] ^ 1
                for t4 in range(4):
                    P, bP = next_pT()
                    for c in range(4):
                        tr(P[:, c * 128:(c + 1) * 128], Xv[:, t4, c * 128:(c + 1) * 128], ident_f[:, :], [bX, b_ident], [bP], c == 3)
                    cp(ckvT[ci][:, :, t4 * 128:(t4 + 1) * 128], P[:, :].rearrange("p (c t) -> p c t", t=128), [bP], [b_ckvT[ci]])
                P, bP = next_pT()
                for t4 in range(4):
                    tr(P[0:64, t4 * 128:(t4 + 1) * 128], Kv[:, t4, :], ident_f[:, :], [bX, b_ident], [bP], t4 == 3)
                cp(krT[:, :], P[0:64, :], [bP], [b_krT])
                T.dma("pool", krs[s, :, blk * 512:(blk + 1) * 512], krT[:, :], [b_krT], [b_scr])
                kv_block(ckvT[ci], b_ckvT[ci], 512,
                         lambda h, blk=blk, s=s: KTs[s, h, :, blk * 512:(blk + 1) * 512],
                         lambda t4, ntok, blk=blk, s=s: Vs[s, :, blk, 0:ntok, t4, :].rearrange("h p d -> p h d"))
                MARKS["SC%d_%d" % (s, blk)] = T.n_emit

        conv_some(10 ** 6)
        x_to_hT(xsam[:, :], 128, hT[:, :, 0:128], b_hT)
        ci = st["ckv"] = st["ckv"] ^ 1
        latents(get_w_res, lambda k: hT[:, k, 0:128], [b_hT], 128, ropeS[0, :, :], ropeS[1, :, :], ckvT[ci], b_ckvT[ci],
                out_ckv_fn=lambda t0, ntok: sckv[t0:t0 + ntok, :], out_kr_fn=lambda t0, ntok: skr[t0:t0 + ntok, :])
        for s in range(2):
            T.dma("pool", krs[s, :, 4096:4160], krT[:, s * 64:(s + 1) * 64], [b_krT], [b_scr])
        for h in range(NH):
            P, bP = next_acc()
            for c in range(4):
                mm(P[:, 0:128], wk_v[:, c, h * 128:(h + 1) * 128], ckvT[ci][:, c, 0:128], c == 0, c == 3, [b_slot[0], b_slot[1], b_ckvT[ci]], [bP])
            act(Bt[0][:, 0:128], P[:, 0:128], AF.Square, [bP], [b_B[0]])
            X, bX = next_pT()
            mm(X[:, 0:128], ones_b[:, :], Bt[0][:, 0:128], True, True, [b_ones, b_B[0]], [bX])
            rstd_from(X[:, 0:128], bX, 128, W[2][:, 0:128], b_W[2], W[2][:, 0:128], b_W[2], 128)
            ki = st["ktt"] = st["ktt"] ^ 1
            stt(KTt[ki][:, 0:128], P[:, 0:128], gcols[:, 45:46], W[2][:, 0:128], ALU.mult, ALU.mult, [bP, b_W[2], b_g], [b_KTt[ki]])
            for s in range(2):
                T.dma("pool", KTs[s, h, :, 4096:4160], KTt[ki][:, s * 64:(s + 1) * 64], [b_KTt[ki]], [b_scr])
        for s in range(2):
            for hg in range(4):
                P, bP = next_acc()
                for c in range(4):
                    mm(P[0:64, :], ckvT[ci][:, c, s * 64:(s + 1) * 64], wv_v[:, c, hg * 512:(hg + 1) * 512], c == 0, c == 3,
                       [b_slot[2], b_slot[3], b_ckvT[ci]], [bP])
                cp(Vts[s][0][0:64, hg * 512:(hg + 1) * 512], P[0:64, :], [bP], Vts[s][1])
            T.dma("pool", Vs[s, :, 8, 0:64, 0, :].rearrange("h p d -> p h d"), Vts[s][0][0:64, :].rearrange("p (h d) -> p h d", d=128), Vts[s][1], [b_scr])

        MARKS["SN"] = T.n_emit
        memset(UH[:, :, :, :], 0.0, [b_UH])
        for s in range(2):
            k = st["xs"] = st["xs"] ^ 1
            X = xs[k]; bX = b_xs[k]
            T.dma("sp", X[0:15, 0:2048], spool[s, :, :], (), [bX])
            for c4 in range(4):
                P, bP = next_pT()
                for q in range(4):
                    c = c4 * 4 + q
                    tr(P[:, q * 16:q * 16 + 15], X[0:15, c * 128:(c + 1) * 128], ident_f[0:15, 0:15], [bX, b_ident], [bP], q == 3)
                cp(UH[:, c4 * 4:(c4 + 1) * 4, s, 1:16], P[:, 0:64].rearrange("p (c t) -> p c t", t=16)[:, :, 0:15], [bP], [b_UH])

        MARKS["UH"] = T.n_emit
        def wslice(col0, width):
            assert width == 128
            return stream_chunk(WIN[win_idx(col0)], 32, 128)

        def own_block(kind, i):
            if kind == "p":
                n, nseq, nt = 512, 1, 512
            else:
                n, nseq, nt = 128, 2, 64
            L = 16 + nt
            rhs = lambda k: hT[:, k, 0:n]
            if kind == "p":
                kh = x_front(xown[i, 0:16, :], 16)
                pend_o = x_front(xown[i, 16:144, :], 128)
                x_back(kh, 16, hTh[:, :, 0:16], b_hTh)
                for t4 in range(4):
                    cur = pend_o
                    if t4 + 1 < 4:
                        pend_o = x_front(xown[i, 16 + (t4 + 1) * 128:16 + (t4 + 2) * 128, :], 128)
                    x_back(cur, 128, hT[:, :, t4 * 128:(t4 + 1) * 128], b_hT)
                kr_w = [None]

                def get_w_str(j):
                    if j < 4:
                        v, b = wslice(OFF_KV + j * 128, 128)
                        return v, b, 0
                    if j == 4:
                        v, b = stream_chunk(WIN[12], 32, 128)
                        kr_w[0] = (v, b)
                        return v, b, 0
                    v, b = kr_w[0]
                    return v, b, 64
                ci = st["ckv"] = st["ckv"] ^ 1
                latents(get_w_str, rhs, [b_hT], n, ropeO[i, 0, :, :], ropeO[i, 1, :, :], ckvT[ci], b_ckvT[ci],
                        out_ckv_fn=lambda t0, ntok: ockv[i, t0:t0 + ntok, :], out_kr_fn=lambda t0, ntok: okr[i, t0:t0 + ntok, :])
            MARKS["B1_%s%d" % (kind, i)] = T.n_emit
            rawt = [W[0], W[1], W[2], W[3], PT[0], PT[1], W[5], W[6]]
            rawb = [b_W[0], b_W[1], b_W[2], b_W[3], b_PT[0], b_PT[1], b_W[5], b_W[6]]
            for c8 in range(8):
                v, b = wslice(OFF_Q + c8 * 128, 128)
                P, bP = next_acc()
                proj(v, b, 0, 128, rhs, 32, n, P, bP, [b_hT])
                cp(rawt[c8][:, 0:n], P[:, 0:n], [bP], [rawb[c8]], eng="dve")
                act(Bt[4 + c8 % 2][:, 0:n], P[:, 0:n], AF.Square, [bP], [b_B[4 + c8 % 2]])
                if c8 > 0:
                    mm(pR[:, 0:n], ones_b[:, :], Bt[4 + (c8 - 1) % 2][:, 0:n], c8 == 1, False, [b_ones, b_B[4 + (c8 - 1) % 2]], [b_pR], False)
            mm(pR[:, 0:n], ones_b[:, :], Bt[5][:, 0:n], False, True, [b_ones, b_B[5]], [b_pR], True)
            rstd_from(pR[:, 0:n], b_pR, QL, W[4][:, 0:n], b_W[4], W[4][:, 0:n], b_W[4], 128)
            for c8 in range(8):
                stt(cqnT[:, c8, 0:n], rawt[c8][:, 0:n], gcols[:, 32 + c8:33 + c8], W[4][:, 0:n], ALU.mult, ALU.mult,
                    [rawb[c8], b_W[4], b_g], [b_cq])

            def silu_from(P, bP, Wg, bWg):
                act(PT[1][:, 0:n], P[:, 0:n], AF.Tanh, [bP], [b_PT[1]], scale=0.5)
                stt(Wg[:, 0:n], PT[1][:, 0:n], 1.0, P[:, 0:n], ALU.add, ALU.mult, [b_PT[1], bP], [bWg])

            def gate_from(colbase, Wg, bWg):
                v, b = wslice(colbase, 128)
                P, bP = next_acc()
                proj(v, b, 0, 128, rhs, 32, n, P, bP, [b_hT])
                silu_from(P, bP, Wg, bWg)

            MARKS["B2_%s%d" % (kind, i)] = T.n_emit
            IC, bIC = W[7], b_W[7]
            Uv = U[:, :, 0:nseq * L].rearrange("p c (s l) -> p c s l", l=L)
            pout = xs[0]; bpout = b_xs[0]
            for g in range(4):
                wwin = 2 << g
                if kind == "p":
                    T.dma("sp", IC[:, 0:n], invc_d[i, :, g, :], (), [bIC])
                else:
                    T.dma("sp", IC[:, 0:n], invs_d[:, g, :], (), [bIC])
                for c in range(4):
                    v, b = wslice(OFF_U + (g * 4 + c) * 128, 128)
                    P, bP = next_acc()
                    proj(v, b, 0, 128, rhs, 32, n, P, bP, [b_hT])
                    cp(Uv[:, c, :, 16:L], P[:, 0:n].rearrange("p (s t) -> p s t", t=nt), [bP], [b_U], eng="dve")
                    if kind == "p":
                        P2, bP2 = next_acc()
                        proj(v, b, 0, 128, lambda k: hTh[:, k, 0:16], 32, 16, P2, bP2, [b_hTh])
                        cp(Uv[:, c, 0, 0:16], P2[:, 0:16], [bP2], [b_U], eng="dve")
                    else:
                        cp(Uv[:, c, :, 0:16], UH[:, g * 4 + c, :, :], [b_UH], [b_U], eng="dve")
                for s in range(nseq):
                    P, bP = next_pT()
                    for c in range(4):
                        tr(P[0:15, c * 128:(c + 1) * 128], Uv[:, c, s, L - 15:L], ident_f[:, :], [b_U, b_ident], [bP], c == 3)
                    cp(pout[32 * s:32 * s + 15, g * 512:(g + 1) * 512], P[0:15, :], [bP], [bpout], eng="dve")
                for c in range(4):
                    src = Uv[:, c, :, :]
                    sh = 1
                    cur = src; bcur = b_U
                    pi = 0
                    lo = 0
                    while sh < wwin:
                        lo2 = lo + sh
                        dstt = PT[pi][:, 0:nseq * L].rearrange("p (s l) -> p s l", l=L)
                        tt(dstt[:, :, lo2:L], cur[:, :, lo2:L], cur[:, :, lo:L - sh], ALU.add, [bcur], [b_PT[pi]])
                        cur = dstt; bcur = b_PT[pi]
                        pi ^= 1
                        lo = lo2
                        sh *= 2
                    tmp = PT[pi][:, 0:n].rearrange("p (s t) -> p s t", t=nt)
                    tt(tmp, cur[:, :, 16:L], IC[:, 0:n].rearrange("p (s t) -> p s t", t=nt), ALU.mult, [bcur, bIC], [b_PT[pi]])
                    tt(Bt[c][:, 0:n].rearrange("p (s t) -> p s t", t=nt), tmp, src[:, :, 16:L], ALU.subtract, [b_PT[pi], b_U], [b_B[c]])
                wp = wpool_sb
                T.dma("sp", wp[:, :, :], WP[g], [b_conv], [b_wpool])
                for ec in range(4):
                    gate_from(OFF_ZP + (g * 4 + ec) * 128, W[4], b_W[4])
                    for c in range(4):
                        mm(pO[:, 0:n], wp[:, c, ec * 128:(ec + 1) * 128], Bt[c][:, 0:n], c == 0, c == 3, [b_wpool, b_B[c]], [b_pO])
                    stt(W[4][:, 0:n], W[4][:, 0:n], 0.5, pO[:, 0:n], ALU.mult, ALU.mult, [b_pO], [b_W[4]])
                    ts(mixT[:, 16 + g * 4 + ec, 0:n], W[4][:, 0:n], gcols[:, 50 + g * 4 + ec:51 + g * 4 + ec], 1.0, ALU.mult, ALU.mult,
                       [b_W[4], b_g], [b_RM])
            for s in range(nseq):
                dst = opool[i, :, :] if kind == "p" else spo[s, :, :]
                T.dma("pool", dst, pout[32 * s:32 * s + 15, 0:2048], [bpout], ())

            MARKS["B3_%s%d" % (kind, i)] = T.n_emit
            if kind == "p":
                E = 2048 * (i + 1)
                seqs = [(0, 512, E, None)]
            else:
                seqs = [(0, 64, 4160, 0), (64, 64, 4160, 1)]
            RC, bRC = W[5], b_W[5]
            RS, bRS = W[6], b_W[6]
            if kind == "p":
                T.dma("sp", RC[0:64, 0:n], ropeO[i, 0, :, :], (), [bRC])
                T.dma("sp", RS[0:64, 0:n], ropeO[i, 1, :, :], (), [bRS])
            else:
                T.dma("sp", RC[0:64, 0:n], ropeS[0, :, :], (), [bRC])
                T.dma("sp", RS[0:64, 0:n], ropeS[1, :, :], (), [bRS])
            qbuf = [(QnT, b_Qn, QrT, b_Qr, W[7], b_W[7]), (QnT2, b_Qn2, QrT2, b_Qr2, W[4], b_W[4])]

            def prep_stages(h):
                Qn_, bQn_, Qr_, bQr_, Wg, bWg = qbuf[h % 2]
                stt_ = {}
                qr = lambda kk: cqnT[:, kk, 0:n]

                def s1():
                    v, b = wslice(OFF_ZA + h * 128, 128)
                    P, bP = next_acc()
                    proj(v, b, 0, 128, rhs, 32, n, P, bP, [b_hT])
                    stt_["g"] = (P, bP)

                def s2():
                    P, bP = stt_["g"]
                    silu_from(P, bP, Wg, bWg)

                def s3():
                    k = st["slot"] = (st["slot"] + 1) % 4
                    wq = slots[k][:, 0:8 * 256].rearrange("p (c e) -> p c e", e=256)
                    T.dma("sp", wq, WQ[h], [b_conv], [b_slot[k]])
                    P, bP = next_acc()
                    proj(wq, b_slot[k], 0, 128, qr, 8, n, P, bP, [b_cq])
                    P1, bP1 = next_acc()
                    proj(wq, b_slot[k], 128, 64, qr, 8, n, P1, bP1, [b_cq])
                    stt_["q"] = (wq, b_slot[k], P, bP, P1, bP1)

                def s4():
                    wq, bwq, P, bP, P1, bP1 = stt_["q"]
                    act(Bt[2][:, 0:n], P[:, 0:n], AF.Square, [bP], [b_B[2]])
                    cp(W[1][0:64, 0:n], P1[0:64, 0:n], [bP1], [b_W[1]], eng="dve")
                    act(Bt[3][0:64, 0:n], P1[0:64, 0:n], AF.Square, [bP1], [b_B[3]])
                    P2, bP2 = next_pT()
                    proj(wq, bwq, 192, 64, qr, 8, n, P2, bP2, [b_cq])
                    stt_["p2"] = (P2, bP2)

                def s5():
                    wq, bwq, P, bP, P1, bP1 = stt_["q"]
                    P2, bP2 = stt_["p2"]
                    X, bX = next_pT()
                    mm(X[:, 0:n], ones_b[:, :], Bt[2][:, 0:n], True, True, [b_ones, b_B[2]], [bX])
                    rstd_from(X[:, 0:n], bX, 128, W[0][:, 0:n], b_W[0], W[0][:, 0:n], b_W[0], 128)
                    stt(Qn_[:, 0:n], P[:, 0:n], gcols[:, 44:45], W[0][:, 0:n], ALU.mult, ALU.mult, [bP, b_W[0], b_g], [bQn_])
                    stt(W[2][0:64, 0:n], P2[0:64, 0:n], gcols[0:64, 47:48], RS[0:64, 0:n], ALU.mult, ALU.mult, [bP2, bRS, b_g], [b_W[2]])

                def s6():
                    X, bX = next_pT()
                    mm(X[0:64, 0:n], ones_b[0:64, 0:64], Bt[3][0:64, 0:n], True, True, [b_ones, b_B[3]], [bX])
                    rstd_from(X[0:64, 0:n], bX, DR, W[0][0:64, 0:n], b_W[0], W[0][0:64, 0:n], b_W[0], 64)
                    stt(W[1][0:64, 0:n], W[1][0:64, 0:n], gcols[0:64, 46:47], RC[0:64, 0:n], ALU.mult, ALU.mult, [bRC, b_g], [b_W[1]])
                    tt(W[1][0:64, 0:n], W[1][0:64, 0:n], W[2][0:64, 0:n], ALU.add, [b_W[2]], [b_W[1]])
                    tt(Qr_[0:64, 0:n], W[1][0:64, 0:n], W[0][0:64, 0:n], ALU.mult, [b_W[1], b_W[0]], [bQr_])

                return [s1, s2, s3, s4, s5, s6]

            def prep(h):
                for f_ in prep_stages(h):
                    f_()

            def load_chunk(h, sidx, S, ci, ntile):
                kc = st["kvc"] = st["kvc"] ^ 1
                bk = b_kvc[kc]
                k0 = ci * 1024
                nkc = min(1024, S - k0)
                if sidx is None:
                    nch = ntile // 8
                    eoff = 1024 if ci == nch - 2 else (2048 if ci == nch - 1 else 0)
                    T.dma("sp", KTc[kc][:, 0:nkc], KTp[h, :, k0:k0 + nkc], [b_scr], [bk])
                    T.dma("sp", krc[kc][0:64, 0:nkc], krp[:, k0:k0 + nkc], [b_scr], [bk])
                    T.dma("sp", krc[kc][64:96, 0:nkc], EKb[:, eoff:eoff + nkc], [b_ek], [bk])
                    T.dma("sp", Vc[kc][:, :, :, :], Vp[h, k0 // 512:k0 // 512 + 2, :, :, :].rearrange("b p t d -> p b t d"), [b_scr], [b_vc[kc]])
                else:
                    T.dma("sp", KTc[kc][:, 0:nkc], KTs[sidx, h, :, k0:k0 + nkc], [b_scr], [bk])
                    T.dma("sp", krc[kc][0:64, 0:nkc], krs[sidx, :, k0:k0 + nkc], [b_scr], [bk])
                    T.dma("sp", krc[kc][64:96, 0:nkc], EKb[:, 0:nkc], [b_ek], [bk])
                    if nkc == 1024:
                        T.dma("sp", Vc[kc][:, :, :, :], Vs[sidx, h, k0 // 512:k0 // 512 + 2, :, :, :].rearrange("b p t d -> p b t d"), [b_scr], [b_vc[kc]])
                    else:
                        T.dma("sp", Vc[kc][0:nkc, 0, 0, :], Vs[sidx, h, k0 // 512, 0:nkc, 0, :], [b_scr], [b_vc[kc]])
                return kc

            chunk_slot = {}

            def attention(h, extra):
                Qn_, bQn_, Qr_, bQr_, Wg, bWg = qbuf[h % 2]
                tiles = []
                for (q0, nq, S, sidx) in seqs:
                    ntile = (S + 127) // 128
                    for t in range(ntile):
                        tiles.append((q0, nq, S, sidx, ntile, t))
                def ensure_chunk(q0, nq, S, sidx, ntile, ci, hh=None):
                    hh = h if hh is None else hh
                    key = (hh, sidx, ci)
                    if key not in chunk_slot and ci * 1024 < S and hh < NH:
                        chunk_slot[key] = load_chunk(hh, sidx, S, ci, ntile)

                groups = []
                for tl_ in tiles:
                    q0, nq, S, sidx, ntile, t = tl_
                    gsz = 512 // nq
                    full = min(128, S - t * 128) == 128
                    if groups and full and len(groups[-1]) < gsz and groups[-1][0][3] == sidx and groups[-1][0][5] // 8 == t // 8 \
                            and min(128, groups[-1][0][2] - groups[-1][0][5] * 128) == 128:
                        groups[-1].append(tl_)
                    else:
                        groups.append([tl_])
                info = {}

                def S_stage(j):
                    si = st["pS"] = st["pS"] ^ 1
                    lst = []
                    for gi, (q0, nq, S, sidx, ntile, t) in enumerate(groups[j]):
                        ci = t // 8
                        ensure_chunk(q0, nq, S, sidx, ntile, ci)
                        if len(groups[j]) == 1 and t % 8 == 1:
                            if (ci + 1) * 1024 < S:
                                ensure_chunk(q0, nq, S, sidx, ntile, ci + 1)
                            elif len(seqs) == 1:
                                ensure_chunk(q0, nq, S, sidx, ntile, 0, hh=h + 1)
                        kc = chunk_slot[(h, sidx, ci)]
                        bk = b_kvc[kc]
                        nk = min(128, S - t * 128)
                        tl = t % 8
                        last = gi == len(groups[j]) - 1
                        mm(pS[si][0:nk, gi * nq:(gi + 1) * nq], KTc[kc][:, tl * 128:tl * 128 + nk], Qn_[:, q0:q0 + nq], True, False, [bk, bQn_], [b_pS[si]], False)
                        mm(pS[si][0:nk, gi * nq:(gi + 1) * nq], krc[kc][:, tl * 128:tl * 128 + nk], Qr_[:, q0:q0 + nq], False, True, [bk, bQr_], [b_pS[si]], last)
                        lst.append((kc, nk, tl))
                    info[j] = (si, lst)

                def E_stage(j):
                    si, lst = info[j]
                    nq = groups[j][0][1]
                    nk = lst[0][1]
                    w = len(lst) * nq
                    bi = 4 + (j % 2)
                    act(Bt[bi][0:nk, 0:w], pS[si][0:nk, 0:w], AF.Exp, [b_pS[si]], [b_B[bi]], scale=SCALE)

                def PV_stage(j):
                    si, lst = info[j]
                    bi = 4 + (j % 2)
                    for gi, (q0, nq, S, sidx, ntile, t) in enumerate(groups[j]):
                        kc, nk, tl = lst[gi]
                        last = gi == len(groups[j]) - 1
                        if dve_sum:
                            mm(pO[:, q0:q0 + nq], Vc[kc][0:nk, tl // 4, tl % 4, :], Bt[bi][0:nk, gi * nq:(gi + 1) * nq], t == 0, t == ntile - 1, [b_vc[kc], b_B[bi]], [b_pO], True)
                            if t == 0:
                                T.op("dve", lambda e, bi=bi: e.tensor_copy(out=acc, in_=Bt[bi][:, 0:512]), [b_B[bi]], [b_U])
                            else:
                                tt(acc, acc, Bt[bi][:, 0:512], ALU.add, [b_B[bi]], [b_U])
                        else:
                            mm(pO[:, q0:q0 + nq], Vc[kc][0:nk, tl // 4, tl % 4, :], Bt[bi][0:nk, gi * nq:(gi + 1) * nq], t == 0, t == ntile - 1, [b_vc[kc], b_B[bi]], [b_pO], False)
                            mm(pR[:, q0:q0 + nq], ones_b[0:nk, :], Bt[bi][0:nk, gi * nq:(gi + 1) * nq], t == 0, t == ntile - 1, [b_ones, b_B[bi]], [b_pR], last)

                ng = len(groups)
                if kind == "p":
                    for f_ in extra:
                        f_()
                    when = {}
                else:
                    when = {1 + k: f_ for k, f_ in enumerate(extra)} if ng > len(extra) else {}
                dve_sum = False
                acc = U[:, 0, 0:512]
                S_stage(0)
                for j in range(ng):
                    if j + 1 < ng:
                        S_stage(j + 1)
                    E_stage(j)
                    PV_stage(j)
                    if j in when:
                        when.pop(j)()
                    if len(seqs) > 1:
                        jj = j + 2
                        hh_ = h
                        if jj >= len(groups):
                            jj -= len(groups)
                            hh_ = h + 1
                        q0_, nq_, S_, sidx_, ntile_, t_ = groups[jj][0]
                        ensure_chunk(q0_, nq_, S_, sidx_, ntile_, t_ // 8, hh=hh_)
                if dve_sum:
                    T.op("dve", lambda e: e.tensor_copy(out=KTt[0][:, 0:512], in_=acc), [b_U], [b_KTt[0]])
                    tt(KTt[1][:, 0:512], acc, KTt[0][:, 0:512], ALU.subtract, [b_U, b_KTt[0]], [b_KTt[1]])
                    mm(pR[:, 0:512], ones_b[:, :], KTt[0][:, 0:512], True, False, [b_ones, b_KTt[0]], [b_pR], False)
                    mm(pR[:, 0:512], ones_b[:, :], KTt[1][:, 0:512], False, True, [b_ones, b_KTt[1]], [b_pR], True)
                cp(PT[0][:, 0:n], pO[:, 0:n], [b_pO], [b_PT[0]], eng="act")
                cp(W[3][:, 0:n], pR[:, 0:n], [b_pR], [b_W[3]], eng="dve")
                T.op("dve", lambda e: e.reciprocal(out=W[3][:, 0:n], in_=W[3][:, 0:n]), [], [b_W[3]])
                stt(W[3][:, 0:n], W[3][:, 0:n], 0.5, Wg[:, 0:n], ALU.mult, ALU.mult, [bWg], [b_W[3]])
                tt(mixT[:, h, 0:n], PT[0][:, 0:n], W[3][:, 0:n], ALU.mult, [b_PT[0], b_W[3]], [b_RM])
                for k_ in sorted(when):
                    when[k_]()

            prep(0)
            for h in range(NH):
                attention(h, prep_stages(h + 1) if h + 1 < NH else [])

            MARKS["B4_%s%d" % (kind, i)] = T.n_emit
            ydst = (lambda t0, ntok, c0: yown[i, t0:t0 + ntok, c0:c0 + 512]) if kind == "p" else (lambda t0, ntok, c0: ysam[t0:t0 + ntok, c0:c0 + 512])
            xsrc = (lambda t0, ntok, c0: xown[i, 16 + t0:16 + t0 + ntok, c0:c0 + 512]) if kind == "p" else (lambda t0, ntok, c0: xsam[t0:t0 + ntok, c0:c0 + 512])
            X2s = [(W[4], b_W[4]), (W[5], b_W[5]), (W[6], b_W[6]), (W[7], b_W[7])]
            for dg in range(8):
                for t0 in range(0, n, 128):
                    X2, bX2 = X2s[(t0 // 128) % 4]
                    T.dma("sp", X2[:, :], xsrc(t0, 128, dg * 512), (), [bX2])
                for q in range(4):
                    dc = dg * 4 + q
                    v, b = stream_chunk(WOUT[dc], 32, 128)
                    P, bP = next_acc()
                    proj(v, b, 0, 128, lambda kk: mixT[:, kk, 0:n], 32, n, P, bP, [b_RM])
                    cp(W[q][:, 0:n], P[:, 0:n], [bP], [b_W[q]])
                for t0 in range(0, n, 128):
                    X2, bX2 = X2s[(t0 // 128) % 4]
                    P, bP = next_pT()
                    for q in range(4):
                        tr(P[:, q * 128:(q + 1) * 128], W[q][:, t0:t0 + 128], ident_f[:, :], [b_W[q], b_ident], [bP], q == 3)
                    tt(X2[:, :], X2[:, :], P[:, :], ALU.add, [bP], [bX2])
                    T.dma("pool", ydst(t0, 128, dg * 512), X2[:, :], [bX2], ())

        own_block("s", 0)
        MARKS["B5_s0"] = T.n_emit
        for i in range(4):
            own_block("p", i)
            MARKS["B5_p%d" % i] = T.n_emit

        T.finish()

        with nc.Block() as block:
            @block.tensor
            def _(e):
                T.replay("pe", e)

            @block.scalar
            def _(e):
                T.replay("act", e)

            @block.vector
            def _(e):
                T.replay("dve", e)

            @block.gpsimd
            def _(e):
                T.replay("pool", e)

            @block.sync
            def _(e):
                T.replay("sp", e)
    return nc


_NC = None


def _rope_tables(pos):
    half = DR // 2
    freqs = (np.float32(10000.0) ** (-np.arange(half, dtype=np.float32) / np.float32(half))).astype(np.float32)
    ang = pos.astype(np.float32)[:, None] * freqs[None, :]
    cos = np.cos(ang).astype(np.float32).T
    sin = np.sin(ang).astype(np.float32).T
    C = np.concatenate([cos, cos], 0)
    S = np.concatenate([-sin, sin], 0)
    return np.ascontiguousarray(np.stack([C, S], 0))


def kernel(x_prompt, x_sample, cache_ckv, cache_krope, state_pool, g_norm, w_in, g_q_lat, w_uq, g_qn, g_qr,
           g_kv_lat, g_kr, w_ukv, g_kn, w_pool, pool_scale, w_out):
    global _NC
    if _NC is None:
        _NC = build_program()
    nc = _NC
    in_maps = prepare(x_prompt, x_sample, cache_ckv, cache_krope, state_pool, g_norm, w_in, g_q_lat, w_uq, g_qn, g_qr,
                      g_kv_lat, g_kr, w_ukv, g_kn, w_pool, pool_scale, w_out)
    res = run_bass_kernel_spmd(nc, in_maps, core_ids=list(range(8)))
    return assemble(res.results)


def prepare(x_prompt, x_sample, cache_ckv, cache_krope, state_pool, g_norm, w_in, g_q_lat, w_uq, g_qn, g_qr,
            g_kv_lat, g_kr, w_ukv, g_kn, w_pool, pool_scale, w_out, cores=range(8)):
    f = lambda a: np.ascontiguousarray(np.asarray(a, dtype=np.float32))
    x_prompt = f(x_prompt); x_sample = f(x_sample); cache_ckv = f(cache_ckv); cache_krope = f(cache_krope)
    state_pool = f(state_pool)
    w_in0 = f(w_in)[0]; w_uq0 = f(w_uq)[0]; w_ukv0 = f(w_ukv)[0]; w_pool0 = f(w_pool)[0]; w_out0 = f(w_out)[0]
    gcols = np.zeros((128, 72), np.float32)
    gcols[:, 0:32] = f(g_norm)[0].reshape(32, 128).T
    gcols[:, 32:40] = f(g_q_lat)[0].reshape(8, 128).T
    gcols[:, 40:44] = f(g_kv_lat)[0].reshape(4, 128).T
    gcols[:, 44] = f(g_qn)[0]
    gcols[:, 45] = f(g_kn)[0]
    gqr = f(g_qr)[0]; gkr = f(g_kr)[0]
    gcols[0:64, 46] = gqr
    gcols[0:64, 47] = np.concatenate([gqr[32:], gqr[:32]])
    gcols[0:64, 48] = gkr
    gcols[0:64, 49] = np.concatenate([gkr[32:], gkr[:32]])
    gcols[:, 50:66] = f(pool_scale)[0].reshape(16, 128).T

    ident = np.eye(128, dtype=np.float32)
    ropeA = _rope_tables(np.arange(SEQ))
    rs = _rope_tables(PAST + np.arange(64))
    ropeS = np.ascontiguousarray(np.concatenate([rs, rs], 2))
    EK = np.zeros((32, 3072), np.float32)
    EK[np.arange(2048) // 64, 1024 + np.arange(2048)] = 1.0
    windows = (2, 4, 8, 16)
    invs = np.zeros((128, 4, 128), np.float32)
    for g, w in enumerate(windows):
        invs[:, g, :] = 1.0 / w

    in_maps = []
    for c in cores:
        s, j = c // 4, c % 4
        xown = np.zeros((4, 528, D), np.float32)
        ropeO = np.zeros((4, 2, 64, 512), np.float32)
        invc = np.zeros((4, 128, 4, 512), np.float32)
        for i in range(4):
            b = 4 * i + j
            lo = b * 512 - 16
            if lo >= 0:
                xown[i] = x_prompt[s, lo:lo + 528]
            else:
                xown[i, 16:] = x_prompt[s, 0:512]
            ropeO[i] = ropeA[:, :, b * 512:(b + 1) * 512]
            pos = b * 512 + np.arange(512)
            for g, w in enumerate(windows):
                invc[i, :, g, :] = (1.0 / np.minimum(pos + 1, w).astype(np.float32))[None, :]
        LQ = np.zeros((32, 512), np.float32)
        qchunk = 8 * j + np.arange(512) // 64
        LQ[np.arange(32)[:, None] > qchunk[None, :]] = NEG
        in_maps.append({
            "xseq": x_prompt[s], "xown": xown, "xsam": np.ascontiguousarray(x_sample[2 * c:2 * c + 2].reshape(128, D)),
            "cckv": np.ascontiguousarray(cache_ckv[0, 2 * c:2 * c + 2]), "ckr": np.ascontiguousarray(cache_krope[0, 2 * c:2 * c + 2]),
            "spool": np.ascontiguousarray(state_pool[0, 2 * c:2 * c + 2]),
            "w_in": w_in0, "w_uq": w_uq0, "w_ukv": w_ukv0, "w_pool": w_pool0, "w_out": w_out0,
            "ident": ident, "gcols": gcols, "ropeA": ropeA, "ropeO": ropeO, "ropeS": ropeS, "EK": EK, "LQ": LQ,
            "invc": invc, "invs": invs,
        })
    return in_maps


def assemble(R):
    y_p = np.zeros((2, SEQ, D), np.float32)
    ckv_p = np.zeros((1, 2, SEQ, KVL), np.float32)
    kr_p = np.zeros((1, 2, SEQ, DR), np.float32)
    pool_p = np.zeros((1, 2, 15, 2048), np.float32)
    y_s = np.zeros((16, 64, D), np.float32)
    ckv_s = np.zeros((1, 16, 64, KVL), np.float32)
    kr_s = np.zeros((1, 16, 64, DR), np.float32)
    pool_s = np.zeros((1, 16, 15, 2048), np.float32)
    for c in range(8):
        s, j = c // 4, c % 4
        r = R[c]
        for i in range(4):
            b = 4 * i + j
            y_p[s, b * 512:(b + 1) * 512] = r["yown"][i]
            ckv_p[0, s, b * 512:(b + 1) * 512] = r["ockv"][i]
            kr_p[0, s, b * 512:(b + 1) * 512] = r["okr"][i]
            if b == 15:
                pool_p[0, s] = r["opool"][i]
        y_s[2 * c:2 * c + 2] = r["ysam"].reshape(2, 64, D)
        ckv_s[0, 2 * c:2 * c + 2] = r["sckv"].reshape(2, 64, KVL)
        kr_s[0, 2 * c:2 * c + 2] = r["skr"].reshape(2, 64, DR)
        pool_s[0, 2 * c:2 * c + 2] = r["spo"]
    return (y_p, y_s, ckv_p, kr_p, pool_p, ckv_s, kr_s, pool_s)
```

```python
import contextlib
import numpy as np
import concourse.bass as bass
import concourse.mybir as mybir
from concourse.bass_utils import run_bass_kernel_spmd

F32 = mybir.dt.float32
BF16 = mybir.dt.bfloat16
AF = mybir.ActivationFunctionType
ALU = mybir.AluOpType

D = 4096
SEQ = 8192
NH = 16
QL = 1024
KVL = 512
DR = 64
DIN = 7744
OFF_Q, OFF_KV, OFF_KR, OFF_ZA, OFF_U, OFF_ZP = 0, 1024, 1536, 1600, 3648, 5696
PAST = 4096
EPS = 1e-6
SCALE = 192.0 ** -0.5
NEG = -30000.0
NDSEM = 20
LIMIT = None
MARKS = {}
PE_AT = {}


class Buf:
    __slots__ = ("w", "r", "excl")

    def __init__(self, excl=False):
        self.w = {}
        self.r = {}
        self.excl = excl


class Tracker:
    def __init__(self, nc, es):
        self.nc = nc
        self.streams = {k: [] for k in ("pe", "act", "dve", "pool", "sp")}
        self.sem = {}
        self.cnt = {}
        self.waited = {k: {} for k in self.streams}
        for k in ("pe", "act", "dve"):
            self.sem[k] = es.enter_context(nc.semaphore("s_" + k))
            self.cnt[k] = 0
        self.n_emit = 0
        self.n_pe = 0
        self.limit = LIMIT
        self.dma_i = {"sp": 0, "pool": 0}
        self.dma_last = {"sp": [None] * NDSEM, "pool": [None] * NDSEM}
        for q in ("sp", "pool"):
            for k in range(NDSEM):
                nm = "%sd%d" % (q, k)
                self.sem[nm] = es.enter_context(nc.semaphore("s_" + nm))
                self.cnt[nm] = 0

    def _wait(self, eng, tok):
        if tok is None:
            return
        s, v = tok
        if s == eng and eng == "pe":
            return
        if self.waited[eng].get(s, 0) >= v:
            return
        self.waited[eng][s] = v
        self.streams[eng].append(("wait", s, v))

    def _deps(self, eng, reads, writes):
        for b in reads:
            for s, v in list(b.w.items()):
                self._wait(eng, (s, v))
            if b.excl:
                for s, v in list(b.r.items()):
                    if s != eng:
                        self._wait(eng, (s, v))
        for b in writes:
            for s, v in list(b.w.items()):
                if s.startswith(eng + "d") or s == eng:
                    continue
                self._wait(eng, (s, v))
            for s, v in list(b.r.items()):
                if s == eng:
                    continue
                self._wait(eng, (s, v))

    def _mark(self, tok, reads, writes):
        s, v = tok
        for b in reads:
            if b.r.get(s, 0) < v:
                b.r[s] = v
        for b in writes:
            if s[-1].isdigit() and not b.r and b.w and all(k[-1].isdigit() for k in b.w):
                b.w[s] = v
            else:
                b.w = {s: v}
            b.r = {}

    def op(self, eng, fn, reads=(), writes=(), signal=True):
        self.n_emit += 1
        if self.limit is not None and self.n_emit > self.limit:
            return None
        self._deps(eng, reads, writes)
        if eng == "pe":
            self.n_pe += 1
            PE_AT[self.n_emit] = self.n_pe
        if signal:
            self.cnt[eng] += 1
            tok = (eng, self.cnt[eng])
        else:
            tok = (eng, self.cnt[eng] + 1)
        self.streams[eng].append(("op", fn, signal))
        self._mark(tok, reads, writes)
        return tok

    def dma(self, q, out, in_, reads=(), writes=()):
        self.n_emit += 1
        if self.limit is not None and self.n_emit > self.limit:
            return None
        self._deps(q, reads, writes)
        k = self.dma_i[q] % NDSEM
        self.dma_i[q] += 1
        self._wait(q, self.dma_last[q][k])
        nm = "%sd%d" % (q, k)
        self.cnt[nm] += 16
        tok = (nm, self.cnt[nm])
        self.dma_last[q][k] = tok
        self.streams[q].append(("dma", out, in_, nm))
        self._mark(tok, reads, writes)
        return tok

    def finish(self):
        for q in ("sp", "pool"):
            for k in range(NDSEM):
                self._wait(q, self.dma_last[q][k])

    def replay(self, eng, e):
        for it in self.streams[eng]:
            if it[0] == "wait":
                e.wait_ge(self.sem[it[1]], it[2])
            elif it[0] == "op":
                ins = it[1](e)
                if it[2]:
                    ins.then_inc(self.sem[eng], 1)
            else:
                e.dma_start(out=it[1], in_=it[2]).then_inc(self.sem[it[3]], 16)


def build_program():
    nc = bass.Bass("TRN2", target_bir_lowering=False)

    def din(name, shape, dt=F32):
        return nc.dram_tensor(name, list(shape), dt, kind="ExternalInput").ap()

    def dout(name, shape):
        return nc.dram_tensor(name, list(shape), F32, kind="ExternalOutput").ap()

    def dscr(name, shape):
        return nc.dram_tensor(name, list(shape), BF16, kind="Internal").ap()

    xseq = din("xseq", [SEQ, D])
    xown = din("xown", [4, 528, D])
    xsam = din("xsam", [128, D])
    cckv = din("cckv", [2, PAST, KVL])
    ckr = din("ckr", [2, PAST, DR])
    spool = din("spool", [2, 15, 2048])
    w_in = din("w_in", [D, DIN])
    w_uq = din("w_uq", [QL, NH, 192])
    w_ukv = din("w_ukv", [KVL, NH, 256])
    w_pool = din("w_pool", [4, 512, 512])
    w_out = din("w_out", [D, D])
    ident = din("ident", [128, 128])
    gcols_d = din("gcols", [128, 72])
    ropeA = din("ropeA", [2, 64, SEQ])
    ropeO = din("ropeO", [4, 2, 64, 512])
    ropeS = din("ropeS", [2, 64, 128])
    EK_d = din("EK", [32, 3072])
    LQ_d = din("LQ", [32, 512])
    invc_d = din("invc", [4, 128, 4, 512])
    invs_d = din("invs", [128, 4, 128])

    yown = dout("yown", [4, 512, D])
    ysam = dout("ysam", [128, D])
    ockv = dout("ockv", [4, 512, KVL])
    okr = dout("okr", [4, 512, DR])
    opool = dout("opool", [4, 15, 2048])
    sckv = dout("sckv", [128, KVL])
    skr = dout("skr", [128, DR])
    spo = dout("spo", [2, 15, 2048])

    KTp = dscr("KTp", [NH, 128, SEQ])
    Vp = dscr("Vp", [NH, 16, 128, 4, 128])
    krp = dscr("krp", [64, SEQ])
    KTs = dscr("KTs", [2, NH, 128, 4608])
    Vs = dscr("Vs", [2, NH, 9, 128, 4, 128])
    krs = dscr("krs", [2, 64, 4608])
    WIN = dscr("WIN", [61, 128, 32, 128])
    WOUT = dscr("WOUT", [32, 128, 32, 128])
    WQ = dscr("WQ", [16, 128, 8, 256])
    WP = dscr("WP", [4, 128, 4, 512])

    es = contextlib.ExitStack()
    with es:
        def sb(name, shape, dt=F32):
            return es.enter_context(nc.sbuf_tensor("sb_" + name, list(shape), dt))

        def ps(name, shape, dt=F32):
            return es.enter_context(nc.psum_tensor("ps_" + name, list(shape), dt))

        T = Tracker(nc, es)

        ident_f = sb("ident_f", [128, 128]); b_ident = Buf()
        ones_b = sb("ones_b", [128, 128], BF16); b_ones = Buf()
        eps_t = sb("eps_t", [128, 1]); b_eps = Buf()
        gcols = sb("gcols", [128, 72]); b_g = Buf()
        EKb = sb("EKb", [32, 3072], BF16); b_ek = Buf()
        xs = [sb("xs%d" % k, [128, D]) for k in range(2)]; b_xs = [Buf(), Buf()]
        ssx = [sb("ssx%d" % k, [128, 8]) for k in range(2)]; b_ssx = [Buf(), Buf()]
        hT = sb("hT", [128, 32, 512], BF16); b_hT = Buf()
        hTh = sb("hTh", [128, 32, 16], BF16); b_hTh = Buf()
        RM = sb("RM", [128, 16384], BF16); b_RM = Buf()
        cqnT = sb("cqnT", [128, 8, 512], BF16); b_cq = Buf()
        RW = sb("RW", [128, 16384], BF16)
        b_slot = [Buf() for _ in range(4)]
        KVR = sb("KVR", [128, 6144], BF16)
        KTc = [KVR[:, k * 1024:(k + 1) * 1024] for k in range(2)]
        Vc = [KVR[:, 2048 + k * 1024:2048 + (k + 1) * 1024].rearrange("p (b t d) -> p b t d", t=4, d=128) for k in range(2)]
        krc = [KVR[:, 4096 + k * 1024:4096 + (k + 1) * 1024] for k in range(2)]
        b_kvc = [Buf(), Buf()]
        b_vc = [Buf(), Buf()]
        b_scr = Buf(); b_conv = Buf()
        junk = KVR[:, 0:2048]
        NW = 8
        W = [sb("W%d" % k, [128, 512]) for k in range(NW)]; b_W = [Buf() for _ in range(NW)]
        NB = 6
        Bt = [sb("B%d" % k, [128, 512], BF16) for k in range(NB)]; b_B = [Buf() for _ in range(NB)]
        U = sb("U", [128, 4, 528]); b_U = Buf()
        PT = [sb("PT%d" % k, [128, 528]) for k in range(2)]; b_PT = [Buf(), Buf()]
        UH = sb("UH", [128, 16, 2, 16]); b_UH = Buf()
        ckvT0 = sb("ckvT0", [128, 4, 512], BF16); ckvT = [ckvT0, ckvT0]; b_ck0 = Buf(); b_ckvT = [b_ck0, b_ck0]
        krT = sb("krT", [64, 512], BF16); b_krT = Buf()
        Vts = [(KVR[:, 0:2048], [b_kvc[0], b_kvc[1]]), (KVR[:, 2048:4096], [b_vc[0], b_vc[1]])]
        KTt = [sb("KTt%d" % k, [128, 512], BF16) for k in range(2)]; b_KTt = [Buf(), Buf()]
        wpool_sb = sb("wpool_sb", [128, 4, 512], BF16); b_wpool = Buf()
        QnT = sb("QnT", [128, 512], BF16); b_Qn = Buf()
        QrT = sb("QrT", [128, 512], BF16); b_Qr = Buf()
        QnT2 = sb("QnT2", [128, 512], BF16); b_Qn2 = Buf()
        QrT2 = sb("QrT2", [128, 512], BF16); b_Qr2 = Buf()

        pA = ps("pA", [128, 512]); pB = ps("pB", [128, 512])
        pS = [ps("pS0", [128, 512]), ps("pS1", [128, 512])]
        pO = ps("pO", [128, 512]); pR = ps("pR", [128, 512])
        pT = [ps("pT0", [128, 512]), ps("pT1", [128, 512])]
        b_pA = Buf(True); b_pB = Buf(True); b_pS = [Buf(True), Buf(True)]; b_pO = Buf(True); b_pR = Buf(True); b_pT = [Buf(True), Buf(True)]
        acc = [(pA, b_pA), (pB, b_pB)]
        st = {"acc": 0, "pT": 0, "xs": 0, "slot": 0, "ev": 0, "kvc": 0, "pS": 0, "B": 0, "ktt": 0, "ckv": 0}

        wkv_v = RM[:, 0:32 * 512].rearrange("p (c e) -> p c e", e=512)
        wkr_v = cqnT[:, :, :].rearrange("p a b -> p (a b)").rearrange("p (c e) -> p c e", e=128)
        mixT = RM[:, 0:32 * 512].rearrange("p (c e) -> p c e", e=512)
        wk_v = RW[:, 0:8192].rearrange("p (c e) -> p c e", e=2048)
        wv_v = RW[:, 8192:16384].rearrange("p (c e) -> p c e", e=2048)
        slots = [RW[:, k * 4096:(k + 1) * 4096] for k in range(4)]

        def mm(out, lhsT, rhs, start, stop, reads, writes, signal=None):
            if signal is None:
                signal = stop
            T.op("pe", lambda e: e.matmul(out, lhsT=lhsT, rhs=rhs, start=start, stop=stop), reads, writes, signal)

        def tr(out, in_, idn, reads, writes, signal):
            T.op("pe", lambda e: e.transpose(out=out, in_=in_, identity=idn), reads, writes, signal)

        def act(out, in_, func, reads, writes, scale=1.0, bias=None, accum=None):
            kw = {}
            if bias is not None:
                kw["bias"] = bias
            if accum is not None:
                kw["accum_out"] = accum
            T.op("act", lambda e: e.activation(out=out, in_=in_, func=func, scale=scale, **kw), reads, writes)

        def stt(out, in0, scalar, in1, op0, op1, reads, writes):
            T.op("dve", lambda e: e.scalar_tensor_tensor(out=out, in0=in0, scalar=scalar, in1=in1, op0=op0, op1=op1), reads, writes)

        def tt(out, in0, in1, op, reads, writes):
            T.op("dve", lambda e: e.tensor_tensor(out=out, in0=in0, in1=in1, op=op), reads, writes)

        def ts(out, in0, s1, s2, op0, op1, reads, writes):
            T.op("dve", lambda e: e.tensor_scalar(out=out, in0=in0, scalar1=s1, scalar2=s2, op0=op0, op1=op1), reads, writes)

        def cp(out, in_, reads, writes, eng=None):
            if eng is None:
                st["ev"] ^= 1
                eng = "act" if st["ev"] else "dve"
            if eng == "act":
                act(out, in_, AF.Copy, reads, writes)
            else:
                T.op("dve", lambda e: e.tensor_copy(out=out, in_=in_), reads, writes)

        def memset(ap, val, writes):
            T.op("dve", lambda e: e.memset(ap, val), (), writes)

        def next_acc():
            st["acc"] ^= 1
            return acc[st["acc"]]

        def next_pT():
            st["pT"] ^= 1
            return pT[st["pT"]], b_pT[st["pT"]]

        def rstd_from(psum_ap, b_ps, n_feat, wl, b_wl, wr, b_wr, npart):
            act(wl, psum_ap, AF.Ln, [b_ps, b_eps], [b_wl], scale=1.0 / n_feat, bias=eps_t[0:npart, 0:1])
            act(wr, wl, AF.Exp, [b_wl], [b_wr], scale=-0.5)

        T.dma("sp", ident_f[:, :], ident[:, :], (), [b_ident])
        T.dma("sp", gcols[:, :], gcols_d[:, :], (), [b_g])
        T.dma("pool", EKb[:, :], EK_d[:, :], (), [b_ek])
        memset(QrT[:, :], 0.0, [b_Qr])
        T.dma("pool", QrT[64:96, :], LQ_d[:, :], (), [b_Qr])
        memset(QrT2[:, :], 0.0, [b_Qr2])
        T.dma("pool", QrT2[64:96, :], LQ_d[:, :], (), [b_Qr2])
        memset(KVR[64:128, 4096:6144], 0.0, [b_kvc[0], b_kvc[1]])
        memset(ones_b[:, :], 1.0, [b_ones])
        memset(eps_t[:, :], EPS, [b_eps])
        MARKS["const"] = T.n_emit

        def x_front(x_rows, ntok):
            k = st["xs"] = st["xs"] ^ 1
            X = xs[k]; bX = b_xs[k]; S = ssx[k]; bS = b_ssx[k]
            T.dma("sp", X[0:ntok, :], x_rows, (), [bX])
            memset(S[:, :], 0.0, [bS])
            for hh in range(2):
                act(junk[0:ntok, :], X[0:ntok, hh * 2048:(hh + 1) * 2048], AF.Square, [bX], [b_kvc[0], b_kvc[1], bS],
                    accum=S[0:ntok, hh:hh + 1])
            act(S[0:ntok, 2:3], S[0:ntok, 0:1], AF.Identity, [bS], [bS], bias=S[0:ntok, 1:2])
            act(S[0:ntok, 3:4], S[0:ntok, 2:3], AF.Ln, [bS, b_eps], [bS], scale=1.0 / D, bias=eps_t[0:ntok, 0:1])
            act(S[0:ntok, 4:5], S[0:ntok, 3:4], AF.Exp, [bS], [bS], scale=-0.5)
            for hh in range(2):
                act(X[0:ntok, hh * 2048:(hh + 1) * 2048], X[0:ntok, hh * 2048:(hh + 1) * 2048], AF.Identity, [bS], [bX],
                    scale=S[0:ntok, 4:5])
            return k

        def x_back(k, ntok, dst, b_dst):
            X = xs[k]; bX = b_xs[k]
            for c4 in range(8):
                P, bP = next_pT()
                for q in range(4):
                    c = c4 * 4 + q
                    tr(P[:, q * 128:q * 128 + ntok], X[0:ntok, c * 128:(c + 1) * 128], ident_f[0:ntok, 0:ntok],
                       [bX, b_ident], [bP], q == 3)
                for q in range(4):
                    c = c4 * 4 + q
                    ts(dst[:, c, :], P[:, q * 128:q * 128 + ntok], gcols[:, c:c + 1], 1.0, ALU.mult, ALU.mult,
                       [bP, b_g], [b_dst])

        def x_to_hT(x_rows, ntok, dst, b_dst):
            x_back(x_front(x_rows, ntok), ntok, dst, b_dst)

        def stream_chunk(src_tile, kc, width):
            k = st["slot"] = (st["slot"] + 1) % 4
            v = slots[k][:, 0:kc * width].rearrange("p (c e) -> p c e", e=width)
            T.dma("sp", v, src_tile, [b_conv], [b_slot[k]])
            return v, b_slot[k]

        def proj(wv, bw, col0, M, rhs_fn, kc, n, out_ps, b_out, rhs_bufs):
            for k in range(kc):
                mm(out_ps[0:M, 0:n], wv[:, k, col0:col0 + M], rhs_fn(k), k == 0, k == kc - 1, [bw] + rhs_bufs, [b_out])

        def out_tokmajor(srcs, bsrcs, npart, n, dst_rows_fn):
            for t0 in range(0, n, 128):
                ntok = min(128, n - t0)
                P, bP = next_pT()
                for q, s in enumerate(srcs):
                    tr(P[0:ntok, q * npart:(q + 1) * npart], s[0:npart, t0:t0 + ntok], ident_f[0:npart, 0:npart],
                       bsrcs + [b_ident], [bP], q == len(srcs) - 1)
                wi = st["B"] = (st["B"] + 1) % 2
                Wt, bWt = (W[7], b_W[7]) if wi == 0 else (PT[0], b_PT[0])
                cp(Wt[0:ntok, 0:len(srcs) * npart], P[0:ntok, 0:len(srcs) * npart], [bP], [bWt])
                T.dma("pool", dst_rows_fn(t0, ntok), Wt[0:ntok, 0:len(srcs) * npart], [bWt], ())

        def latents(get_w, rhs_fn, rhs_bufs, n, ropeC_ap, ropeS_ap, ck, b_ck, out_ckv_fn=None, out_kr_fn=None):
            RC, bRC = W[5], b_W[5]
            RS, bRS = W[6], b_W[6]
            T.dma("sp", RC[0:64, 0:n], ropeC_ap, (), [bRC])
            T.dma("sp", RS[0:64, 0:n], ropeS_ap, (), [bRS])
            for c4 in range(4):
                wv, bw, col0 = get_w(c4)
                P, bP = next_acc()
                proj(wv, bw, col0, 128, rhs_fn, 32, n, P, bP, rhs_bufs)
                cp(W[c4][:, 0:n], P[:, 0:n], [bP], [b_W[c4]], eng="dve")
                act(Bt[4 + c4 % 2][:, 0:n], P[:, 0:n], AF.Square, [bP], [b_B[4 + c4 % 2]])
                if c4 > 0:
                    mm(pR[:, 0:n], ones_b[:, :], Bt[4 + (c4 - 1) % 2][:, 0:n], c4 == 1, False, [b_ones, b_B[4 + (c4 - 1) % 2]], [b_pR], False)
            mm(pR[:, 0:n], ones_b[:, :], Bt[5][:, 0:n], False, True, [b_ones, b_B[5]], [b_pR], True)
            rstd_from(pR[:, 0:n], b_pR, KVL, W[4][:, 0:n], b_W[4], W[4][:, 0:n], b_W[4], 128)
            for c4 in range(4):
                stt(ck[:, c4, 0:n], W[c4][:, 0:n], gcols[:, 40 + c4:41 + c4], W[4][:, 0:n], ALU.mult, ALU.mult,
                    [b_W[c4], b_W[4], b_g], [b_ck])
                if out_ckv_fn is not None:
                    stt(W[c4][:, 0:n], W[c4][:, 0:n], gcols[:, 40 + c4:41 + c4], W[4][:, 0:n], ALU.mult, ALU.mult,
                        [b_W[4], b_g], [b_W[c4]])
            if out_ckv_fn is not None:
                out_tokmajor([W[c][:, :] for c in range(4)], [b_W[c] for c in range(4)], 128, n, out_ckv_fn)
            wv, bw, col0 = get_w(4)
            proj(wv, bw, col0, 64, rhs_fn, 32, n, pA, b_pA, rhs_bufs)
            wv, bw, col0 = get_w(5)
            proj(wv, bw, col0, 64, rhs_fn, 32, n, pB, b_pB, rhs_bufs)
            act(Bt[4][0:64, 0:n], pA[0:64, 0:n], AF.Square, [b_pA], [b_B[4]])
            mm(pR[0:64, 0:n], ones_b[0:64, 0:64], Bt[4][0:64, 0:n], True, True, [b_ones, b_B[4]], [b_pR])
            rstd_from(pR[0:64, 0:n], b_pR, DR, W[4][0:64, 0:n], b_W[4], W[4][0:64, 0:n], b_W[4], 64)
            stt(W[0][0:64, 0:n], pA[0:64, 0:n], gcols[0:64, 48:49], RC[0:64, 0:n], ALU.mult, ALU.mult, [b_pA, bRC, b_g], [b_W[0]])
            stt(W[1][0:64, 0:n], pB[0:64, 0:n], gcols[0:64, 49:50], RS[0:64, 0:n], ALU.mult, ALU.mult, [b_pB, bRS, b_g], [b_W[1]])
            tt(W[0][0:64, 0:n], W[0][0:64, 0:n], W[1][0:64, 0:n], ALU.add, [b_W[1]], [b_W[0]])
            tt(krT[0:64, 0:n], W[0][0:64, 0:n], W[4][0:64, 0:n], ALU.mult, [b_W[0], b_W[4]], [b_krT])
            if out_kr_fn is not None:
                tt(W[1][0:64, 0:n], W[0][0:64, 0:n], W[4][0:64, 0:n], ALU.mult, [b_W[0], b_W[4]], [b_W[1]])
                out_tokmajor([W[1][:, :]], [b_W[1]], 64, n, out_kr_fn)

        def kv_block(ck, b_ck, n, kt_dst_fn, v_dst_fn):
            accs3 = [(pA, b_pA), (pB, b_pB), (pO, b_pO)]
            wrs = [(W[2], b_W[2]), (W[3], b_W[3]), (W[7], b_W[7])]
            hold = {}

            def k_stage1(h):
                P, bP = accs3[h % 3]
                for c in range(4):
                    mm(P[:, 0:n], wk_v[:, c, h * 128:(h + 1) * 128], ck[:, c, 0:n], c == 0, c == 3, [b_slot[0], b_slot[1], b_ck], [bP])
                act(Bt[h % 3][:, 0:n], P[:, 0:n], AF.Square, [bP], [b_B[h % 3]])

            def k_stage2(h):
                P, bP = accs3[h % 3]
                X, bX = next_pT()
                mm(X[:, 0:n], ones_b[:, :], Bt[h % 3][:, 0:n], True, True, [b_ones, b_B[h % 3]], [bX])
                Wr, bWr = wrs[h % 3]
                rstd_from(X[:, 0:n], bX, 128, Wr[:, 0:n], bWr, Wr[:, 0:n], bWr, 128)
                ki = h % 2
                stt(KTt[ki][:, 0:n], P[:, 0:n], gcols[:, 45:46], Wr[:, 0:n], ALU.mult, ALU.mult, [bP, bWr, b_g], [b_KTt[ki]])
                T.dma("pool", kt_dst_fn(h), KTt[ki][:, 0:n], [b_KTt[ki]], [b_scr])

            def v_tile(t0):
                ntok = min(128, n - t0)
                vi = st["vt"] = st.get("vt", 0) ^ 1
                Vt, bVt = Vts[vi]
                for hg in range(4):
                    si_ = st["pS"] = st["pS"] ^ 1
                    P, bP = pS[si_], b_pS[si_]
                    for c in range(4):
                        mm(P[0:ntok, :], ck[:, c, t0:t0 + ntok], wv_v[:, c, hg * 512:(hg + 1) * 512], c == 0, c == 3,
                           [b_slot[2], b_slot[3], b_ck], [bP])
                    cp(Vt[0:ntok, hg * 512:(hg + 1) * 512], P[0:ntok, :], [bP], bVt, eng="dve")
                T.dma("pool", v_dst_fn(t0 // 128, ntok), Vt[0:ntok, :].rearrange("p (h d) -> p h d", d=128), bVt, [b_scr])

            vt_list = list(range(0, n, 128))
            k_stage1(0)
            for h in range(NH):
                if h + 1 < NH:
                    k_stage1(h + 1)
                k_stage2(h)
                if h % 4 == 3 and vt_list:
                    v_tile(vt_list.pop(0))
            while vt_list:
                v_tile(vt_list.pop(0))

        T.dma("pool", wkv_v[:, :, 0:512], w_in[:, OFF_KV:OFF_KV + 512].rearrange("(c p) e -> p c e", p=128), (), [b_RM])
        T.dma("pool", wkr_v[:, :, 0:64], w_in[:, OFF_KR:OFF_KR + 64].rearrange("(c p) e -> p c e", p=128), (), [b_cq])
        T.dma("pool", wkr_v[:, :, 64:96], w_in[:, OFF_KR + 32:OFF_KR + 64].rearrange("(c p) e -> p c e", p=128), (), [b_cq])
        T.dma("pool", wkr_v[:, :, 96:128], w_in[:, OFF_KR:OFF_KR + 32].rearrange("(c p) e -> p c e", p=128), (), [b_cq])
        for c in range(4):
            T.dma("pool", wk_v[:, c, :].rearrange("p (h d) -> p h d", d=128), w_ukv[c * 128:(c + 1) * 128, :, 0:128], (), [b_slot[0], b_slot[1]])
            T.dma("pool", wv_v[:, c, :].rearrange("p (h d) -> p h d", d=128), w_ukv[c * 128:(c + 1) * 128, :, 128:256], (), [b_slot[2], b_slot[3]])

        def win_idx(col0):
            return col0 // 128 if col0 < OFF_KR else 13 + (col0 - OFF_ZA) // 128
        rr = lambda ap: ap.rearrange("(c p) e -> p c e", p=128)
        conv = []
        for col0 in list(range(0, OFF_KR, 128)) + list(range(OFF_ZA, DIN, 128)):
            conv.append((WIN[win_idx(col0)], rr(w_in[:, col0:col0 + 128])))
        conv.append((WIN[12, :, :, 0:64], rr(w_in[:, OFF_KR:OFF_KR + 64])))
        conv.append((WIN[12, :, :, 64:96], rr(w_in[:, OFF_KR + 32:OFF_KR + 64])))
        conv.append((WIN[12, :, :, 96:128], rr(w_in[:, OFF_KR:OFF_KR + 32])))
        for h in range(NH):
            conv.append((WQ[h, :, :, 0:192], rr(w_uq[:, h, :])))
            conv.append((WQ[h, :, :, 192:224], rr(w_uq[:, h, 160:192])))
            conv.append((WQ[h, :, :, 224:256], rr(w_uq[:, h, 128:160])))
        for g in range(4):
            conv.append((WP[g], rr(w_pool[g, :, :])))
        for dc in range(32):
            conv.append((WOUT[dc], rr(w_out[:, dc * 128:(dc + 1) * 128])))
        conv_pos = [0]

        def conv_some(k):
            while k > 0 and conv_pos[0] < len(conv):
                d_, s_ = conv[conv_pos[0]]
                T.dma("pool", d_, s_, (), [b_conv])
                conv_pos[0] += 1
                k -= 1

        def get_w_res(j):
            if j < 4:
                return wkv_v, b_RM, j * 128
            return wkr_v, b_cq, (j - 4) * 64

        pend = x_front(xseq[0:128, :], 128)
        for blk in range(16):
            conv_some(6)
            for t4 in range(4):
                r0 = blk * 512 + t4 * 128
                cur = pend
                if r0 + 128 < SEQ:
                    pend = x_front(xseq[r0 + 128:r0 + 256, :], 128)
                x_back(cur, 128, hT[:, :, t4 * 128:(t4 + 1) * 128], b_hT)
            ci = st["ckv"] = st["ckv"] ^ 1
            latents(get_w_res, lambda k: hT[:, k, 0:512], [b_hT], 512,
                    ropeA[0, :, blk * 512:(blk + 1) * 512], ropeA[1, :, blk * 512:(blk + 1) * 512], ckvT[ci], b_ckvT[ci])
            T.dma("pool", krp[:, blk * 512:(blk + 1) * 512], krT[:, :], [b_krT], [b_scr])
            kv_block(ckvT[ci], b_ckvT[ci], 512,
                     lambda h, blk=blk: KTp[h, :, blk * 512:(blk + 1) * 512],
                     lambda t4, ntok, blk=blk: Vp[:, blk, 0:ntok, t4, :].rearrange("h p d -> p h d"))
            MARKS["A%d" % blk] = T.n_emit

        for s in range(2):
            for blk in range(8):
                conv_some(6)
                k = st["xs"] = st["xs"] ^ 1
                X = xs[k]; bX = b_xs[k]
                Xv = X[:, 0:2048].rearrange("p (t f) -> p t f", f=512)
                T.dma("sp", Xv, cckv[s, blk * 512:(blk + 1) * 512, :].rearrange("(t p) f -> p t f", p=128), (), [bX])
                Kv = X[:, 2048:2304].rearrange("p (t f) -> p t f", f=64)
                T.dma("sp", Kv, ckr[s, blk * 512:(blk + 1) * 512, :].rearrange("(t p) f -> p t f", p=128), (), [bX])
                ci = st["ckv"] = st["ckv"] ^ 1
                for t4 in range(4):
                    P, bP = next_pT()
                    for c in range(4):
                        tr(P[:, c * 128:(c + 1) * 128], Xv[:, t4, c * 128:(c + 1) * 128], ident_f[:, :], [bX, b_ident], [bP], c == 3)
                    cp(ckvT[ci][:, :, t4 * 128:(t4 + 1) * 128], P[:, :].rearrange("p (c t) -> p c t", t=128), [bP], [b_ckvT[ci]])
                P, bP = next_pT()
                for t4 in range(4):
                    tr(P[0:64, t4 * 128:(t4 + 1) * 128], Kv[:, t4, :], ident_f[:, :], [bX, b_ident], [bP], t4 == 3)
                cp(krT[:, :], P[0:64, :], [bP], [b_krT])
                T.dma("pool", krs[s, :, blk * 512:(blk + 1) * 512], krT[:, :], [b_krT], [b_scr])
                kv_block(ckvT[ci], b_ckvT[ci], 512,
                         lambda h, blk=blk, s=s: KTs[s, h, :, blk * 512:(blk + 1) * 512],
                         lambda t4, ntok, blk=blk, s=s: Vs[s, :, blk, 0:ntok, t4, :].rearrange("h p d -> p h d"))
                MARKS["SC%d_%d" % (s, blk)] = T.n_emit

        conv_some(10 ** 6)
        x_to_hT(xsam[:, :], 128, hT[:, :, 0:128], b_hT)
        ci = st["ckv"] = st["ckv"] ^ 1
        latents(get_w_res, lambda k: hT[:, k, 0:128], [b_hT], 128, ropeS[0, :, :], ropeS[1, :, :], ckvT[ci], b_ckvT[ci],
                out_ckv_fn=lambda t0, ntok: sckv[t0:t0 + ntok, :], out_kr_fn=lambda t0, ntok: skr[t0:t0 + ntok, :])
        for s in range(2):
            T.dma("pool", krs[s, :, 4096:4160], krT[:, s * 64:(s + 1) * 64], [b_krT], [b_scr])
        for h in range(NH):
            P, bP = next_acc()
            for c in range(4):
                mm(P[:, 0:128], wk_v[:, c, h * 128:(h + 1) * 128], ckvT[ci][:, c, 0:128], c == 0, c == 3, [b_slot[0], b_slot[1], b_ckvT[ci]], [bP])
            act(Bt[0][:, 0:128], P[:, 0:128], AF.Square, [bP], [b_B[0]])
            X, bX = next_pT()
            mm(X[:, 0:128], ones_b[:, :], Bt[0][:, 0:128], True, True, [b_ones, b_B[0]], [bX])
            rstd_from(X[:, 0:128], bX, 128, W[2][:, 0:128], b_W[2], W[2][:, 0:128], b_W[2], 128)
            ki = st["ktt"] = st["ktt"] ^ 1
            stt(KTt[ki][:, 0:128], P[:, 0:128], gcols[:, 45:46], W[2][:, 0:128], ALU.mult, ALU.mult, [bP, b_W[2], b_g], [b_KTt[ki]])
            for s in range(2):
                T.dma("pool", KTs[s, h, :, 4096:4160], KTt[ki][:, s * 64:(s + 1) * 64], [b_KTt[ki]], [b_scr])
        for s in range(2):
            for hg in range(4):
                P, bP = next_acc()
                for c in range(4):
                    mm(P[0:64, :], ckvT[ci][:, c, s * 64:(s + 1) * 64], wv_v[:, c, hg * 512:(hg + 1) * 512], c == 0, c == 3,
                       [b_slot[2], b_slot[3], b_ckvT[ci]], [bP])
                cp(Vts[s][0][0:64, hg * 512:(hg + 1) * 512], P[0:64, :], [bP], Vts[s][1])
            T.dma("pool", Vs[s, :, 8, 0:64, 0, :].rearrange("h p d -> p h d"), Vts[s][0][0:64, :].rearrange("p (h d) -> p h d", d=128), Vts[s][1], [b_scr])

        MARKS["SN"] = T.n_emit
        memset(UH[:, :, :, :], 0.0, [b_UH])
        for s in range(2):
            k = st["xs"] = st["xs"] ^ 1
            X = xs[k]; bX = b_xs[k]
            T.dma("sp", X[0:15, 0:2048], spool[s, :, :], (), [bX])
            for c4 in range(4):
                P, bP = next_pT()
                for q in range(4):
                    c = c4 * 4 + q
                    tr(P[:, q * 16:q * 16 + 15], X[0:15, c * 128:(c + 1) * 128], ident_f[0:15, 0:15], [bX, b_ident], [bP], q == 3)
                cp(UH[:, c4 * 4:(c4 + 1) * 4, s, 1:16], P[:, 0:64].rearrange("p (c t) -> p c t", t=16)[:, :, 0:15], [bP], [b_UH])

        MARKS["UH"] = T.n_emit
        def wslice(col0, width):
            assert width == 128
            return stream_chunk(WIN[win_idx(col0)], 32, 128)

        def own_block(kind, i):
            if kind == "p":
                n, nseq, nt = 512, 1, 512
            else:
                n, nseq, nt = 128, 2, 64
            L = 16 + nt
            rhs = lambda k: hT[:, k, 0:n]
            if kind == "p":
                kh = x_front(xown[i, 0:16, :], 16)
                pend_o = x_front(xown[i, 16:144, :], 128)
                x_back(kh, 16, hTh[:, :, 0:16], b_hTh)
                for t4 in range(4):
                    cur = pend_o
                    if t4 + 1 < 4:
                        pend_o = x_front(xown[i, 16 + (t4 + 1) * 128:16 + (t4 + 2) * 128, :], 128)
                    x_back(cur, 128, hT[:, :, t4 * 128:(t4 + 1) * 128], b_hT)
                kr_w = [None]

                def get_w_str(j):
                    if j < 4:
                        v, b = wslice(OFF_KV + j * 128, 128)
                        return v, b, 0
                    if j == 4:
                        v, b = stream_chunk(WIN[12], 32, 128)
                        kr_w[0] = (v, b)
                        return v, b, 0
                    v, b = kr_w[0]
                    return v, b, 64
                ci = st["ckv"] = st["ckv"] ^ 1
                latents(get_w_str, rhs, [b_hT], n, ropeO[i, 0, :, :], ropeO[i, 1, :, :], ckvT[ci], b_ckvT[ci],
                        out_ckv_fn=lambda t0, ntok: ockv[i, t0:t0 + ntok, :], out_kr_fn=lambda t0, ntok: okr[i, t0:t0 + ntok, :])
            MARKS["B1_%s%d" % (kind, i)] = T.n_emit
            rawt = [W[0], W[1], W[2], W[3], PT[0], PT[1], W[5], W[6]]
            rawb = [b_W[0], b_W[1], b_W[2], b_W[3], b_PT[0], b_PT[1], b_W[5], b_W[6]]
            for c8 in range(8):
                v, b = wslice(OFF_Q + c8 * 128, 128)
                P, bP = next_acc()
                proj(v, b, 0, 128, rhs, 32, n, P, bP, [b_hT])
                cp(rawt[c8][:, 0:n], P[:, 0:n], [bP], [rawb[c8]], eng="dve")
                act(Bt[4 + c8 % 2][:, 0:n], P[:, 0:n], AF.Square, [bP], [b_B[4 + c8 % 2]])
                if c8 > 0:
                    mm(pR[:, 0:n], ones_b[:, :], Bt[4 + (c8 - 1) % 2][:, 0:n], c8 == 1, False, [b_ones, b_B[4 + (c8 - 1) % 2]], [b_pR], False)
            mm(pR[:, 0:n], ones_b[:, :], Bt[5][:, 0:n], False, True, [b_ones, b_B[5]], [b_pR], True)
            rstd_from(pR[:, 0:n], b_pR, QL, W[4][:, 0:n], b_W[4], W[4][:, 0:n], b_W[4], 128)
            for c8 in range(8):
                stt(cqnT[:, c8, 0:n], rawt[c8][:, 0:n], gcols[:, 32 + c8:33 + c8], W[4][:, 0:n], ALU.mult, ALU.mult,
                    [rawb[c8], b_W[4], b_g], [b_cq])

            def silu_from(P, bP, Wg, bWg):
                act(PT[1][:, 0:n], P[:, 0:n], AF.Tanh, [bP], [b_PT[1]], scale=0.5)
                stt(Wg[:, 0:n], PT[1][:, 0:n], 1.0, P[:, 0:n], ALU.add, ALU.mult, [b_PT[1], bP], [bWg])

            def gate_from(colbase, Wg, bWg):
                v, b = wslice(colbase, 128)
                P, bP = next_acc()
                proj(v, b, 0, 128, rhs, 32, n, P, bP, [b_hT])
                silu_from(P, bP, Wg, bWg)

            MARKS["B2_%s%d" % (kind, i)] = T.n_emit
            IC, bIC = W[7], b_W[7]
            Uv = U[:, :, 0:nseq * L].rearrange("p c (s l) -> p c s l", l=L)
            pout = xs[0]; bpout = b_xs[0]
            for g in range(4):
                wwin = 2 << g
                if kind == "p":
                    T.dma("sp", IC[:, 0:n], invc_d[i, :, g, :], (), [bIC])
                else:
                    T.dma("sp", IC[:, 0:n], invs_d[:, g, :], (), [bIC])
                for c in range(4):
                    v, b = wslice(OFF_U + (g * 4 + c) * 128, 128)
                    P, bP = next_acc()
                    proj(v, b, 0, 128, rhs, 32, n, P, bP, [b_hT])
                    cp(Uv[:, c, :, 16:L], P[:, 0:n].rearrange("p (s t) -> p s t", t=nt), [bP], [b_U], eng="dve")
                    if kind == "p":
                        P2, bP2 = next_acc()
                        proj(v, b, 0, 128, lambda k: hTh[:, k, 0:16], 32, 16, P2, bP2, [b_hTh])
                        cp(Uv[:, c, 0, 0:16], P2[:, 0:16], [bP2], [b_U], eng="dve")
                    else:
                        cp(Uv[:, c, :, 0:16], UH[:, g * 4 + c, :, :], [b_UH], [b_U], eng="dve")
                for s in range(nseq):
                    P, bP = next_pT()
                    for c in range(4):
                        tr(P[0:15, c * 128:(c + 1) * 128], Uv[:, c, s, L - 15:L], ident_f[:, :], [b_U, b_ident], [bP], c == 3)
                    cp(pout[32 * s:32 * s + 15, g * 512:(g + 1) * 512], P[0:15, :], [bP], [bpout], eng="dve")
                for c in range(4):
                    src = Uv[:, c, :, :]
                    sh = 1
                    cur = src; bcur = b_U
                    pi = 0
                    lo = 0
                    while sh < wwin:
                        lo2 = lo + sh
                        dstt = PT[pi][:, 0:nseq * L].rearrange("p (s l) -> p s l", l=L)
                        tt(dstt[:, :, lo2:L], cur[:, :, lo2:L], cur[:, :, lo:L - sh], ALU.add, [bcur], [b_PT[pi]])
                        cur = dstt; bcur = b_PT[pi]
                        pi ^= 1
                        lo = lo2
                        sh *= 2
                    tmp = PT[pi][:, 0:n].rearrange("p (s t) -> p s t", t=nt)
                    tt(tmp, cur[:, :, 16:L], IC[:, 0:n].rearrange("p (s t) -> p s t", t=nt), ALU.mult, [bcur, bIC], [b_PT[pi]])
                    tt(Bt[c][:, 0:n].rearrange("p (s t) -> p s t", t=nt), tmp, src[:, :, 16:L], ALU.subtract, [b_PT[pi], b_U], [b_B[c]])
                wp = wpool_sb
                T.dma("sp", wp[:, :, :], WP[g], [b_conv], [b_wpool])
                for ec in range(4):
                    gate_from(OFF_ZP + (g * 4 + ec) * 128, W[4], b_W[4])
                    for c in range(4):
                        mm(pO[:, 0:n], wp[:, c, ec * 128:(ec + 1) * 128], Bt[c][:, 0:n], c == 0, c == 3, [b_wpool, b_B[c]], [b_pO])
                    stt(W[4][:, 0:n], W[4][:, 0:n], 0.5, pO[:, 0:n], ALU.mult, ALU.mult, [b_pO], [b_W[4]])
                    ts(mixT[:, 16 + g * 4 + ec, 0:n], W[4][:, 0:n], gcols[:, 50 + g * 4 + ec:51 + g * 4 + ec], 1.0, ALU.mult, ALU.mult,
                       [b_W[4], b_g], [b_RM])
            for s in range(nseq):
                dst = opool[i, :, :] if kind == "p" else spo[s, :, :]
                T.dma("pool", dst, pout[32 * s:32 * s + 15, 0:2048], [bpout], ())

            MARKS["B3_%s%d" % (kind, i)] = T.n_emit
            if kind == "p":
                E = 2048 * (i + 1)
                seqs = [(0, 512, E, None)]
            else:
                seqs = [(0, 64, 4160, 0), (64, 64, 4160, 1)]
            RC, bRC = W[5], b_W[5]
            RS, bRS = W[6], b_W[6]
            if kind == "p":
                T.dma("sp", RC[0:64, 0:n], ropeO[i, 0, :, :], (), [bRC])
                T.dma("sp", RS[0:64, 0:n], ropeO[i, 1, :, :], (), [bRS])
            else:
                T.dma("sp", RC[0:64, 0:n], ropeS[0, :, :], (), [bRC])
                T.dma("sp", RS[0:64, 0:n], ropeS[1, :, :], (), [bRS])
            qbuf = [(QnT, b_Qn, QrT, b_Qr, W[7], b_W[7]), (QnT2, b_Qn2, QrT2, b_Qr2, W[4], b_W[4])]

            def prep_stages(h):
                Qn_, bQn_, Qr_, bQr_, Wg, bWg = qbuf[h % 2]
                stt_ = {}
                qr = lambda kk: cqnT[:, kk, 0:n]

                def s1():
                    v, b = wslice(OFF_ZA + h * 128, 128)
                    P, bP = next_acc()
                    proj(v, b, 0, 128, rhs, 32, n, P, bP, [b_hT])
                    stt_["g"] = (P, bP)

                def s2():
                    P, bP = stt_["g"]
                    silu_from(P, bP, Wg, bWg)

                def s3():
                    k = st["slot"] = (st["slot"] + 1) % 4
                    wq = slots[k][:, 0:8 * 256].rearrange("p (c e) -> p c e", e=256)
                    T.dma("sp", wq, WQ[h], [b_conv], [b_slot[k]])
                    P, bP = next_acc()
                    proj(wq, b_slot[k], 0, 128, qr, 8, n, P, bP, [b_cq])
                    P1, bP1 = next_acc()
                    proj(wq, b_slot[k], 128, 64, qr, 8, n, P1, bP1, [b_cq])
                    stt_["q"] = (wq, b_slot[k], P, bP, P1, bP1)

                def s4():
                    wq, bwq, P, bP, P1, bP1 = stt_["q"]
                    act(Bt[2][:, 0:n], P[:, 0:n], AF.Square, [bP], [b_B[2]])
                    cp(W[1][0:64, 0:n], P1[0:64, 0:n], [bP1], [b_W[1]], eng="dve")
                    act(Bt[3][0:64, 0:n], P1[0:64, 0:n], AF.Square, [bP1], [b_B[3]])
                    P2, bP2 = next_pT()
                    proj(wq, bwq, 192, 64, qr, 8, n, P2, bP2, [b_cq])
                    stt_["p2"] = (P2, bP2)

                def s5():
                    wq, bwq, P, bP, P1, bP1 = stt_["q"]
                    P2, bP2 = stt_["p2"]
                    X, bX = next_pT()
                    mm(X[:, 0:n], ones_b[:, :], Bt[2][:, 0:n], True, True, [b_ones, b_B[2]], [bX])
                    rstd_from(X[:, 0:n], bX, 128, W[0][:, 0:n], b_W[0], W[0][:, 0:n], b_W[0], 128)
                    stt(Qn_[:, 0:n], P[:, 0:n], gcols[:, 44:45], W[0][:, 0:n], ALU.mult, ALU.mult, [bP, b_W[0], b_g], [bQn_])
                    stt(W[2][0:64, 0:n], P2[0:64, 0:n], gcols[0:64, 47:48], RS[0:64, 0:n], ALU.mult, ALU.mult, [bP2, bRS, b_g], [b_W[2]])

                def s6():
                    X, bX = next_pT()
                    mm(X[0:64, 0:n], ones_b[0:64, 0:64], Bt[3][0:64, 0:n], True, True, [b_ones, b_B[3]], [bX])
                    rstd_from(X[0:64, 0:n], bX, DR, W[0][0:64, 0:n], b_W[0], W[0][0:64, 0:n], b_W[0], 64)
                    stt(W[1][0:64, 0:n], W[1][0:64, 0:n], gcols[0:64, 46:47], RC[0:64, 0:n], ALU.mult, ALU.mult, [bRC, b_g], [b_W[1]])
                    tt(W[1][0:64, 0:n], W[1][0:64, 0:n], W[2][0:64, 0:n], ALU.add, [b_W[2]], [b_W[1]])
                    tt(Qr_[0:64, 0:n], W[1][0:64, 0:n], W[0][0:64, 0:n], ALU.mult, [b_W[1], b_W[0]], [bQr_])

                return [s1, s2, s3, s4, s5, s6]

            def prep(h):
                for f_ in prep_stages(h):
                    f_()

            def load_chunk(h, sidx, S, ci, ntile):
                kc = st["kvc"] = st["kvc"] ^ 1
                bk = b_kvc[kc]
                k0 = ci * 1024
                nkc = min(1024, S - k0)
                if sidx is None:
                    nch = ntile // 8
                    eoff = 1024 if ci == nch - 2 else (2048 if ci == nch - 1 else 0)
                    T.dma("sp", KTc[kc][:, 0:nkc], KTp[h, :, k0:k0 + nkc], [b_scr], [bk])
                    T.dma("sp", krc[kc][0:64, 0:nkc], krp[:, k0:k0 + nkc], [b_scr], [bk])
                    T.dma("sp", krc[kc][64:96, 0:nkc], EKb[:, eoff:eoff + nkc], [b_ek], [bk])
                    T.dma("sp", Vc[kc][:, :, :, :], Vp[h, k0 // 512:k0 // 512 + 2, :, :, :].rearrange("b p t d -> p b t d"), [b_scr], [b_vc[kc]])
                else:
                    T.dma("sp", KTc[kc][:, 0:nkc], KTs[sidx, h, :, k0:k0 + nkc], [b_scr], [bk])
                    T.dma("sp", krc[kc][0:64, 0:nkc], krs[sidx, :, k0:k0 + nkc], [b_scr], [bk])
                    T.dma("sp", krc[kc][64:96, 0:nkc], EKb[:, 0:nkc], [b_ek], [bk])
                    if nkc == 1024:
                        T.dma("sp", Vc[kc][:, :, :, :], Vs[sidx, h, k0 // 512:k0 // 512 + 2, :, :, :].rearrange("b p t d -> p b t d"), [b_scr], [b_vc[kc]])
                    else:
                        T.dma("sp", Vc[kc][0:nkc, 0, 0, :], Vs[sidx, h, k0 // 512, 0:nkc, 0, :], [b_scr], [b_vc[kc]])
                return kc

            chunk_slot = {}

            def attention(h, extra):
                Qn_, bQn_, Qr_, bQr_, Wg, bWg = qbuf[h % 2]
                tiles = []
                for (q0, nq, S, sidx) in seqs:
                    ntile = (S + 127) // 128
                    for t in range(ntile):
                        tiles.append((q0, nq, S, sidx, ntile, t))
                def ensure_chunk(q0, nq, S, sidx, ntile, ci, hh=None):
                    hh = h if hh is None else hh
                    key = (hh, sidx, ci)
                    if key not in chunk_slot and ci * 1024 < S and hh < NH:
                        chunk_slot[key] = load_chunk(hh, sidx, S, ci, ntile)

                groups = []
                for tl_ in tiles:
                    q0, nq, S, sidx, ntile, t = tl_
                    gsz = 512 // nq
                    full = min(128, S - t * 128) == 128
                    if groups and full and len(groups[-1]) < gsz and groups[-1][0][3] == sidx and groups[-1][0][5] // 8 == t // 8 \
                            and min(128, groups[-1][0][2] - groups[-1][0][5] * 128) == 128:
                        groups[-1].append(tl_)
                    else:
                        groups.append([tl_])
                info = {}

                def S_stage(j):
                    si = st["pS"] = st["pS"] ^ 1
                    lst = []
                    for gi, (q0, nq, S, sidx, ntile, t) in enumerate(groups[j]):
                        ci = t // 8
                        ensure_chunk(q0, nq, S, sidx, ntile, ci)
                        if len(groups[j]) == 1 and t % 8 == 1:
                            if (ci + 1) * 1024 < S:
                                ensure_chunk(q0, nq, S, sidx, ntile, ci + 1)
                            elif len(seqs) == 1:
                                ensure_chunk(q0, nq, S, sidx, ntile, 0, hh=h + 1)
                        kc = chunk_slot[(h, sidx, ci)]
                        bk = b_kvc[kc]
                        nk = min(128, S - t * 128)
                        tl = t % 8
                        last = gi == len(groups[j]) - 1
                        mm(pS[si][0:nk, gi * nq:(gi + 1) * nq], KTc[kc][:, tl * 128:tl * 128 + nk], Qn_[:, q0:q0 + nq], True, False, [bk, bQn_], [b_pS[si]], False)
                        mm(pS[si][0:nk, gi * nq:(gi + 1) * nq], krc[kc][:, tl * 128:tl * 128 + nk], Qr_[:, q0:q0 + nq], False, True, [bk, bQr_], [b_pS[si]], last)
                        lst.append((kc, nk, tl))
                    info[j] = (si, lst)

                def E_stage(j):
                    si, lst = info[j]
                    nq = groups[j][0][1]
                    nk = lst[0][1]
                    w = len(lst) * nq
                    bi = 4 + (j % 2)
                    act(Bt[bi][0:nk, 0:w], pS[si][0:nk, 0:w], AF.Exp, [b_pS[si]], [b_B[bi]], scale=SCALE)

                def PV_stage(j):
                    si, lst = info[j]
                    bi = 4 + (j % 2)
                    for gi, (q0, nq, S, sidx, ntile, t) in enumerate(groups[j]):
                        kc, nk, tl = lst[gi]
                        last = gi == len(groups[j]) - 1
                        if dve_sum:
                            mm(pO[:, q0:q0 + nq], Vc[kc][0:nk, tl // 4, tl % 4, :], Bt[bi][0:nk, gi * nq:(gi + 1) * nq], t == 0, t == ntile - 1, [b_vc[kc], b_B[bi]], [b_pO], True)
                            if t == 0:
                                T.op("dve", lambda e, bi=bi: e.tensor_copy(out=acc, in_=Bt[bi][:, 0:512]), [b_B[bi]], [b_U])
                            else:
                                tt(acc, acc, Bt[bi][:, 0:512], ALU.add, [b_B[bi]], [b_U])
                        else:
                            mm(pO[:, q0:q0 + nq], Vc[kc][0:nk, tl // 4, tl % 4, :], Bt[bi][0:nk, gi * nq:(gi + 1) * nq], t == 0, t == ntile - 1, [b_vc[kc], b_B[bi]], [b_pO], False)
                            mm(pR[:, q0:q0 + nq], ones_b[0:nk, :], Bt[bi][0:nk, gi * nq:(gi + 1) * nq], t == 0, t == ntile - 1, [b_ones, b_B[bi]], [b_pR], last)

                ng = len(groups)
                if kind == "p":
                    for f_ in extra:
                        f_()
                    when = {}
                else:
                    when = {1 + k: f_ for k, f_ in enumerate(extra)} if ng > len(extra) else {}
                dve_sum = False
                acc = U[:, 0, 0:512]
                S_stage(0)
                for j in range(ng):
                    if j + 1 < ng:
                        S_stage(j + 1)
                    E_stage(j)
                    PV_stage(j)
                    if j in when:
                        when.pop(j)()
                    if len(seqs) > 1:
                        jj = j + 2
                        hh_ = h
                        if jj >= len(groups):
                            jj -= len(groups)
                            hh_ = h + 1
                        q0_, nq_, S_, sidx_, ntile_, t_ = groups[jj][0]
                        ensure_chunk(q0_, nq_, S_, sidx_, ntile_, t_ // 8, hh=hh_)
                if dve_sum:
                    T.op("dve", lambda e: e.tensor_copy(out=KTt[0][:, 0:512], in_=acc), [b_U], [b_KTt[0]])
                    tt(KTt[1][:, 0:512], acc, KTt[0][:, 0:512], ALU.subtract, [b_U, b_KTt[0]], [b_KTt[1]])
                    mm(pR[:, 0:512], ones_b[:, :], KTt[0][:, 0:512], True, False, [b_ones, b_KTt[0]], [b_pR], False)
                    mm(pR[:, 0:512], ones_b[:, :], KTt[1][:, 0:512], False, True, [b_ones, b_KTt[1]], [b_pR], True)
                cp(PT[0][:, 0:n], pO[:, 0:n], [b_pO], [b_PT[0]], eng="act")
                cp(W[3][:, 0:n], pR[:, 0:n], [b_pR], [b_W[3]], eng="dve")
                T.op("dve", lambda e: e.reciprocal(out=W[3][:, 0:n], in_=W[3][:, 0:n]), [], [b_W[3]])
                stt(W[3][:, 0:n], W[3][:, 0:n], 0.5, Wg[:, 0:n], ALU.mult, ALU.mult, [bWg], [b_W[3]])
                tt(mixT[:, h, 0:n], PT[0][:, 0:n], W[3][:, 0:n], ALU.mult, [b_PT[0], b_W[3]], [b_RM])
                for k_ in sorted(when):
                    when[k_]()

            prep(0)
            for h in range(NH):
                attention(h, prep_stages(h + 1) if h + 1 < NH else [])

            MARKS["B4_%s%d" % (kind, i)] = T.n_emit
            ydst = (lambda t0, ntok, c0: yown[i, t0:t0 + ntok, c0:c0 + 512]) if kind == "p" else (lambda t0, ntok, c0: ysam[t0:t0 + ntok, c0:c0 + 512])
            xsrc = (lambda t0, ntok, c0: xown[i, 16 + t0:16 + t0 + ntok, c0:c0 + 512]) if kind == "p" else (lambda t0, ntok, c0: xsam[t0:t0 + ntok, c0:c0 + 512])
            X2s = [(W[4], b_W[4]), (W[5], b_W[5]), (W[6], b_W[6]), (W[7], b_W[7])]
            for dg in range(8):
                for t0 in range(0, n, 128):
                    X2, bX2 = X2s[(t0 // 128) % 4]
                    T.dma("sp", X2[:, :], xsrc(t0, 128, dg * 512), (), [bX2])
                for q in range(4):
                    dc = dg * 4 + q
                    v, b = stream_chunk(WOUT[dc], 32, 128)
                    P, bP = next_acc()
                    proj(v, b, 0, 128, lambda kk: mixT[:, kk, 0:n], 32, n, P, bP, [b_RM])
                    cp(W[q][:, 0:n], P[:, 0:n], [bP], [b_W[q]])
                for t0 in range(0, n, 128):
                    X2, bX2 = X2s[(t0 // 128) % 4]
                    P, bP = next_pT()
                    for q in range(4):
                        tr(P[:, q * 128:(q + 1) * 128], W[q][:, t0:t0 + 128], ident_f[:, :], [b_W[q], b_ident], [bP], q == 3)
                    tt(X2[:, :], X2[:, :], P[:, :], ALU.add, [bP], [bX2])
                    T.dma("pool", ydst(t0, 128, dg * 512), X2[:, :], [bX2], ())

        own_block("s", 0)
        MARKS["B5_s0"] = T.n_emit
        for i in range(4):
            own_block("p", i)
            MARKS["B5_p%d" % i] = T.n_emit

        T.finish()

        with nc.Block() as block:
            @block.tensor
            def _(e):
                T.replay("pe", e)

            @block.scalar
            def _(e):
                T.replay("act", e)

            @block.vector
            def _(e):
                T.replay("dve", e)

            @block.gpsimd
            def _(e):
                T.replay("pool", e)

            @block.sync
            def _(e):
                T.replay("sp", e)
    return nc


_NC = None


def _rope_tables(pos):
    half = DR // 2
    freqs = (np.float32(10000.0) ** (-np.arange(half, dtype=np.float32) / np.float32(half))).astype(np.float32)
    ang = pos.astype(np.float32)[:, None] * freqs[None, :]
    cos = np.cos(ang).astype(np.float32).T
    sin = np.sin(ang).astype(np.float32).T
    C = np.concatenate([cos, cos], 0)
    S = np.concatenate([-sin, sin], 0)
    return np.ascontiguousarray(np.stack([C, S], 0))


def kernel(x_prompt, x_sample, cache_ckv, cache_krope, state_pool, g_norm, w_in, g_q_lat, w_uq, g_qn, g_qr,
           g_kv_lat, g_kr, w_ukv, g_kn, w_pool, pool_scale, w_out):
    global _NC
    if _NC is None:
        _NC = build_program()
    nc = _NC
    in_maps = prepare(x_prompt, x_sample, cache_ckv, cache_krope, state_pool, g_norm, w_in, g_q_lat, w_uq, g_qn, g_qr,
                      g_kv_lat, g_kr, w_ukv, g_kn, w_pool, pool_scale, w_out)
    res = run_bass_kernel_spmd(nc, in_maps, core_ids=list(range(8)))
    return assemble(res.results)


def prepare(x_prompt, x_sample, cache_ckv, cache_krope, state_pool, g_norm, w_in, g_q_lat, w_uq, g_qn, g_qr,
            g_kv_lat, g_kr, w_ukv, g_kn, w_pool, pool_scale, w_out, cores=range(8)):
    f = lambda a: np.ascontiguousarray(np.asarray(a, dtype=np.float32))
    x_prompt = f(x_prompt); x_sample = f(x_sample); cache_ckv = f(cache_ckv); cache_krope = f(cache_krope)
    state_pool = f(state_pool)
    w_in0 = f(w_in)[0]; w_uq0 = f(w_uq)[0]; w_ukv0 = f(w_ukv)[0]; w_pool0 = f(w_pool)[0]; w_out0 = f(w_out)[0]
    gcols = np.zeros((128, 72), np.float32)
    gcols[:, 0:32] = f(g_norm)[0].reshape(32, 128).T
    gcols[:, 32:40] = f(g_q_lat)[0].reshape(8, 128).T
    gcols[:, 40:44] = f(g_kv_lat)[0].reshape(4, 128).T
    gcols[:, 44] = f(g_qn)[0]
    gcols[:, 45] = f(g_kn)[0]
    gqr = f(g_qr)[0]; gkr = f(g_kr)[0]
    gcols[0:64, 46] = gqr
    gcols[0:64, 47] = np.concatenate([gqr[32:], gqr[:32]])
    gcols[0:64, 48] = gkr
    gcols[0:64, 49] = np.concatenate([gkr[32:], gkr[:32]])
    gcols[:, 50:66] = f(pool_scale)[0].reshape(16, 128).T

    ident = np.eye(128, dtype=np.float32)
    ropeA = _rope_tables(np.arange(SEQ))
    rs = _rope_tables(PAST + np.arange(64))
    ropeS = np.ascontiguousarray(np.concatenate([rs, rs], 2))
    EK = np.zeros((32, 3072), np.float32)
    EK[np.arange(2048) // 64, 1024 + np.arange(2048)] = 1.0
    windows = (2, 4, 8, 16)
    invs = np.zeros((128, 4, 128), np.float32)
    for g, w in enumerate(windows):
        invs[:, g, :] = 1.0 / w

    in_maps = []
    for c in cores:
        s, j = c // 4, c % 4
        xown = np.zeros((4, 528, D), np.float32)
        ropeO = np.zeros((4, 2, 64, 512), np.float32)
        invc = np.zeros((4, 128, 4, 512), np.float32)
        for i in range(4):
            b = 4 * i + j
            lo = b * 512 - 16
            if lo >= 0:
                xown[i] = x_prompt[s, lo:lo + 528]
            else:
                xown[i, 16:] = x_prompt[s, 0:512]
            ropeO[i] = ropeA[:, :, b * 512:(b + 1) * 512]
            pos = b * 512 + np.arange(512)
            for g, w in enumerate(windows):
                invc[i, :, g, :] = (1.0 / np.minimum(pos + 1, w).astype(np.float32))[None, :]
        LQ = np.zeros((32, 512), np.float32)
        qchunk = 8 * j + np.arange(512) // 64
        LQ[np.arange(32)[:, None] > qchunk[None, :]] = NEG
        in_maps.append({
            "xseq": x_prompt[s], "xown": xown, "xsam": np.ascontiguousarray(x_sample[2 * c:2 * c + 2].reshape(128, D)),
            "cckv": np.ascontiguousarray(cache_ckv[0, 2 * c:2 * c + 2]), "ckr": np.ascontiguousarray(cache_krope[0, 2 * c:2 * c + 2]),
            "spool": np.ascontiguousarray(state_pool[0, 2 * c:2 * c + 2]),
            "w_in": w_in0, "w_uq": w_uq0, "w_ukv": w_ukv0, "w_pool": w_pool0, "w_out": w_out0,
            "ident": ident, "gcols": gcols, "ropeA": ropeA, "ropeO": ropeO, "ropeS": ropeS, "EK": EK, "LQ": LQ,
            "invc": invc, "invs": invs,
        })
    return in_maps


def assemble(R):
    y_p = np.zeros((2, SEQ, D), np.float32)
    ckv_p = np.zeros((1, 2, SEQ, KVL), np.float32)
    kr_p = np.zeros((1, 2, SEQ, DR), np.float32)
    pool_p = np.zeros((1, 2, 15, 2048), np.float32)
    y_s = np.zeros((16, 64, D), np.float32)
    ckv_s = np.zeros((1, 16, 64, KVL), np.float32)
    kr_s = np.zeros((1, 16, 64, DR), np.float32)
    pool_s = np.zeros((1, 16, 15, 2048), np.float32)
    for c in range(8):
        s, j = c // 4, c % 4
        r = R[c]
        for i in range(4):
            b = 4 * i + j
            y_p[s, b * 512:(b + 1) * 512] = r["yown"][i]
            ckv_p[0, s, b * 512:(b + 1) * 512] = r["ockv"][i]
            kr_p[0, s, b * 512:(b + 1) * 512] = r["okr"][i]
            if b == 15:
                pool_p[0, s] = r["opool"][i]
        y_s[2 * c:2 * c + 2] = r["ysam"].reshape(2, 64, D)
        ckv_s[0, 2 * c:2 * c + 2] = r["sckv"].reshape(2, 64, KVL)
        kr_s[0, 2 * c:2 * c + 2] = r["skr"].reshape(2, 64, DR)
        pool_s[0, 2 * c:2 * c + 2] = r["spo"]
    return (y_p, y_s, ckv_p, kr_p, pool_p, ckv_s, kr_s, pool_s)
```

```python
import contextlib
import numpy as np
import concourse.bass as bass
import concourse.mybir as mybir
from concourse.bass_utils import run_bass_kernel_spmd

F32 = mybir.dt.float32
BF16 = mybir.dt.bfloat16
AF = mybir.ActivationFunctionType
ALU = mybir.AluOpType

D = 4096
SEQ = 8192
NH = 16
QL = 1024
KVL = 512
DR = 64
DIN = 7744
OFF_Q, OFF_KV, OFF_KR, OFF_ZA, OFF_U, OFF_ZP = 0, 1024, 1536, 1600, 3648, 5696
PAST = 4096
EPS = 1e-6
SCALE = 192.0 ** -0.5
NEG = -30000.0
NDSEM = 20
LIMIT = None
MARKS = {}
PE_AT = {}


class Buf:
    __slots__ = ("w", "r", "excl")

    def __init__(self, excl=False):
        self.w = {}
        self.r = {}
        self.excl = excl


class Tracker:
    def __init__(self, nc, es):
        self.nc = nc
        self.streams = {k: [] for k in ("pe", "act", "dve", "pool", "sp")}
        self.sem = {}
        self.cnt = {}
        self.waited = {k: {} for k in self.streams}
        for k in ("pe", "act", "dve"):
            self.sem[k] = es.enter_context(nc.semaphore("s_" + k))
            self.cnt[k] = 0
        self.n_emit = 0
        self.n_pe = 0
        self.limit = LIMIT
        self.dma_i = {"sp": 0, "pool": 0}
        self.dma_last = {"sp": [None] * NDSEM, "pool": [None] * NDSEM}
        for q in ("sp", "pool"):
            for k in range(NDSEM):
                nm = "%sd%d" % (q, k)
                self.sem[nm] = es.enter_context(nc.semaphore("s_" + nm))
                self.cnt[nm] = 0

    def _wait(self, eng, tok):
        if tok is None:
            return
        s, v = tok
        if s == eng and eng == "pe":
            return
        if self.waited[eng].get(s, 0) >= v:
            return
        self.waited[eng][s] = v
        self.streams[eng].append(("wait", s, v))

    def _deps(self, eng, reads, writes):
        for b in reads:
            for s, v in list(b.w.items()):
                self._wait(eng, (s, v))
            if b.excl:
                for s, v in list(b.r.items()):
                    if s != eng:
                        self._wait(eng, (s, v))
        for b in writes:
            for s, v in list(b.w.items()):
                if s.startswith(eng + "d") or s == eng:
                    continue
                self._wait(eng, (s, v))
            for s, v in list(b.r.items()):
                if s == eng:
                    continue
                self._wait(eng, (s, v))

    def _mark(self, tok, reads, writes):
        s, v = tok
        for b in reads:
            if b.r.get(s, 0) < v:
                b.r[s] = v
        for b in writes:
            if s[-1].isdigit() and not b.r and b.w and all(k[-1].isdigit() for k in b.w):
                b.w[s] = v
            else:
                b.w = {s: v}
            b.r = {}

    def op(self, eng, fn, reads=(), writes=(), signal=True):
        self.n_emit += 1
        if self.limit is not None and self.n_emit > self.limit:
            return None
        self._deps(eng, reads, writes)
        if eng == "pe":
            self.n_pe += 1
            PE_AT[self.n_emit] = self.n_pe
        if signal:
            self.cnt[eng] += 1
            tok = (eng, self.cnt[eng])
        else:
            tok = (eng, self.cnt[eng] + 1)
        self.streams[eng].append(("op", fn, signal))
        self._mark(tok, reads, writes)
        return tok

    def dma(self, q, out, in_, reads=(), writes=()):
        self.n_emit += 1
        if self.limit is not None and self.n_emit > self.limit:
            return None
        self._deps(q, reads, writes)
        k = self.dma_i[q] % NDSEM
        self.dma_i[q] += 1
        self._wait(q, self.dma_last[q][k])
        nm = "%sd%d" % (q, k)
        self.cnt[nm] += 16
        tok = (nm, self.cnt[nm])
        self.dma_last[q][k] = tok
        self.streams[q].append(("dma", out, in_, nm))
        self._mark(tok, reads, writes)
        return tok

    def finish(self):
        for q in ("sp", "pool"):
            for k in range(NDSEM):
                self._wait(q, self.dma_last[q][k])

    def replay(self, eng, e):
        for it in self.streams[eng]:
            if it[0] == "wait":
                e.wait_ge(self.sem[it[1]], it[2])
            elif it[0] == "op":
                ins = it[1](e)
                if it[2]:
                    ins.then_inc(self.sem[eng], 1)
            else:
                e.dma_start(out=it[1], in_=it[2]).then_inc(self.sem[it[3]], 16)


def build_program():
    nc = bass.Bass("TRN2", target_bir_lowering=False)

    def din(name, shape, dt=F32):
        return nc.dram_tensor(name, list(shape), dt, kind="ExternalInput").ap()

    def dout(name, shape):
        return nc.dram_tensor(name, list(shape), F32, kind="ExternalOutput").ap()

    def dscr(name, shape):
        return nc.dram_tensor(name, list(shape), BF16, kind="Internal").ap()

    xseq = din("xseq", [SEQ, D])
    xown = din("xown", [4, 528, D])
    xsam = din("xsam", [128, D])
    cckv = din("cckv", [2, PAST, KVL])
    ckr = din("ckr", [2, PAST, DR])
    spool = din("spool", [2, 15, 2048])
    w_in = din("w_in", [D, DIN])
    w_uq = din("w_uq", [QL, NH, 192])
    w_ukv = din("w_ukv", [KVL, NH, 256])
    w_pool = din("w_pool", [4, 512, 512])
    w_out = din("w_out", [D, D])
    ident = din("ident", [128, 128])
    gcols_d = din("gcols", [128, 72])
    ropeA = din("ropeA", [2, 64, SEQ])
    ropeO = din("ropeO", [4, 2, 64, 512])
    ropeS = din("ropeS", [2, 64, 128])
    EK_d = din("EK", [32, 3072])
    LQ_d = din("LQ", [32, 512])
    invc_d = din("invc", [4, 128, 4, 512])
    invs_d = din("invs", [128, 4, 128])

    yown = dout("yown", [4, 512, D])
    ysam = dout("ysam", [128, D])
    ockv = dout("ockv", [4, 512, KVL])
    okr = dout("okr", [4, 512, DR])
    opool = dout("opool", [4, 15, 2048])
    sckv = dout("sckv", [128, KVL])
    skr = dout("skr", [128, DR])
    spo = dout("spo", [2, 15, 2048])

    KTp = dscr("KTp", [NH, 128, SEQ])
    Vp = dscr("Vp", [NH, 16, 128, 4, 128])
    krp = dscr("krp", [64, SEQ])
    KTs = dscr("KTs", [2, NH, 128, 4608])
    Vs = dscr("Vs", [2, NH, 9, 128, 4, 128])
    krs = dscr("krs", [2, 64, 4608])
    WIN = dscr("WIN", [61, 128, 32, 128])
    WOUT = dscr("WOUT", [32, 128, 32, 128])
    WQ = dscr("WQ", [16, 128, 8, 256])
    WP = dscr("WP", [4, 128, 4, 512])

    es = contextlib.ExitStack()
    with es:
        def sb(name, shape, dt=F32):
            return es.enter_context(nc.sbuf_tensor("sb_" + name, list(shape), dt))

        def ps(name, shape, dt=F32):
            return es.enter_context(nc.psum_tensor("ps_" + name, list(shape), dt))

        T = Tracker(nc, es)

        ident_f = sb("ident_f", [128, 128]); b_ident = Buf()
        ones_b = sb("ones_b", [128, 128], BF16); b_ones = Buf()
        eps_t = sb("eps_t", [128, 1]); b_eps = Buf()
        gcols = sb("gcols", [128, 72]); b_g = Buf()
        EKb = sb("EKb", [32, 3072], BF16); b_ek = Buf()
        xs = [sb("xs%d" % k, [128, D]) for k in range(2)]; b_xs = [Buf(), Buf()]
        ssx = [sb("ssx%d" % k, [128, 8]) for k in range(2)]; b_ssx = [Buf(), Buf()]
        hT = sb("hT", [128, 32, 512], BF16); b_hT = Buf()
        hTh = sb("hTh", [128, 32, 16], BF16); b_hTh = Buf()
        RM = sb("RM", [128, 16384], BF16); b_RM = Buf()
        cqnT = sb("cqnT", [128, 8, 512], BF16); b_cq = Buf()
        RW = sb("RW", [128, 16384], BF16)
        b_slot = [Buf() for _ in range(4)]
        KVR = sb("KVR", [128, 6144], BF16)
        KTc = [KVR[:, k * 1024:(k + 1) * 1024] for k in range(2)]
        Vc = [KVR[:, 2048 + k * 1024:2048 + (k + 1) * 1024].rearrange("p (b t d) -> p b t d", t=4, d=128) for k in range(2)]
        krc = [KVR[:, 4096 + k * 1024:4096 + (k + 1) * 1024] for k in range(2)]
        b_kvc = [Buf(), Buf()]
        b_vc = [Buf(), Buf()]
        b_scr = Buf(); b_conv = Buf()
        junk = KVR[:, 0:2048]
        NW = 8
        W = [sb("W%d" % k, [128, 512]) for k in range(NW)]; b_W = [Buf() for _ in range(NW)]
        NB = 6
        Bt = [sb("B%d" % k, [128, 512], BF16) for k in range(NB)]; b_B = [Buf() for _ in range(NB)]
        U = sb("U", [128, 4, 528]); b_U = Buf()
        PT = [sb("PT%d" % k, [128, 528]) for k in range(2)]; b_PT = [Buf(), Buf()]
        UH = sb("UH", [128, 16, 2, 16]); b_UH = Buf()
        ckvT0 = sb("ckvT0", [128, 4, 512], BF16); ckvT = [ckvT0, ckvT0]; b_ck0 = Buf(); b_ckvT = [b_ck0, b_ck0]
        krT = sb("krT", [64, 512], BF16); b_krT = Buf()
        Vts = [(KVR[:, 0:2048], [b_kvc[0], b_kvc[1]]), (KVR[:, 2048:4096], [b_vc[0], b_vc[1]])]
        KTt = [sb("KTt%d" % k, [128, 512], BF16) for k in range(2)]; b_KTt = [Buf(), Buf()]
        wpool_sb = sb("wpool_sb", [128, 4, 512], BF16); b_wpool = Buf()
        QnT = sb("QnT", [128, 512], BF16); b_Qn = Buf()
        QrT = sb("QrT", [128, 512], BF16); b_Qr = Buf()
        QnT2 = sb("QnT2", [128, 512], BF16); b_Qn2 = Buf()
        QrT2 = sb("QrT2", [128, 512], BF16); b_Qr2 = Buf()

        pA = ps("pA", [128, 512]); pB = ps("pB", [128, 512])
        pS = [ps("pS0", [128, 512]), ps("pS1", [128, 512])]
        pO = ps("pO", [128, 512]); pR = ps("pR", [128, 512])
        pT = [ps("pT0", [128, 512]), ps("pT1", [128, 512])]
        b_pA = Buf(True); b_pB = Buf(True); b_pS = [Buf(True), Buf(True)]; b_pO = Buf(True); b_pR = Buf(True); b_pT = [Buf(True), Buf(True)]
        acc = [(pA, b_pA), (pB, b_pB)]
        st = {"acc": 0, "pT": 0, "xs": 0, "slot": 0, "ev": 0, "kvc": 0, "pS": 0, "B": 0, "ktt": 0, "ckv": 0}

        wkv_v = RM[:, 0:32 * 512].rearrange("p (c e) -> p c e", e=512)
        wkr_v = cqnT[:, :, :].rearrange("p a b -> p (a b)").rearrange("p (c e) -> p c e", e=128)
        mixT = RM[:, 0:32 * 512].rearrange("p (c e) -> p c e", e=512)
        wk_v = RW[:, 0:8192].rearrange("p (c e) -> p c e", e=2048)
        wv_v = RW[:, 8192:16384].rearrange("p (c e) -> p c e", e=2048)
        slots = [RW[:, k * 4096:(k + 1) * 4096] for k in range(4)]

        def mm(out, lhsT, rhs, start, stop, reads, writes, signal=None):
            if signal is None:
                signal = stop
            T.op("pe", lambda e: e.matmul(out, lhsT=lhsT, rhs=rhs, start=start, stop=stop), reads, writes, signal)

        def tr(out, in_, idn, reads, writes, signal):
            T.op("pe", lambda e: e.transpose(out=out, in_=in_, identity=idn), reads, writes, signal)

        def act(out, in_, func, reads, writes, scale=1.0, bias=None, accum=None):
            kw = {}
            if bias is not None:
                kw["bias"] = bias
            if accum is not None:
                kw["accum_out"] = accum
            T.op("act", lambda e: e.activation(out=out, in_=in_, func=func, scale=scale, **kw), reads, writes)

        def stt(out, in0, scalar, in1, op0, op1, reads, writes):
            T.op("dve", lambda e: e.scalar_tensor_tensor(out=out, in0=in0, scalar=scalar, in1=in1, op0=op0, op1=op1), reads, writes)

        def tt(out, in0, in1, op, reads, writes):
            T.op("dve", lambda e: e.tensor_tensor(out=out, in0=in0, in1=in1, op=op), reads, writes)

        def ts(out, in0, s1, s2, op0, op1, reads, writes):
            T.op("dve", lambda e: e.tensor_scalar(out=out, in0=in0, scalar1=s1, scalar2=s2, op0=op0, op1=op1), reads, writes)

        def cp(out, in_, reads, writes, eng=None):
            if eng is None:
                st["ev"] ^= 1
                eng = "act" if st["ev"] else "dve"
            if eng == "act":
                act(out, in_, AF.Copy, reads, writes)
            else:
                T.op("dve", lambda e: e.tensor_copy(out=out, in_=in_), reads, writes)

        def memset(ap, val, writes):
            T.op("dve", lambda e: e.memset(ap, val), (), writes)

        def next_acc():
            st["acc"] ^= 1
            return acc[st["acc"]]

        def next_pT():
            st["pT"] ^= 1
            return pT[st["pT"]], b_pT[st["pT"]]

        def rstd_from(psum_ap, b_ps, n_feat, wl, b_wl, wr, b_wr, npart):
            act(wl, psum_ap, AF.Ln, [b_ps, b_eps], [b_wl], scale=1.0 / n_feat, bias=eps_t[0:npart, 0:1])
            act(wr, wl, AF.Exp, [b_wl], [b_wr], scale=-0.5)

        T.dma("sp", ident_f[:, :], ident[:, :], (), [b_ident])
        T.dma("sp", gcols[:, :], gcols_d[:, :], (), [b_g])
        T.dma("pool", EKb[:, :], EK_d[:, :], (), [b_ek])
        memset(QrT[:, :], 0.0, [b_Qr])
        T.dma("pool", QrT[64:96, :], LQ_d[:, :], (), [b_Qr])
        memset(QrT2[:, :], 0.0, [b_Qr2])
        T.dma("pool", QrT2[64:96, :], LQ_d[:, :], (), [b_Qr2])
        memset(KVR[64:128, 4096:6144], 0.0, [b_kvc[0], b_kvc[1]])
        memset(ones_b[:, :], 1.0, [b_ones])
        memset(eps_t[:, :], EPS, [b_eps])
        MARKS["const"] = T.n_emit

        def x_front(x_rows, ntok):
            k = st["xs"] = st["xs"] ^ 1
            X = xs[k]; bX = b_xs[k]; S = ssx[k]; bS = b_ssx[k]
            T.dma("sp", X[0:ntok, :], x_rows, (), [bX])
            memset(S[:, :], 0.0, [bS])
            for hh in range(2):
                act(junk[0:ntok, :], X[0:ntok, hh * 2048:(hh + 1) * 2048], AF.Square, [bX], [b_kvc[0], b_kvc[1], bS],
                    accum=S[0:ntok, hh:hh + 1])
            act(S[0:ntok, 2:3], S[0:ntok, 0:1], AF.Identity, [bS], [bS], bias=S[0:ntok, 1:2])
            act(S[0:ntok, 3:4], S[0:ntok, 2:3], AF.Ln, [bS, b_eps], [bS], scale=1.0 / D, bias=eps_t[0:ntok, 0:1])
            act(S[0:ntok, 4:5], S[0:ntok, 3:4], AF.Exp, [bS], [bS], scale=-0.5)
            for hh in range(2):
                act(X[0:ntok, hh * 2048:(hh + 1) * 2048], X[0:ntok, hh * 2048:(hh + 1) * 2048], AF.Identity, [bS], [bX],
                    scale=S[0:ntok, 4:5])
            return k

        def x_back(k, ntok, dst, b_dst):
            X = xs[k]; bX = b_xs[k]
            for c4 in range(8):
                P, bP = next_pT()
                for q in range(4):
                    c = c4 * 4 + q
                    tr(P[:, q * 128:q * 128 + ntok], X[0:ntok, c * 128:(c + 1) * 128], ident_f[0:ntok, 0:ntok],
                       [bX, b_ident], [bP], q == 3)
                for q in range(4):
                    c = c4 * 4 + q
                    ts(dst[:, c, :], P[:, q * 128:q * 128 + ntok], gcols[:, c:c + 1], 1.0, ALU.mult, ALU.mult,
                       [bP, b_g], [b_dst])

        def x_to_hT(x_rows, ntok, dst, b_dst):
            x_back(x_front(x_rows, ntok), ntok, dst, b_dst)

        def stream_chunk(src_tile, kc, width):
            k = st["slot"] = (st["slot"] + 1) % 4
            v = slots[k][:, 0:kc * width].rearrange("p (c e) -> p c e", e=width)
            T.dma("sp", v, src_tile, [b_conv], [b_slot[k]])
            return v, b_slot[k]

        def proj(wv, bw, col0, M, rhs_fn, kc, n, out_ps, b_out, rhs_bufs):
            for k in range(kc):
                mm(out_ps[0:M, 0:n], wv[:, k, col0:col0 + M], rhs_fn(k), k == 0, k == kc - 1, [bw] + rhs_bufs, [b_out])

        def out_tokmajor(srcs, bsrcs, npart, n, dst_rows_fn):
            for t0 in range(0, n, 128):
                ntok = min(128, n - t0)
                P, bP = next_pT()
                for q, s in enumerate(srcs):
                    tr(P[0:ntok, q * npart:(q + 1) * npart], s[0:npart, t0:t0 + ntok], ident_f[0:npart, 0:npart],
                       bsrcs + [b_ident], [bP], q == len(srcs) - 1)
                wi = st["B"] = (st["B"] + 1) % 2
                Wt, bWt = (W[7], b_W[7]) if wi == 0 else (PT[0], b_PT[0])
                cp(Wt[0:ntok, 0:len(srcs) * npart], P[0:ntok, 0:len(srcs) * npart], [bP], [bWt])
                T.dma("pool", dst_rows_fn(t0, ntok), Wt[0:ntok, 0:len(srcs) * npart], [bWt], ())

        def latents(get_w, rhs_fn, rhs_bufs, n, ropeC_ap, ropeS_ap, ck, b_ck, out_ckv_fn=None, out_kr_fn=None):
            RC, bRC = W[5], b_W[5]
            RS, bRS = W[6], b_W[6]
            T.dma("sp", RC[0:64, 0:n], ropeC_ap, (), [bRC])
            T.dma("sp", RS[0:64, 0:n], ropeS_ap, (), [bRS])
            for c4 in range(4):
                wv, bw, col0 = get_w(c4)
                P, bP = next_acc()
                proj(wv, bw, col0, 128, rhs_fn, 32, n, P, bP, rhs_bufs)
                cp(W[c4][:, 0:n], P[:, 0:n], [bP], [b_W[c4]], eng="dve")
                act(Bt[4 + c4 % 2][:, 0:n], P[:, 0:n], AF.Square, [bP], [b_B[4 + c4 % 2]])
                if c4 > 0:
                    mm(pR[:, 0:n], ones_b[:, :], Bt[4 + (c4 - 1) % 2][:, 0:n], c4 == 1, False, [b_ones, b_B[4 + (c4 - 1) % 2]], [b_pR], False)
            mm(pR[:, 0:n], ones_b[:, :], Bt[5][:, 0:n], False, True, [b_ones, b_B[5]], [b_pR], True)
            rstd_from(pR[:, 0:n], b_pR, KVL, W[4][:, 0:n], b_W[4], W[4][:, 0:n], b_W[4], 128)
            for c4 in range(4):
                stt(ck[:, c4, 0:n], W[c4][:, 0:n], gcols[:, 40 + c4:41 + c4], W[4][:, 0:n], ALU.mult, ALU.mult,
                    [b_W[c4], b_W[4], b_g], [b_ck])
                if out_ckv_fn is not None:
                    stt(W[c4][:, 0:n], W[c4][:, 0:n], gcols[:, 40 + c4:41 + c4], W[4][:, 0:n], ALU.mult, ALU.mult,
                        [b_W[4], b_g], [b_W[c4]])
            if out_ckv_fn is not None:
                out_tokmajor([W[c][:, :] for c in range(4)], [b_W[c] for c in range(4)], 128, n, out_ckv_fn)
            wv, bw, col0 = get_w(4)
            proj(wv, bw, col0, 64, rhs_fn, 32, n, pA, b_pA, rhs_bufs)
            wv, bw, col0 = get_w(5)
            proj(wv, bw, col0, 64, rhs_fn, 32, n, pB, b_pB, rhs_bufs)
            act(Bt[4][0:64, 0:n], pA[0:64, 0:n], AF.Square, [b_pA], [b_B[4]])
            mm(pR[0:64, 0:n], ones_b[0:64, 0:64], Bt[4][0:64, 0:n], True, True, [b_ones, b_B[4]], [b_pR])
            rstd_from(pR[0:64, 0:n], b_pR, DR, W[4][0:64, 0:n], b_W[4], W[4][0:64, 0:n], b_W[4], 64)
            stt(W[0][0:64, 0:n], pA[0:64, 0:n], gcols[0:64, 48:49], RC[0:64, 0:n], ALU.mult, ALU.mult, [b_pA, bRC, b_g], [b_W[0]])
            stt(W[1][0:64, 0:n], pB[0:64, 0:n], gcols[0:64, 49:50], RS[0:64, 0:n], ALU.mult, ALU.mult, [b_pB, bRS, b_g], [b_W[1]])
            tt(W[0][0:64, 0:n], W[0][0:64, 0:n], W[1][0:64, 0:n], ALU.add, [b_W[1]], [b_W[0]])
            tt(krT[0:64, 0:n], W[0][0:64, 0:n], W[4][0:64, 0:n], ALU.mult, [b_W[0], b_W[4]], [b_krT])
            if out_kr_fn is not None:
                tt(W[1][0:64, 0:n], W[0][0:64, 0:n], W[4][0:64, 0:n], ALU.mult, [b_W[0], b_W[4]], [b_W[1]])
                out_tokmajor([W[1][:, :]], [b_W[1]], 64, n, out_kr_fn)

        def kv_block(ck, b_ck, n, kt_dst_fn, v_dst_fn, hooks=()):
            hooks = list(hooks)
            accs3 = [(pA, b_pA), (pB, b_pB), (pO, b_pO)]
            wrs = [(W[2], b_W[2]), (W[3], b_W[3]), (W[7], b_W[7])]
            hold = {}

            def k_stage1(h):
                P, bP = accs3[h % 3]
                for c in range(4):
                    mm(P[:, 0:n], wk_v[:, c, h * 128:(h + 1) * 128], ck[:, c, 0:n], c == 0, c == 3, [b_slot[0], b_slot[1], b_ck], [bP])
                act(Bt[h % 3][:, 0:n], P[:, 0:n], AF.Square, [bP], [b_B[h % 3]])

            def k_stage2(h):
                P, bP = accs3[h % 3]
                X, bX = next_pT()
                mm(X[:, 0:n], ones_b[:, :], Bt[h % 3][:, 0:n], True, True, [b_ones, b_B[h % 3]], [bX])
                Wr, bWr = wrs[h % 3]
                rstd_from(X[:, 0:n], bX, 128, Wr[:, 0:n], bWr, Wr[:, 0:n], bWr, 128)
                ki = h % 2
                stt(KTt[ki][:, 0:n], P[:, 0:n], gcols[:, 45:46], Wr[:, 0:n], ALU.mult, ALU.mult, [bP, bWr, b_g], [b_KTt[ki]])
                T.dma("pool", kt_dst_fn(h), KTt[ki][:, 0:n], [b_KTt[ki]], [b_scr])

            def v_tile(t0):
                ntok = min(128, n - t0)
                vi = st["vt"] = st.get("vt", 0) ^ 1
                Vt, bVt = Vts[vi]
                for hg in range(4):
                    si_ = st["pS"] = st["pS"] ^ 1
                    P, bP = pS[si_], b_pS[si_]
                    for c in range(4):
                        mm(P[0:ntok, :], ck[:, c, t0:t0 + ntok], wv_v[:, c, hg * 512:(hg + 1) * 512], c == 0, c == 3,
                           [b_slot[2], b_slot[3], b_ck], [bP])
                    cp(Vt[0:ntok, hg * 512:(hg + 1) * 512], P[0:ntok, :], [bP], bVt, eng="dve")
                T.dma("pool", v_dst_fn(t0 // 128, ntok), Vt[0:ntok, :].rearrange("p (h d) -> p h d", d=128), bVt, [b_scr])

            vt_list = list(range(0, n, 128))
            k_stage1(0)
            for h in range(NH):
                if h + 1 < NH:
                    k_stage1(h + 1)
                k_stage2(h)
                if h % 4 == 3 and vt_list:
                    v_tile(vt_list.pop(0))
                if h % 4 == 3 and hooks:
                    hooks.pop(0)()
            while vt_list:
                v_tile(vt_list.pop(0))
            while hooks:
                hooks.pop(0)()

        T.dma("pool", wkv_v[:, :, 0:512], w_in[:, OFF_KV:OFF_KV + 512].rearrange("(c p) e -> p c e", p=128), (), [b_RM])
        T.dma("pool", wkr_v[:, :, 0:64], w_in[:, OFF_KR:OFF_KR + 64].rearrange("(c p) e -> p c e", p=128), (), [b_cq])
        T.dma("pool", wkr_v[:, :, 64:96], w_in[:, OFF_KR + 32:OFF_KR + 64].rearrange("(c p) e -> p c e", p=128), (), [b_cq])
        T.dma("pool", wkr_v[:, :, 96:128], w_in[:, OFF_KR:OFF_KR + 32].rearrange("(c p) e -> p c e", p=128), (), [b_cq])
        for c in range(4):
            T.dma("pool", wk_v[:, c, :].rearrange("p (h d) -> p h d", d=128), w_ukv[c * 128:(c + 1) * 128, :, 0:128], (), [b_slot[0], b_slot[1]])
            T.dma("pool", wv_v[:, c, :].rearrange("p (h d) -> p h d", d=128), w_ukv[c * 128:(c + 1) * 128, :, 128:256], (), [b_slot[2], b_slot[3]])

        def win_idx(col0):
            return col0 // 128 if col0 < OFF_KR else 13 + (col0 - OFF_ZA) // 128
        rr = lambda ap: ap.rearrange("(c p) e -> p c e", p=128)
        conv = []
        for col0 in list(range(0, OFF_KR, 128)) + list(range(OFF_ZA, DIN, 128)):
            conv.append((WIN[win_idx(col0)], rr(w_in[:, col0:col0 + 128])))
        conv.append((WIN[12, :, :, 0:64], rr(w_in[:, OFF_KR:OFF_KR + 64])))
        conv.append((WIN[12, :, :, 64:96], rr(w_in[:, OFF_KR + 32:OFF_KR + 64])))
        conv.append((WIN[12, :, :, 96:128], rr(w_in[:, OFF_KR:OFF_KR + 32])))
        for h in range(NH):
            conv.append((WQ[h, :, :, 0:192], rr(w_uq[:, h, :])))
            conv.append((WQ[h, :, :, 192:224], rr(w_uq[:, h, 160:192])))
            conv.append((WQ[h, :, :, 224:256], rr(w_uq[:, h, 128:160])))
        for g in range(4):
            conv.append((WP[g], rr(w_pool[g, :, :])))
        for dc in range(32):
            conv.append((WOUT[dc], rr(w_out[:, dc * 128:(dc + 1) * 128])))
        conv_pos = [0]

        def conv_some(k):
            while k > 0 and conv_pos[0] < len(conv):
                d_, s_ = conv[conv_pos[0]]
                T.dma("pool", d_, s_, (), [b_conv])
                conv_pos[0] += 1
                k -= 1

        def get_w_res(j):
            if j < 4:
                return wkv_v, b_RM, j * 128
            return wkr_v, b_cq, (j - 4) * 64

        pendA = [x_front(xseq[0:128, :], 128)]

        def x_tile_A(idx):
            r0 = idx * 128
            cur = pendA[0]
            if r0 + 128 < SEQ:
                pendA[0] = x_front(xseq[r0 + 128:r0 + 256, :], 128)
            x_back(cur, 128, hT[:, :, (idx % 4) * 128:(idx % 4 + 1) * 128], b_hT)

        for t4 in range(4):
            x_tile_A(t4)
        for blk in range(16):
            conv_some(6)
            ci = st["ckv"] = st["ckv"] ^ 1
            latents(get_w_res, lambda k: hT[:, k, 0:512], [b_hT], 512,
                    ropeA[0, :, blk * 512:(blk + 1) * 512], ropeA[1, :, blk * 512:(blk + 1) * 512], ckvT[ci], b_ckvT[ci])
            T.dma("pool", krp[:, blk * 512:(blk + 1) * 512], krT[:, :], [b_krT], [b_scr])
            hk = [(lambda idx=(blk + 1) * 4 + t4: x_tile_A(idx)) for t4 in range(4)] if blk + 1 < 16 else []
            kv_block(ckvT[ci], b_ckvT[ci], 512,
                     lambda h, blk=blk: KTp[h, :, blk * 512:(blk + 1) * 512],
                     lambda t4, ntok, blk=blk: Vp[:, blk, 0:ntok, t4, :].rearrange("h p d -> p h d"), hooks=hk)
            MARKS["A%d" % blk] = T.n_emit

        for s in range(2):
            for blk in range(8):
                conv_some(6)
                k = st["xs"] = st["xs"] ^ 1
                X = xs[k]; bX = b_xs[k]
                Xv = X[:, 0:2048].rearrange("p (t f) -> p t f", f=512)
                T.dma("sp", Xv, cckv[s, blk * 512:(blk + 1) * 512, :].rearrange("(t p) f -> p t f", p=128), (), [bX])
                Kv = X[:, 2048:2304].rearrange("p (t f) -> p t f", f=64)
                T.dma("sp", Kv, ckr[s, blk * 512:(blk + 1) * 512, :].rearrange("(t p) f -> p t f", p=128), (), [bX])
                ci = st["ckv"] = st["ckv"] ^ 1
                for t4 in range(4):
                    P, bP = next_pT()
                    for c in range(4):
                        tr(P[:, c * 128:(c + 1) * 128], Xv[:, t4, c * 128:(c + 1) * 128], ident_f[:, :], [bX, b_ident], [bP], c == 3)
                    cp(ckvT[ci][:, :, t4 * 128:(t4 + 1) * 128], P[:, :].rearrange("p (c t) -> p c t", t=128), [bP], [b_ckvT[ci]])
                P, bP = next_pT()
                for t4 in range(4):
                    tr(P[0:64, t4 * 128:(t4 + 1) * 128], Kv[:, t4, :], ident_f[:, :], [bX, b_ident], [bP], t4 == 3)
                cp(krT[:, :], P[0:64, :], [bP], [b_krT])
                T.dma("pool", krs[s, :, blk * 512:(blk + 1) * 512], krT[:, :], [b_krT], [b_scr])
                kv_block(ckvT[ci], b_ckvT[ci], 512,
                         lambda h, blk=blk, s=s: KTs[s, h, :, blk * 512:(blk + 1) * 512],
                         lambda t4, ntok, blk=blk, s=s: Vs[s, :, blk, 0:ntok, t4, :].rearrange("h p d -> p h d"))
                MARKS["SC%d_%d" % (s, blk)] = T.n_emit

        conv_some(10 ** 6)
        x_to_hT(xsam[:, :], 128, hT[:, :, 0:128], b_hT)
        ci = st["ckv"] = st["ckv"] ^ 1
        latents(get_w_res, lambda k: hT[:, k, 0:128], [b_hT], 128, ropeS[0, :, :], ropeS[1, :, :], ckvT[ci], b_ckvT[ci],
                out_ckv_fn=lambda t0, ntok: sckv[t0:t0 + ntok, :], out_kr_fn=lambda t0, ntok: skr[t0:t0 + ntok, :])
        for s in range(2):
            T.dma("pool", krs[s, :, 4096:4160], krT[:, s * 64:(s + 1) * 64], [b_krT], [b_scr])
        for h in range(NH):
            P, bP = next_acc()
            for c in range(4):
                mm(P[:, 0:128], wk_v[:, c, h * 128:(h + 1) * 128], ckvT[ci][:, c, 0:128], c == 0, c == 3, [b_slot[0], b_slot[1], b_ckvT[ci]], [bP])
            act(Bt[0][:, 0:128], P[:, 0:128], AF.Square, [bP], [b_B[0]])
            X, bX = next_pT()
            mm(X[:, 0:128], ones_b[:, :], Bt[0][:, 0:128], True, True, [b_ones, b_B[0]], [bX])
            rstd_from(X[:, 0:128], bX, 128, W[2][:, 0:128], b_W[2], W[2][:, 0:128], b_W[2], 128)
            ki = st["ktt"] = st["ktt"] ^ 1
            stt(KTt[ki][:, 0:128], P[:, 0:128], gcols[:, 45:46], W[2][:, 0:128], ALU.mult, ALU.mult, [bP, b_W[2], b_g], [b_KTt[ki]])
            for s in range(2):
                T.dma("pool", KTs[s, h, :, 4096:4160], KTt[ki][:, s * 64:(s + 1) * 64], [b_KTt[ki]], [b_scr])
        for s in range(2):
            for hg in range(4):
                P, bP = next_acc()
                for c in range(4):
                    mm(P[0:64, :], ckvT[ci][:, c, s * 64:(s + 1) * 64], wv_v[:, c, hg * 512:(hg + 1) * 512], c == 0, c == 3,
                       [b_slot[2], b_slot[3], b_ckvT[ci]], [bP])
                cp(Vts[s][0][0:64, hg * 512:(hg + 1) * 512], P[0:64, :], [bP], Vts[s][1])
            T.dma("pool", Vs[s, :, 8, 0:64, 0, :].rearrange("h p d -> p h d"), Vts[s][0][0:64, :].rearrange("p (h d) -> p h d", d=128), Vts[s][1], [b_scr])

        MARKS["SN"] = T.n_emit
        memset(UH[:, :, :, :], 0.0, [b_UH])
        for s in range(2):
            k = st["xs"] = st["xs"] ^ 1
            X = xs[k]; bX = b_xs[k]
            T.dma("sp", X[0:15, 0:2048], spool[s, :, :], (), [bX])
            for c4 in range(4):
                P, bP = next_pT()
                for q in range(4):
                    c = c4 * 4 + q
                    tr(P[:, q * 16:q * 16 + 15], X[0:15, c * 128:(c + 1) * 128], ident_f[0:15, 0:15], [bX, b_ident], [bP], q == 3)
                cp(UH[:, c4 * 4:(c4 + 1) * 4, s, 1:16], P[:, 0:64].rearrange("p (c t) -> p c t", t=16)[:, :, 0:15], [bP], [b_UH])

        MARKS["UH"] = T.n_emit
        def wslice(col0, width):
            assert width == 128
            return stream_chunk(WIN[win_idx(col0)], 32, 128)

        def own_block(kind, i):
            if kind == "p":
                n, nseq, nt = 512, 1, 512
            else:
                n, nseq, nt = 128, 2, 64
            L = 16 + nt
            rhs = lambda k: hT[:, k, 0:n]
            if kind == "p":
                kh = x_front(xown[i, 0:16, :], 16)
                pend_o = x_front(xown[i, 16:144, :], 128)
                x_back(kh, 16, hTh[:, :, 0:16], b_hTh)
                for t4 in range(4):
                    cur = pend_o
                    if t4 + 1 < 4:
                        pend_o = x_front(xown[i, 16 + (t4 + 1) * 128:16 + (t4 + 2) * 128, :], 128)
                    x_back(cur, 128, hT[:, :, t4 * 128:(t4 + 1) * 128], b_hT)
                kr_w = [None]

                def get_w_str(j):
                    if j < 4:
                        v, b = wslice(OFF_KV + j * 128, 128)
                        return v, b, 0
                    if j == 4:
                        v, b = stream_chunk(WIN[12], 32, 128)
                        kr_w[0] = (v, b)
                        return v, b, 0
                    v, b = kr_w[0]
                    return v, b, 64
                ci = st["ckv"] = st["ckv"] ^ 1
                latents(get_w_str, rhs, [b_hT], n, ropeO[i, 0, :, :], ropeO[i, 1, :, :], ckvT[ci], b_ckvT[ci],
                        out_ckv_fn=lambda t0, ntok: ockv[i, t0:t0 + ntok, :], out_kr_fn=lambda t0, ntok: okr[i, t0:t0 + ntok, :])
            MARKS["B1_%s%d" % (kind, i)] = T.n_emit
            rawt = [W[0], W[1], W[2], W[3], PT[0], PT[1], W[5], W[6]]
            rawb = [b_W[0], b_W[1], b_W[2], b_W[3], b_PT[0], b_PT[1], b_W[5], b_W[6]]
            for c8 in range(8):
                v, b = wslice(OFF_Q + c8 * 128, 128)
                P, bP = next_acc()
                proj(v, b, 0, 128, rhs, 32, n, P, bP, [b_hT])
                cp(rawt[c8][:, 0:n], P[:, 0:n], [bP], [rawb[c8]], eng="dve")
                act(Bt[4 + c8 % 2][:, 0:n], P[:, 0:n], AF.Square, [bP], [b_B[4 + c8 % 2]])
                if c8 > 0:
                    mm(pR[:, 0:n], ones_b[:, :], Bt[4 + (c8 - 1) % 2][:, 0:n], c8 == 1, False, [b_ones, b_B[4 + (c8 - 1) % 2]], [b_pR], False)
            mm(pR[:, 0:n], ones_b[:, :], Bt[5][:, 0:n], False, True, [b_ones, b_B[5]], [b_pR], True)
            rstd_from(pR[:, 0:n], b_pR, QL, W[4][:, 0:n], b_W[4], W[4][:, 0:n], b_W[4], 128)
            for c8 in range(8):
                stt(cqnT[:, c8, 0:n], rawt[c8][:, 0:n], gcols[:, 32 + c8:33 + c8], W[4][:, 0:n], ALU.mult, ALU.mult,
                    [rawb[c8], b_W[4], b_g], [b_cq])

            def silu_from(P, bP, Wg, bWg):
                act(PT[1][:, 0:n], P[:, 0:n], AF.Tanh, [bP], [b_PT[1]], scale=0.5)
                stt(Wg[:, 0:n], PT[1][:, 0:n], 1.0, P[:, 0:n], ALU.add, ALU.mult, [b_PT[1], bP], [bWg])

            def gate_from(colbase, Wg, bWg):
                v, b = wslice(colbase, 128)
                P, bP = next_acc()
                proj(v, b, 0, 128, rhs, 32, n, P, bP, [b_hT])
                silu_from(P, bP, Wg, bWg)

            MARKS["B2_%s%d" % (kind, i)] = T.n_emit
            IC, bIC = W[7], b_W[7]
            Uv = U[:, :, 0:nseq * L].rearrange("p c (s l) -> p c s l", l=L)
            pout = xs[0]; bpout = b_xs[0]
            for g in range(4):
                wwin = 2 << g
                if kind == "p":
                    T.dma("sp", IC[:, 0:n], invc_d[i, :, g, :], (), [bIC])
                else:
                    T.dma("sp", IC[:, 0:n], invs_d[:, g, :], (), [bIC])
                for c in range(4):
                    v, b = wslice(OFF_U + (g * 4 + c) * 128, 128)
                    P, bP = next_acc()
                    proj(v, b, 0, 128, rhs, 32, n, P, bP, [b_hT])
                    cp(Uv[:, c, :, 16:L], P[:, 0:n].rearrange("p (s t) -> p s t", t=nt), [bP], [b_U], eng="dve")
                    if kind == "p":
                        P2, bP2 = next_acc()
                        proj(v, b, 0, 128, lambda k: hTh[:, k, 0:16], 32, 16, P2, bP2, [b_hTh])
                        cp(Uv[:, c, 0, 0:16], P2[:, 0:16], [bP2], [b_U], eng="dve")
                    else:
                        cp(Uv[:, c, :, 0:16], UH[:, g * 4 + c, :, :], [b_UH], [b_U], eng="dve")
                for s in range(nseq):
                    P, bP = next_pT()
                    for c in range(4):
                        tr(P[0:15, c * 128:(c + 1) * 128], Uv[:, c, s, L - 15:L], ident_f[:, :], [b_U, b_ident], [bP], c == 3)
                    cp(pout[32 * s:32 * s + 15, g * 512:(g + 1) * 512], P[0:15, :], [bP], [bpout], eng="dve")
                for c in range(4):
                    src = Uv[:, c, :, :]
                    sh = 1
                    cur = src; bcur = b_U
                    pi = 0
                    lo = 0
                    while sh < wwin:
                        lo2 = lo + sh
                        dstt = PT[pi][:, 0:nseq * L].rearrange("p (s l) -> p s l", l=L)
                        tt(dstt[:, :, lo2:L], cur[:, :, lo2:L], cur[:, :, lo:L - sh], ALU.add, [bcur], [b_PT[pi]])
                        cur = dstt; bcur = b_PT[pi]
                        pi ^= 1
                        lo = lo2
                        sh *= 2
                    tmp = PT[pi][:, 0:n].rearrange("p (s t) -> p s t", t=nt)
                    tt(tmp, cur[:, :, 16:L], IC[:, 0:n].rearrange("p (s t) -> p s t", t=nt), ALU.mult, [bcur, bIC], [b_PT[pi]])
                    tt(Bt[c][:, 0:n].rearrange("p (s t) -> p s t", t=nt), tmp, src[:, :, 16:L], ALU.subtract, [b_PT[pi], b_U], [b_B[c]])
                wp = wpool_sb
                T.dma("sp", wp[:, :, :], WP[g], [b_conv], [b_wpool])
                for ec in range(4):
                    gate_from(OFF_ZP + (g * 4 + ec) * 128, W[4], b_W[4])
                    for c in range(4):
                        mm(pO[:, 0:n], wp[:, c, ec * 128:(ec + 1) * 128], Bt[c][:, 0:n], c == 0, c == 3, [b_wpool, b_B[c]], [b_pO])
                    stt(W[4][:, 0:n], W[4][:, 0:n], 0.5, pO[:, 0:n], ALU.mult, ALU.mult, [b_pO], [b_W[4]])
                    ts(mixT[:, 16 + g * 4 + ec, 0:n], W[4][:, 0:n], gcols[:, 50 + g * 4 + ec:51 + g * 4 + ec], 1.0, ALU.mult, ALU.mult,
                       [b_W[4], b_g], [b_RM])
            for s in range(nseq):
                dst = opool[i, :, :] if kind == "p" else spo[s, :, :]
                T.dma("pool", dst, pout[32 * s:32 * s + 15, 0:2048], [bpout], ())

            MARKS["B3_%s%d" % (kind, i)] = T.n_emit
            if kind == "p":
                E = 2048 * (i + 1)
                seqs = [(0, 512, E, None)]
            else:
                seqs = [(0, 64, 4160, 0), (64, 64, 4160, 1)]
            RC, bRC = W[5], b_W[5]
            RS, bRS = W[6], b_W[6]
            if kind == "p":
                T.dma("sp", RC[0:64, 0:n], ropeO[i, 0, :, :], (), [bRC])
                T.dma("sp", RS[0:64, 0:n], ropeO[i, 1, :, :], (), [bRS])
            else:
                T.dma("sp", RC[0:64, 0:n], ropeS[0, :, :], (), [bRC])
                T.dma("sp", RS[0:64, 0:n], ropeS[1, :, :], (), [bRS])
            qbuf = [(QnT, b_Qn, QrT, b_Qr, W[7], b_W[7]), (QnT2, b_Qn2, QrT2, b_Qr2, W[4], b_W[4])]

            def prep_stages(h):
                Qn_, bQn_, Qr_, bQr_, Wg, bWg = qbuf[h % 2]
                stt_ = {}
                qr = lambda kk: cqnT[:, kk, 0:n]

                def s1():
                    v, b = wslice(OFF_ZA + h * 128, 128)
                    P, bP = next_acc()
                    proj(v, b, 0, 128, rhs, 32, n, P, bP, [b_hT])
                    stt_["g"] = (P, bP)

                def s2():
                    P, bP = stt_["g"]
                    silu_from(P, bP, Wg, bWg)

                def s3():
                    k = st["slot"] = (st["slot"] + 1) % 4
                    wq = slots[k][:, 0:8 * 256].rearrange("p (c e) -> p c e", e=256)
                    T.dma("sp", wq, WQ[h], [b_conv], [b_slot[k]])
                    P, bP = next_acc()
                    proj(wq, b_slot[k], 0, 128, qr, 8, n, P, bP, [b_cq])
                    P1, bP1 = next_acc()
                    proj(wq, b_slot[k], 128, 64, qr, 8, n, P1, bP1, [b_cq])
                    stt_["q"] = (wq, b_slot[k], P, bP, P1, bP1)

                def s4():
                    wq, bwq, P, bP, P1, bP1 = stt_["q"]
                    act(Bt[2][:, 0:n], P[:, 0:n], AF.Square, [bP], [b_B[2]])
                    cp(W[1][0:64, 0:n], P1[0:64, 0:n], [bP1], [b_W[1]], eng="dve")
                    act(Bt[3][0:64, 0:n], P1[0:64, 0:n], AF.Square, [bP1], [b_B[3]])
                    P2, bP2 = next_pT()
                    proj(wq, bwq, 192, 64, qr, 8, n, P2, bP2, [b_cq])
                    stt_["p2"] = (P2, bP2)

                def s5():
                    wq, bwq, P, bP, P1, bP1 = stt_["q"]
                    P2, bP2 = stt_["p2"]
                    X, bX = next_pT()
                    mm(X[:, 0:n], ones_b[:, :], Bt[2][:, 0:n], True, True, [b_ones, b_B[2]], [bX])
                    rstd_from(X[:, 0:n], bX, 128, W[0][:, 0:n], b_W[0], W[0][:, 0:n], b_W[0], 128)
                    stt(Qn_[:, 0:n], P[:, 0:n], gcols[:, 44:45], W[0][:, 0:n], ALU.mult, ALU.mult, [bP, b_W[0], b_g], [bQn_])
                    stt(W[2][0:64, 0:n], P2[0:64, 0:n], gcols[0:64, 47:48], RS[0:64, 0:n], ALU.mult, ALU.mult, [bP2, bRS, b_g], [b_W[2]])

                def s6():
                    X, bX = next_pT()
                    mm(X[0:64, 0:n], ones_b[0:64, 0:64], Bt[3][0:64, 0:n], True, True, [b_ones, b_B[3]], [bX])
                    rstd_from(X[0:64, 0:n], bX, DR, W[0][0:64, 0:n], b_W[0], W[0][0:64, 0:n], b_W[0], 64)
                    stt(W[1][0:64, 0:n], W[1][0:64, 0:n], gcols[0:64, 46:47], RC[0:64, 0:n], ALU.mult, ALU.mult, [bRC, b_g], [b_W[1]])
                    tt(W[1][0:64, 0:n], W[1][0:64, 0:n], W[2][0:64, 0:n], ALU.add, [b_W[2]], [b_W[1]])
                    tt(Qr_[0:64, 0:n], W[1][0:64, 0:n], W[0][0:64, 0:n], ALU.mult, [b_W[1], b_W[0]], [bQr_])

                return [s1, s2, s3, s4, s5, s6]

            def prep(h):
                for f_ in prep_stages(h):
                    f_()

            def load_chunk(h, sidx, S, ci, ntile):
                kc = st["kvc"] = st["kvc"] ^ 1
                bk = b_kvc[kc]
                k0 = ci * 1024
                nkc = min(1024, S - k0)
                if sidx is None:
                    nch = ntile // 8
                    eoff = 1024 if ci == nch - 2 else (2048 if ci == nch - 1 else 0)
                    T.dma("sp", KTc[kc][:, 0:nkc], KTp[h, :, k0:k0 + nkc], [b_scr], [bk])
                    T.dma("sp", krc[kc][0:64, 0:nkc], krp[:, k0:k0 + nkc], [b_scr], [bk])
                    T.dma("sp", krc[kc][64:96, 0:nkc], EKb[:, eoff:eoff + nkc], [b_ek], [bk])
                    T.dma("sp", Vc[kc][:, :, :, :], Vp[h, k0 // 512:k0 // 512 + 2, :, :, :].rearrange("b p t d -> p b t d"), [b_scr], [b_vc[kc]])
                else:
                    T.dma("sp", KTc[kc][:, 0:nkc], KTs[sidx, h, :, k0:k0 + nkc], [b_scr], [bk])
                    T.dma("sp", krc[kc][0:64, 0:nkc], krs[sidx, :, k0:k0 + nkc], [b_scr], [bk])
                    T.dma("sp", krc[kc][64:96, 0:nkc], EKb[:, 0:nkc], [b_ek], [bk])
                    if nkc == 1024:
                        T.dma("sp", Vc[kc][:, :, :, :], Vs[sidx, h, k0 // 512:k0 // 512 + 2, :, :, :].rearrange("b p t d -> p b t d"), [b_scr], [b_vc[kc]])
                    else:
                        T.dma("sp", Vc[kc][0:nkc, 0, 0, :], Vs[sidx, h, k0 // 512, 0:nkc, 0, :], [b_scr], [b_vc[kc]])
                return kc

            chunk_slot = {}

            def attention(h, extra):
                Qn_, bQn_, Qr_, bQr_, Wg, bWg = qbuf[h % 2]
                tiles = []
                for (q0, nq, S, sidx) in seqs:
                    ntile = (S + 127) // 128
                    for t in range(ntile):
                        tiles.append((q0, nq, S, sidx, ntile, t))
                def ensure_chunk(q0, nq, S, sidx, ntile, ci, hh=None):
                    hh = h if hh is None else hh
                    key = (hh, sidx, ci)
                    if key not in chunk_slot and ci * 1024 < S and hh < NH:
                        chunk_slot[key] = load_chunk(hh, sidx, S, ci, ntile)

                groups = []
                for tl_ in tiles:
                    q0, nq, S, sidx, ntile, t = tl_
                    gsz = 512 // nq
                    full = min(128, S - t * 128) == 128
                    if groups and full and len(groups[-1]) < gsz and groups[-1][0][3] == sidx and groups[-1][0][5] // 8 == t // 8 \
                            and min(128, groups[-1][0][2] - groups[-1][0][5] * 128) == 128:
                        groups[-1].append(tl_)
                    else:
                        groups.append([tl_])
                info = {}

                def S_stage(j):
                    si = st["pS"] = st["pS"] ^ 1
                    lst = []
                    for gi, (q0, nq, S, sidx, ntile, t) in enumerate(groups[j]):
                        ci = t // 8
                        ensure_chunk(q0, nq, S, sidx, ntile, ci)
                        if len(groups[j]) == 1 and t % 8 == 1:
                            if (ci + 1) * 1024 < S:
                                ensure_chunk(q0, nq, S, sidx, ntile, ci + 1)
                            elif len(seqs) == 1:
                                ensure_chunk(q0, nq, S, sidx, ntile, 0, hh=h + 1)
                        kc = chunk_slot[(h, sidx, ci)]
                        bk = b_kvc[kc]
                        nk = min(128, S - t * 128)
                        tl = t % 8
                        last = gi == len(groups[j]) - 1
                        mm(pS[si][0:nk, gi * nq:(gi + 1) * nq], KTc[kc][:, tl * 128:tl * 128 + nk], Qn_[:, q0:q0 + nq], True, False, [bk, bQn_], [b_pS[si]], False)
                        mm(pS[si][0:nk, gi * nq:(gi + 1) * nq], krc[kc][:, tl * 128:tl * 128 + nk], Qr_[:, q0:q0 + nq], False, True, [bk, bQr_], [b_pS[si]], last)
                        lst.append((kc, nk, tl))
                    info[j] = (si, lst)

                def E_stage(j):
                    si, lst = info[j]
                    nq = groups[j][0][1]
                    nk = lst[0][1]
                    w = len(lst) * nq
                    bi = 4 + (j % 2)
                    act(Bt[bi][0:nk, 0:w], pS[si][0:nk, 0:w], AF.Exp, [b_pS[si]], [b_B[bi]], scale=SCALE)

                def PV_stage(j):
                    si, lst = info[j]
                    bi = 4 + (j % 2)
                    for gi, (q0, nq, S, sidx, ntile, t) in enumerate(groups[j]):
                        kc, nk, tl = lst[gi]
                        last = gi == len(groups[j]) - 1
                        if dve_sum:
                            mm(pO[:, q0:q0 + nq], Vc[kc][0:nk, tl // 4, tl % 4, :], Bt[bi][0:nk, gi * nq:(gi + 1) * nq], t == 0, t == ntile - 1, [b_vc[kc], b_B[bi]], [b_pO], True)
                            if t == 0:
                                T.op("dve", lambda e, bi=bi: e.tensor_copy(out=acc, in_=Bt[bi][:, 0:512]), [b_B[bi]], [b_U])
                            else:
                                tt(acc, acc, Bt[bi][:, 0:512], ALU.add, [b_B[bi]], [b_U])
                        else:
                            mm(pO[:, q0:q0 + nq], Vc[kc][0:nk, tl // 4, tl % 4, :], Bt[bi][0:nk, gi * nq:(gi + 1) * nq], t == 0, t == ntile - 1, [b_vc[kc], b_B[bi]], [b_pO], False)
                            mm(pR[:, q0:q0 + nq], ones_b[0:nk, :], Bt[bi][0:nk, gi * nq:(gi + 1) * nq], t == 0, t == ntile - 1, [b_ones, b_B[bi]], [b_pR], last)

                ng = len(groups)
                if kind == "p":
                    for f_ in extra:
                        f_()
                    when = {}
                else:
                    when = {1 + k: f_ for k, f_ in enumerate(extra)} if ng > len(extra) else {}
                dve_sum = False
                acc = U[:, 0, 0:512]
                S_stage(0)
                for j in range(ng):
                    if j + 1 < ng:
                        S_stage(j + 1)
                    E_stage(j)
                    PV_stage(j)
                    if j in when:
                        when.pop(j)()
                    if len(seqs) > 1:
                        jj = j + 2
                        hh_ = h
                        if jj >= len(groups):
                            jj -= len(groups)
                            hh_ = h + 1
                        q0_, nq_, S_, sidx_, ntile_, t_ = groups[jj][0]
                        ensure_chunk(q0_, nq_, S_, sidx_, ntile_, t_ // 8, hh=hh_)
                if dve_sum:
                    T.op("dve", lambda e: e.tensor_copy(out=KTt[0][:, 0:512], in_=acc), [b_U], [b_KTt[0]])
                    tt(KTt[1][:, 0:512], acc, KTt[0][:, 0:512], ALU.subtract, [b_U, b_KTt[0]], [b_KTt[1]])
                    mm(pR[:, 0:512], ones_b[:, :], KTt[0][:, 0:512], True, False, [b_ones, b_KTt[0]], [b_pR], False)
                    mm(pR[:, 0:512], ones_b[:, :], KTt[1][:, 0:512], False, True, [b_ones, b_KTt[1]], [b_pR], True)
                cp(PT[0][:, 0:n], pO[:, 0:n], [b_pO], [b_PT[0]], eng="act")
                cp(W[3][:, 0:n], pR[:, 0:n], [b_pR], [b_W[3]], eng="dve")
                T.op("dve", lambda e: e.reciprocal(out=W[3][:, 0:n], in_=W[3][:, 0:n]), [], [b_W[3]])
                stt(W[3][:, 0:n], W[3][:, 0:n], 0.5, Wg[:, 0:n], ALU.mult, ALU.mult, [bWg], [b_W[3]])
                tt(mixT[:, h, 0:n], PT[0][:, 0:n], W[3][:, 0:n], ALU.mult, [b_PT[0], b_W[3]], [b_RM])
                for k_ in sorted(when):
                    when[k_]()

            prep(0)
            for h in range(NH):
                attention(h, prep_stages(h + 1) if h + 1 < NH else [])

            MARKS["B4_%s%d" % (kind, i)] = T.n_emit
            ydst = (lambda t0, ntok, c0: yown[i, t0:t0 + ntok, c0:c0 + 512]) if kind == "p" else (lambda t0, ntok, c0: ysam[t0:t0 + ntok, c0:c0 + 512])
            xsrc = (lambda t0, ntok, c0: xown[i, 16 + t0:16 + t0 + ntok, c0:c0 + 512]) if kind == "p" else (lambda t0, ntok, c0: xsam[t0:t0 + ntok, c0:c0 + 512])
            X2s = [(W[4], b_W[4]), (W[5], b_W[5]), (W[6], b_W[6]), (W[7], b_W[7])]
            for dg in range(8):
                for t0 in range(0, n, 128):
                    X2, bX2 = X2s[(t0 // 128) % 4]
                    T.dma("sp", X2[:, :], xsrc(t0, 128, dg * 512), (), [bX2])
                for q in range(4):
                    dc = dg * 4 + q
                    v, b = stream_chunk(WOUT[dc], 32, 128)
                    P, bP = next_acc()
                    proj(v, b, 0, 128, lambda kk: mixT[:, kk, 0:n], 32, n, P, bP, [b_RM])
                    cp(W[q][:, 0:n], P[:, 0:n], [bP], [b_W[q]])
                for t0 in range(0, n, 128):
                    X2, bX2 = X2s[(t0 // 128) % 4]
                    P, bP = next_pT()
                    for q in range(4):
                        tr(P[:, q * 128:(q + 1) * 128], W[q][:, t0:t0 + 128], ident_f[:, :], [b_W[q], b_ident], [bP], q == 3)
                    tt(X2[:, :], X2[:, :], P[:, :], ALU.add, [bP], [bX2])
                    T.dma("pool", ydst(t0, 128, dg * 512), X2[:, :], [bX2], ())

        own_block("s", 0)
        MARKS["B5_s0"] = T.n_emit
        for i in range(4):
            own_block("p", i)
            MARKS["B5_p%d" % i] = T.n_emit

        T.finish()

        with nc.Block() as block:
            @block.tensor
            def _(e):
                T.replay("pe", e)

            @block.scalar
            def _(e):
                T.replay("act", e)

            @block.vector
            def _(e):
                T.replay("dve", e)

            @block.gpsimd
            def _(e):
                T.replay("pool", e)

            @block.sync
            def _(e):
                T.replay("sp", e)
    return nc


_NC = None


def _rope_tables(pos):
    half = DR // 2
    freqs = (np.float32(10000.0) ** (-np.arange(half, dtype=np.float32) / np.float32(half))).astype(np.float32)
    ang = pos.astype(np.float32)[:, None] * freqs[None, :]
    cos = np.cos(ang).astype(np.float32).T
    sin = np.sin(ang).astype(np.float32).T
    C = np.concatenate([cos, cos], 0)
    S = np.concatenate([-sin, sin], 0)
    return np.ascontiguousarray(np.stack([C, S], 0))


def kernel(x_prompt, x_sample, cache_ckv, cache_krope, state_pool, g_norm, w_in, g_q_lat, w_uq, g_qn, g_qr,
           g_kv_lat, g_kr, w_ukv, g_kn, w_pool, pool_scale, w_out):
    global _NC
    if _NC is None:
        _NC = build_program()
    nc = _NC
    in_maps = prepare(x_prompt, x_sample, cache_ckv, cache_krope, state_pool, g_norm, w_in, g_q_lat, w_uq, g_qn, g_qr,
                      g_kv_lat, g_kr, w_ukv, g_kn, w_pool, pool_scale, w_out)
    res = run_bass_kernel_spmd(nc, in_maps, core_ids=list(range(8)))
    return assemble(res.results)


def prepare(x_prompt, x_sample, cache_ckv, cache_krope, state_pool, g_norm, w_in, g_q_lat, w_uq, g_qn, g_qr,
            g_kv_lat, g_kr, w_ukv, g_kn, w_pool, pool_scale, w_out, cores=range(8)):
    f = lambda a: np.ascontiguousarray(np.asarray(a, dtype=np.float32))
    x_prompt = f(x_prompt); x_sample = f(x_sample); cache_ckv = f(cache_ckv); cache_krope = f(cache_krope)
    state_pool = f(state_pool)
    w_in0 = f(w_in)[0]; w_uq0 = f(w_uq)[0]; w_ukv0 = f(w_ukv)[0]; w_pool0 = f(w_pool)[0]; w_out0 = f(w_out)[0]
    gcols = np.zeros((128, 72), np.float32)
    gcols[:, 0:32] = f(g_norm)[0].reshape(32, 128).T
    gcols[:, 32:40] = f(g_q_lat)[0].reshape(8, 128).T
    gcols[:, 40:44] = f(g_kv_lat)[0].reshape(4, 128).T
    gcols[:, 44] = f(g_qn)[0]
    gcols[:, 45] = f(g_kn)[0]
    gqr = f(g_qr)[0]; gkr = f(g_kr)[0]
    gcols[0:64, 46] = gqr
    gcols[0:64, 47] = np.concatenate([gqr[32:], gqr[:32]])
    gcols[0:64, 48] = gkr
    gcols[0:64, 49] = np.concatenate([gkr[32:], gkr[:32]])
    gcols[:, 50:66] = f(pool_scale)[0].reshape(16, 128).T

    ident = np.eye(128, dtype=np.float32)
    ropeA = _rope_tables(np.arange(SEQ))
    rs = _rope_tables(PAST + np.arange(64))
    ropeS = np.ascontiguousarray(np.concatenate([rs, rs], 2))
    EK = np.zeros((32, 3072), np.float32)
    EK[np.arange(2048) // 64, 1024 + np.arange(2048)] = 1.0
    windows = (2, 4, 8, 16)
    invs = np.zeros((128, 4, 128), np.float32)
    for g, w in enumerate(windows):
        invs[:, g, :] = 1.0 / w

    in_maps = []
    for c in cores:
        s, j = c // 4, c % 4
        xown = np.zeros((4, 528, D), np.float32)
        ropeO = np.zeros((4, 2, 64, 512), np.float32)
        invc = np.zeros((4, 128, 4, 512), np.float32)
        for i in range(4):
            b = 4 * i + j
            lo = b * 512 - 16
            if lo >= 0:
                xown[i] = x_prompt[s, lo:lo + 528]
            else:
                xown[i, 16:] = x_prompt[s, 0:512]
            ropeO[i] = ropeA[:, :, b * 512:(b + 1) * 512]
            pos = b * 512 + np.arange(512)
            for g, w in enumerate(windows):
                invc[i, :, g, :] = (1.0 / np.minimum(pos + 1, w).astype(np.float32))[None, :]
        LQ = np.zeros((32, 512), np.float32)
        qchunk = 8 * j + np.arange(512) // 64
        LQ[np.arange(32)[:, None] > qchunk[None, :]] = NEG
        in_maps.append({
            "xseq": x_prompt[s], "xown": xown, "xsam": np.ascontiguousarray(x_sample[2 * c:2 * c + 2].reshape(128, D)),
            "cckv": np.ascontiguousarray(cache_ckv[0, 2 * c:2 * c + 2]), "ckr": np.ascontiguousarray(cache_krope[0, 2 * c:2 * c + 2]),
            "spool": np.ascontiguousarray(state_pool[0, 2 * c:2 * c + 2]),
            "w_in": w_in0, "w_uq": w_uq0, "w_ukv": w_ukv0, "w_pool": w_pool0, "w_out": w_out0,
            "ident": ident, "gcols": gcols, "ropeA": ropeA, "ropeO": ropeO, "ropeS": ropeS, "EK": EK, "LQ": LQ,
            "invc": invc, "invs": invs,
        })
    return in_maps


def assemble(R):
    y_p = np.zeros((2, SEQ, D), np.float32)
    ckv_p = np.zeros((1, 2, SEQ, KVL), np.float32)
    kr_p = np.zeros((1, 2, SEQ, DR), np.float32)
    pool_p = np.zeros((1, 2, 15, 2048), np.float32)
    y_s = np.zeros((16, 64, D), np.float32)
    ckv_s = np.zeros((1, 16, 64, KVL), np.float32)
    kr_s = np.zeros((1, 16, 64, DR), np.float32)
    pool_s = np.zeros((1, 16, 15, 2048), np.float32)
    for c in range(8):
        s, j = c // 4, c % 4
        r = R[c]
        for i in range(4):
            b = 4 * i + j
            y_p[s, b * 512:(b + 1) * 512] = r["yown"][i]
            ckv_p[0, s, b * 512:(b + 1) * 512] = r["ockv"][i]
            kr_p[0, s, b * 512:(b + 1) * 512] = r["okr"][i]
            if b == 15:
                pool_p[0, s] = r["opool"][i]
        y_s[2 * c:2 * c + 2] = r["ysam"].reshape(2, 64, D)
        ckv_s[0, 2 * c:2 * c + 2] = r["sckv"].reshape(2, 64, KVL)
        kr_s[0, 2 * c:2 * c + 2] = r["skr"].reshape(2, 64, DR)
        pool_s[0, 2 * c:2 * c + 2] = r["spo"]
    return (y_p, y_s, ckv_p, kr_p, pool_p, ckv_s, kr_s, pool_s)
```
